# Optimizing a Trainium2 kernel written in Bass

```python
import jax, jax.numpy as jnp
from jax import lax
import numpy as np

D_MODEL = 2048
BATCH = 4
SEQ = 2048
DEPTH = 1
DEC_BATCH = 128
DEC_SEQ = 8
PAST_LEN = 16384
PAGE_SIZE = 128

MIX_WIDTH = D_MODEL
GLA_WIDTH = MIX_WIDTH // 2
GMLP_WIDTH = MIX_WIDTH - GLA_WIDTH
GLA_HEADS = 4
GLA_DV = GLA_WIDTH // GLA_HEADS
GLA_DK = GLA_DV // 2
GLA_KEY_WIDTH = GLA_HEADS * GLA_DK
GLA_GATE_RANK = 16
GLA_TAU = 16.0
GLA_CHUNK = 64
GMLP_GROUPS = 8
GMLP_GROUP_DIM = GMLP_WIDTH // GMLP_GROUPS
GMLP_CHUNK = 128
D_FF = -(-8 * D_MODEL // (3 * 256)) * 256
PLE_DIM = 256
EPS = 1e-6
IN_SIZES = (GLA_KEY_WIDTH, GLA_KEY_WIDTH, GLA_WIDTH, GLA_WIDTH, GLA_GATE_RANK, GMLP_WIDTH, GMLP_WIDTH)
IN_WIDTH = sum(IN_SIZES)

kernel_name = 'hybrid_gla_gmlp_step'


def rmsnorm(x, g):
    xf = x.astype(jnp.float32)
    y = xf * lax.rsqrt(jnp.mean(xf * xf, axis=-1, keepdims=True) + EPS)
    return (y * g.astype(jnp.float32)).astype(x.dtype)


def layernorm(x, g, b):
    xf = x.astype(jnp.float32)
    mu = jnp.mean(xf, axis=-1, keepdims=True)
    xc = xf - mu
    y = xc * lax.rsqrt(jnp.mean(xc * xc, axis=-1, keepdims=True) + EPS)
    return (y * g.astype(jnp.float32) + b.astype(jnp.float32)).astype(x.dtype)


def split_cols(z, sizes):
    out, start = [], 0
    for s in sizes:
        out.append(z[..., start:start + s])
        start += s
    return out


def gla_recurrence(q, k, v, log_a, s0):
    f32 = jnp.float32
    Bn, T = q.shape[0], q.shape[1]
    C = min(GLA_CHUNK, T)
    n = -(-T // C)
    pad = n * C - T
    def prep(a):
        a = a.astype(f32)
        if pad:
            a = jnp.pad(a, ((0, 0), (0, pad), (0, 0), (0, 0)))
        return a.reshape(Bn, n, C, a.shape[2], a.shape[3]).transpose(1, 0, 3, 2, 4)
    qc, kc, vc, lc = prep(q), prep(k), prep(v), prep(log_a)
    mask = jnp.tril(jnp.ones((C, C), dtype=bool))
    mid = C // 2

    def step(S, inp):
        qb, kb, vb, lb = inp
        b = jnp.cumsum(lb, axis=2)
        b_mid = b[:, :, mid:mid + 1]
        b_last = b[:, :, -1:]
        o_inter = jnp.einsum('bhtk,bhkv->bhtv', qb * jnp.exp(b), S)
        att = jnp.einsum('bhtk,bhsk->bhts', qb * jnp.exp(b - b_mid), kb * jnp.exp(b_mid - b))
        att = jnp.where(mask, att, 0.0)
        o = o_inter + jnp.einsum('bhts,bhsv->bhtv', att, vb)
        S_new = jnp.exp(b_last)[:, :, 0, :, None] * S + jnp.einsum(
            'bhsk,bhsv->bhkv', kb * jnp.exp(b_last - b), vb)
        return S_new, o

    S, o = lax.scan(step, s0.astype(f32), (qc, kc, vc, lc))
    o = o.transpose(1, 0, 3, 2, 4).reshape(Bn, n * C, o.shape[2], o.shape[4])[:, :T]
    return o, S


def chunk_spatial_gate(u, vn, w_s, b_s):
    Bn, T = u.shape[0], u.shape[1]
    C = GMLP_CHUNK
    n = -(-T // C)
    pad = n * C - T
    vp = jnp.pad(vn, ((0, 0), (0, pad), (0, 0), (0, 0))) if pad else vn
    vp = vp.reshape(Bn, n, C, GMLP_GROUPS, GMLP_GROUP_DIM)
    w = jnp.where(jnp.tril(jnp.ones((C, C), dtype=bool))[None], w_s, 0.0).astype(vn.dtype)
    mixed = jnp.einsum('gts,bnsgd->bntgd', w, vp) + b_s.T.astype(vn.dtype)[None, None, :, :, None]
    mixed = mixed.reshape(Bn, n * C, GMLP_GROUPS, GMLP_GROUP_DIM)[:, :T]
    return u * mixed


def hybrid_layer(h, p_l, s0, g_mix, w_in, w_a2, b_a, g_gla_norm, g_gmlp_ln, b_gmlp_ln,
                 w_s, b_s, g_gmlp_out, w_out, g_ffn, w_ffn_in, w_ffn_out,
                 w_ple, g_ple, g_ple_gate, w_ple_gate):
    Bn, T = h.shape[0], h.shape[1]
    a = rmsnorm(h, g_mix)
    z = a @ w_in
    q, k, v, r, alow, u, vg = split_cols(z, IN_SIZES)
    q = q.reshape(Bn, T, GLA_HEADS, GLA_DK) * (GLA_DK ** -0.5)
    k = k.reshape(Bn, T, GLA_HEADS, GLA_DK)
    v = v.reshape(Bn, T, GLA_HEADS, GLA_DV)
    log_a = jax.nn.log_sigmoid((alow @ w_a2 + b_a).astype(jnp.float32)) / GLA_TAU
    log_a = log_a.reshape(Bn, T, GLA_HEADS, GLA_DK)
    o, s_new = gla_recurrence(q, k, v, log_a, s0)
    o = rmsnorm(o.astype(h.dtype), g_gla_norm)
    o = o.reshape(Bn, T, GLA_WIDTH) * jax.nn.silu(r)
    u = jax.nn.gelu(u)
    vn = layernorm(jax.nn.gelu(vg), g_gmlp_ln, b_gmlp_ln)
    m = chunk_spatial_gate(u.reshape(Bn, T, GMLP_GROUPS, GMLP_GROUP_DIM),
                           vn.reshape(Bn, T, GMLP_GROUPS, GMLP_GROUP_DIM), w_s, b_s)
    m = rmsnorm(m.reshape(Bn, T, GMLP_WIDTH), g_gmlp_out)
    h = h + jnp.concatenate([o, m], axis=-1) @ w_out
    f = rmsnorm(h, g_ffn)
    gate, up = split_cols(f @ w_ffn_in, (D_FF, D_FF))
    h = h + (jax.nn.silu(gate) * up) @ w_ffn_out
    pe = rmsnorm(p_l @ w_ple, g_ple)
    h = h + pe * jax.nn.sigmoid(rmsnorm(h, g_ple_gate) @ w_ple_gate)
    return h, s_new, vn


def setup_inputs(seed: int = 0) -> dict:
    key = jax.random.key(seed)
    ks = jax.random.split(key, 26)
    def nrm(k, shape, scale):
        return jax.random.normal(k, shape, jnp.float32) * scale
    def gain(k, shape):
        return 1.0 + nrm(k, shape, 0.02)
    return {
        'x_prompt': nrm(ks[0], (BATCH, SEQ, D_MODEL), 1.0),
        'x_sample': nrm(ks[1], (DEC_BATCH, DEC_SEQ, D_MODEL), 1.0),
        'state_gla': nrm(ks[2], (DEPTH, DEC_BATCH, GLA_HEADS, GLA_DK, GLA_DV), 0.3),
        'p_prompt': nrm(ks[3], (DEPTH, BATCH, SEQ, PLE_DIM), 1.0),
        'p_sample': nrm(ks[4], (DEPTH, DEC_BATCH, DEC_SEQ, PLE_DIM), 1.0),
        'g_mix': gain(ks[5], (DEPTH, D_MODEL)),
        'w_in': nrm(ks[6], (DEPTH, D_MODEL, IN_WIDTH), D_MODEL ** -0.5),
        'w_a2': nrm(ks[7], (DEPTH, GLA_GATE_RANK, GLA_KEY_WIDTH), GLA_GATE_RANK ** -0.5),
        'b_a': nrm(ks[8], (DEPTH, GLA_KEY_WIDTH), 0.1),
        'g_gla_norm': gain(ks[9], (DEPTH, GLA_HEADS, GLA_DV)),
        'g_gmlp_ln': gain(ks[10], (DEPTH, GMLP_WIDTH)),
        'b_gmlp_ln': nrm(ks[11], (DEPTH, GMLP_WIDTH), 0.02),
        'w_s': nrm(ks[12], (DEPTH, GMLP_GROUPS, GMLP_CHUNK, GMLP_CHUNK), 0.5 * GMLP_CHUNK ** -0.5),
        'b_s': 1.0 + nrm(ks[13], (DEPTH, GMLP_GROUPS, GMLP_CHUNK), 0.1),
        'g_gmlp_out': gain(ks[14], (DEPTH, GMLP_WIDTH)),
        'w_out': nrm(ks[15], (DEPTH, MIX_WIDTH, D_MODEL), MIX_WIDTH ** -0.5),
        'g_ffn': gain(ks[16], (DEPTH, D_MODEL)),
        'w_ffn_in': nrm(ks[17], (DEPTH, D_MODEL, 2 * D_FF), D_MODEL ** -0.5),
        'w_ffn_out': nrm(ks[18], (DEPTH, D_FF, D_MODEL), D_FF ** -0.5),
        'w_ple': nrm(ks[19], (DEPTH, PLE_DIM, D_MODEL), PLE_DIM ** -0.5),
        'g_ple': gain(ks[20], (DEPTH, D_MODEL)),
        'g_ple_gate': gain(ks[21], (DEPTH, D_MODEL)),
        'w_ple_gate': nrm(ks[22], (DEPTH, D_MODEL, D_MODEL), D_MODEL ** -0.5),
        'g_final': gain(ks[23], (D_MODEL,)),
    }


def reference(x_prompt, x_sample, state_gla, p_prompt, p_sample, g_mix, w_in, w_a2, b_a,
              g_gla_norm, g_gmlp_ln, b_gmlp_ln, w_s, b_s, g_gmlp_out, w_out, g_ffn,
              w_ffn_in, w_ffn_out, w_ple, g_ple, g_ple_gate, w_ple_gate, g_final):
    hp, hs = x_prompt, x_sample
    sp_list, ss_list, vs_list = [], [], []
    for l in range(DEPTH):
        lw = (g_mix[l], w_in[l], w_a2[l], b_a[l], g_gla_norm[l], g_gmlp_ln[l], b_gmlp_ln[l],
              w_s[l], b_s[l], g_gmlp_out[l], w_out[l], g_ffn[l], w_ffn_in[l], w_ffn_out[l],
              w_ple[l], g_ple[l], g_ple_gate[l], w_ple_gate[l])
        s0_p = jnp.zeros((hp.shape[0], GLA_HEADS, GLA_DK, GLA_DV), jnp.float32)
        hp, sp, _ = hybrid_layer(hp, p_prompt[l], s0_p, *lw)
        hs, ss, vs = hybrid_layer(hs, p_sample[l], state_gla[l], *lw)
        sp_list.append(sp)
        ss_list.append(ss)
        vs_list.append(vs)
    y_prompt = rmsnorm(hp, g_final)
    y_sample = rmsnorm(hs, g_final)
    new_state_gla_prompt = jnp.stack(sp_list, axis=0)
    new_state_gla_sample = jnp.stack(ss_list, axis=0)
    new_gmlp_v_sample = jnp.stack(vs_list, axis=0)
    return (y_prompt, y_sample, new_state_gla_prompt, new_state_gla_sample, new_gmlp_v_sample)
```

```python
import numpy as np
import concourse.bass as bass
import concourse.mybir as mybir
from concourse.bass_utils import run_bass_kernel_spmd
from contextlib import ExitStack
import types

F32 = mybir.dt.float32
BF16 = mybir.dt.bfloat16
AF = mybir.ActivationFunctionType
ALU = mybir.AluOpType
AX = mybir.AxisListType

D = 2048
NPT = 8
NT = 9
NTOK = NT * 128
NPREV = 8
DFF = 5632
EPS = 1e-6
C_Q, C_K, C_V, C_R, C_A, C_U, C_VG = 0, 512, 1024, 2048, 3072, 3088, 4112
GELU_C = 1.5957691216057308


REORDER = True
SLACK = 1.0
LAT0 = 0.4


def _snap(fn, depth=0):
    if not isinstance(fn, types.FunctionType) or depth > 4:
        return fn
    cl = fn.__closure__
    if not cl:
        return fn
    cells = []
    for c in cl:
        try:
            v = c.cell_contents
        except ValueError:
            cells.append(c)
            continue
        if isinstance(v, types.FunctionType):
            v = _snap(v, depth + 1)
        cells.append(types.CellType(v))
    g = types.FunctionType(fn.__code__, fn.__globals__, fn.__name__, fn.__defaults__, tuple(cells))
    g.__kwdefaults__ = fn.__kwdefaults__
    return g


class Buf:
    __slots__ = ("name", "w", "r", "dsem", "dcnt", "dcls")

    def __init__(self, name):
        self.name = name
        self.w = {}
        self.r = {}
        self.dsem = None
        self.dcnt = 0
        self.dcls = None


class Sched:
    def __init__(self, nc, es):
        self.nc = nc
        self.es = es
        self.eng = {"pe": nc.tensor, "act": nc.scalar, "dve": nc.vector, "pool": nc.gpsimd, "sp": nc.sync}
        self.sem = {e: es.enter_context(nc.semaphore("sem_" + e)) for e in self.eng}
        self.cnt = {e: 0 for e in self.eng}
        self.waited = {e: {} for e in self.eng}
        self.dsems = []
        self.pool = {"sw": [], "hw": []}
        self.pend = []
        self.efree = {}
        self.cur_fence = None
        self.epoch = {}
        self.nsem = 0
        self.nb = 0

    def buf(self, name=None):
        self.nb += 1
        b = Buf(name or ("b%d" % self.nb))
        b.w = dict(self.epoch)
        if self.cur_fence is not None:
            self.cur_fence.append(b)
        return b

    def soft_barrier(self):
        self.flush()
        fb = []
        self.cur_fence = fb
        self.pend.append(("fence", None, None, [], [], None, 0.0, fb))

    def _emit_fence(self, fb):
        ep = {self.sem[e]: self.cnt[e] for e in self.eng if self.cnt[e] > 0}
        for kb in self.dsems:
            ep[kb.dsem] = kb.dcnt
        for s_, v in self.epoch.items():
            if ep.get(s_, 0) < v:
                ep[s_] = v
        self.epoch = ep
        for b in fb:
            for s_, v in ep.items():
                if b.w.get(s_, 0) < v:
                    b.w[s_] = v

    def bufs(self, n, name="b"):
        return [self.buf("%s%d" % (name, i)) for i in range(n)]

    def _wait(self, e, deps):
        w = self.waited[e]
        for sem, val in deps.items():
            if w.get(sem, 0) < val:
                self.eng[e].wait_ge(sem, val)
                w[sem] = val

    def _deps(self, e, reads, writes):
        deps = {}

        def add(d):
            for s, v in d.items():
                if deps.get(s, 0) < v:
                    deps[s] = v
        for b in reads:
            add(b.w)
        for b in writes:
            add(b.w)
            add(b.r)
        if e == "pe":
            deps.pop(self.sem["pe"], None)
        return deps

    def _commit(self, tok, reads, writes):
        s, v = tok
        for b in reads:
            if b.r.get(s, 0) < v:
                b.r[s] = v
        for b in writes:
            b.w = {s: v}
            b.r = {}

    def op(self, e, fn, reads=(), writes=(), cost=None):
        if cost is None:
            cost = getattr(fn, "cost", None)
        if cost is None:
            cost = {"pe": 0.6, "act": 0.8, "dve": 0.8, "pool": 1.2, "sp": 0.2}[e]
        self.pend.append(("op", e, _snap(fn), list(reads), list(writes), None, float(cost)))

    def dma(self, e, out, in_, reads=(), writes=(), key=None):
        kb = key if key is not None else (writes[0] if writes else reads[0])
        try:
            nb = float(out.nbytes())
        except Exception:
            nb = 1e6
        self.pend.append(("dma", e, (out, in_), list(reads), list(writes), kb, 2.0 + nb / 3.0e5))

    def _emit_op(self, e, fn, reads, writes):
        self._wait(e, self._deps(e, reads, writes))
        ins = fn()
        self.cnt[e] += 1
        ins.then_inc(self.sem[e], 1)
        self._commit((self.sem[e], self.cnt[e]), reads, writes)

    def _emit_dma(self, e, out, in_, reads, writes, kb):
        cls = "sw" if e == "pool" else "hw"
        if kb.dsem is not None and kb.dcls != cls:
            raise RuntimeError("buffer %s used as DMA key from both DGE kinds" % kb.name)
        if kb.dsem is None:
            if self.pool[cls]:
                kb.dsem, kb.dcnt = self.pool[cls].pop()
            else:
                self.nsem += 1
                kb.dsem = self.es.enter_context(self.nc.semaphore("ds%s%d" % (cls, self.nsem)))
                kb.dcnt = 0
            kb.dcls = cls
            self.dsems.append(kb)
        self._wait(e, self._deps(e, reads, writes))
        kb.dcnt += 16
        self.eng[e].dma_start(out=out, in_=in_).then_inc(kb.dsem, 16)
        self._commit((kb.dsem, kb.dcnt), reads, writes)

    def flush(self):
        ops = self.pend
        self.pend = []
        n = len(ops)
        if n == 0:
            return
        lastw, readers, lastkey = {}, {}, {}
        deps = [set() for _ in range(n)]
        fence_of = {}
        prev_all = []
        for i, op in enumerate(ops):
            kind, e, fn, reads, writes, kb, cost = op[:7]
            if kind == "fence":
                deps[i].update(prev_all)
                for b in op[7]:
                    fence_of[id(b)] = i
                prev_all = [i]
                continue
            prev_all.append(i)
            for b in list(reads) + list(writes):
                if id(b) in fence_of:
                    deps[i].add(fence_of[id(b)])
            for b in reads:
                if id(b) in lastw:
                    deps[i].add(lastw[id(b)])
            for b in writes:
                if id(b) in lastw:
                    deps[i].add(lastw[id(b)])
                deps[i].update(readers.get(id(b), ()))
            if kb is not None:
                if id(kb) in lastkey:
                    deps[i].add(lastkey[id(kb)])
                lastkey[id(kb)] = i
            for b in reads:
                readers.setdefault(id(b), []).append(i)
            for b in writes:
                lastw[id(b)] = i
                readers[id(b)] = []
            deps[i].discard(i)
        if not REORDER:
            order = list(range(n))
        else:
            users = [[] for _ in range(n)]
            ndep = [len(d) for d in deps]
            for i, d in enumerate(deps):
                for j in d:
                    users[j].append(i)
            ready = [i for i in range(n) if ndep[i] == 0]
            bl = [0.0] * n
            for i in range(n - 1, -1, -1):
                m = 0.0
                for u in users[i]:
                    if bl[u] > m:
                        m = bl[u]
                bl[i] = ops[i][6] + m + (LAT0 if users[i] else 0.0)
            efree = dict(self.efree)
            fin = [0.0] * n
            order = []
            LAT = 0.4
            while ready:
                best, bt = None, None
                cand = []
                for i in ready:
                    kind, e, fn, reads, writes, kb, cost = ops[i][:7]
                    t = efree.get(e, 0.0) if e is not None else 0.0
                    for j in deps[i]:
                        tj = fin[j] + (0.0 if (ops[j][1] == e and ops[j][0] == "op") or ops[j][0] == "fence" else LAT)
                        if tj > t:
                            t = tj
                    cand.append((t, i))
                    if bt is None or t < bt:
                        bt = t
                best, bbl = None, None
                for (t, i) in cand:
                    if t <= bt + SLACK and (bbl is None or bl[i] > bbl + 1e-9 or (abs(bl[i] - bbl) <= 1e-9 and i < best)):
                        best, bbl, tb = i, bl[i], t
                bt = tb
                i = best
                ready.remove(i)
                kind, e, fn, reads, writes, kb, cost = ops[i][:7]
                if kind == "fence":
                    fin[i] = bt
                elif kind == "dma":
                    efree[e] = bt + (1.4 if e == "pool" else 0.15)
                    fin[i] = bt + cost
                else:
                    efree[e] = bt + cost
                    fin[i] = bt + cost
                order.append(i)
                for u in users[i]:
                    ndep[u] -= 1
                    if ndep[u] == 0:
                        ready.append(u)
            tmax = max(fin) if fin else 0.0
            self.efree = {e: max(0.0, efree.get(e, 0.0) - tmax) for e in self.eng}
            assert len(order) == n
        for i in order:
            kind, e, fn, reads, writes, kb, cost = ops[i][:7]
            if kind == "op":
                self._emit_op(e, fn, reads, writes)
            elif kind == "fence":
                self._emit_fence(ops[i][7])
            else:
                self._emit_dma(e, fn[0], fn[1], reads, writes, kb)
        self.cur_fence = None

    def barrier(self):
        self.flush()
        deps = {self.sem[e]: self.cnt[e] for e in self.eng if self.cnt[e] > 0}
        for kb in self.dsems:
            deps[kb.dsem] = kb.dcnt
        for e in self.eng:
            self._wait(e, dict(deps))
        for kb in self.dsems:
            self.pool[kb.dcls].append((kb.dsem, kb.dcnt))
            kb.dsem = None
        self.dsems = []
        self.epoch = {}

    def finish(self):
        self.barrier()


class Arena:
    def __init__(self, nc, nbytes):
        self.n = nbytes // 4
        self.t = nc.alloc_sbuf_tensor("arena", [128, self.n], F32)
        self.l = 0
        self.r = self.n
        self.peak = 0

    def alloc(self, shape, dt, side="l"):
        p = shape[0]
        n = int(np.prod(shape[1:]))
        words = (n * (2 if dt == BF16 else 4) + 3) // 4
        words = (words + 15) // 16 * 16
        if side == "l":
            off = self.l
            self.l += words
        else:
            self.r -= words
            off = self.r
        assert self.l <= self.r, "SBUF arena overflow l=%d r=%d" % (self.l, self.r)
        self.peak = max(self.peak, self.l + self.n - self.r)
        ap = self.t[0:p, off:off + words]
        if dt == BF16:
            ap = ap.bitcast(BF16)
        ap = ap[:, 0:n]
        if len(shape) == 3:
            ap = ap.rearrange("p (a b) -> p a b", a=shape[1])
        elif len(shape) == 4:
            ap = ap.rearrange("p (a b c) -> p a b c", a=shape[1], b=shape[2])
        return ap

    def mark(self):
        return (self.l, self.r)

    def release(self, m):
        self.l, self.r = m


def build_nc():
    nc = bass.Bass("TRN2", target_bir_lowering=False)

    def din(name, shape):
        return nc.dram_tensor(name, list(shape), F32, kind="ExternalInput").ap()

    def dout(name, shape):
        return nc.dram_tensor(name, list(shape), F32, kind="ExternalOutput").ap()

    xm = din("xm", [NTOK, D])
    xp = din("xp", [NPREV * 128, D])
    pm = din("pm", [NTOK, 256])
    st0 = din("st0", [16, 4, 128, 256])
    g_mix = din("g_mix", [D])
    w_in = din("w_in", [D, 5136])
    w_a2 = din("w_a2", [16, 512])
    b_a = din("b_a", [1, 512])
    g_gla = din("g_gla", [1024])
    g_ln = din("g_ln", [1024])
    b_ln = din("b_ln", [1024])
    w_s = din("w_s", [8, 128, 128])
    b_s = din("b_s", [1, 1024])
    g_gout = din("g_gout", [1024])
    w_out = din("w_out", [D, D])
    g_ffn = din("g_ffn", [D])
    w_f1 = din("w_f1", [D, 2 * DFF])
    w_f2 = din("w_f2", [DFF, D])
    w_ple = din("w_ple", [256, D])
    g_ple = din("g_ple", [D])
    g_pg = din("g_pg", [D])
    w_pg = din("w_pg", [D, D])
    g_fin = din("g_fin", [D])
    y = dout("y", [NTOK, D])
    spo = dout("spo", [4, 128, 256])
    sso = dout("sso", [16, 4, 128, 256])
    vso = dout("vso", [128, 1024])

    w_in_v = w_in.rearrange("(k p) n -> p k n", p=128)
    w_out_v = w_out.rearrange("(k p) n -> p k n", p=128)
    w_f1_v = w_f1.rearrange("(k p) n -> p k n", p=128)
    w_f2_v = w_f2.rearrange("(j p) n -> p j n", p=128)
    w_ple_v = w_ple.rearrange("(k p) n -> p k n", p=128)
    w_pg_v = w_pg.rearrange("(k p) n -> p k n", p=128)

    with ExitStack() as es:
        S = Sched(nc, es)
        AR = Arena(nc, 204 * 1024)
        T = AR.alloc

        pf = [es.enter_context(nc.psum_tensor("pf%d" % i, [128, 512], F32)) for i in range(6)]
        pb = [es.enter_context(nc.psum_tensor("pb%d" % i, [128, 1024], BF16)) for i in range(2)]
        Bpf = S.bufs(6, "pf")
        Bpb = S.bufs(2, "pb")
        rr = {"pf": 0, "pb": 0}
        held = set()

        def nextpf():
            while True:
                i = rr["pf"]
                rr["pf"] = (i + 1) % 6
                if Bpf[i] not in held:
                    return pf[i], Bpf[i]

        def run_streams(*gens):
            gl = []
            for g in gens:
                if g is None:
                    continue
                gl.append(list(g) if isinstance(g, tuple) else [g, 1])
            while gl:
                for item in list(gl):
                    for _ in range(item[1]):
                        try:
                            next(item[0])
                        except StopIteration:
                            gl.remove(item)
                            break

        def nextpb():
            i = rr["pb"]
            rr["pb"] = (i + 1) % 2
            return pb[i], Bpb[i]

        ident = T([128, 128], BF16)
        identf = T([128, 128], F32, "r")
        tri = T([128, 128], F32)
        blk = T([128, 128], F32)
        ET = T([128, 16], F32)
        Ebc = T([128, 16, 128], BF16)
        Ebcf = T([128, 16, 128], F32, "r")
        onescol = T([128, 1], F32)
        onesrow = T([1, 128], F32)
        Bc = S.buf("consts")

        def asel(ap, pattern, base, cm, op=ALU.is_ge, fill=0.0):
            S.op("pool", lambda: nc.gpsimd.affine_select(out=ap, in_=ap, pattern=pattern, compare_op=op, fill=fill,
                                                          base=base, channel_multiplier=cm), writes=[Bc])

        S.op("pool", lambda: nc.gpsimd.memset(identf, 0.0), writes=[Bc])
        asel(identf, [[-1, 128]], 0, 1, op=ALU.not_equal, fill=1.0)
        S.op("pool", lambda: nc.gpsimd.tensor_copy(out=ident, in_=identf), writes=[Bc])
        S.op("pool", lambda: nc.gpsimd.memset(tri, 1.0), writes=[Bc])
        asel(tri, [[1, 128]], 0, -1)
        S.op("pool", lambda: nc.gpsimd.memset(blk, 1.0), writes=[Bc])
        asel(blk, [[1, 128]], 0, -1)
        blk3 = blk.rearrange("p (i s) -> p i s", i=16)
        asel(blk3, [[-8, 16], [0, 8]], 0, 1)
        asel(blk3, [[8, 16], [0, 8]], 7, -1)
        S.op("pool", lambda: nc.gpsimd.memset(ET, 1.0), writes=[Bc])
        asel(ET, [[-8, 16]], 0, 1)
        asel(ET, [[8, 16]], 7, -1)
        S.op("pool", lambda: nc.gpsimd.memset(Ebcf, 1.0), writes=[Bc])
        asel(Ebcf, [[-8, 16], [1, 128]], 0, 0)
        asel(Ebcf, [[8, 16], [-1, 128]], 7, 0)
        S.op("pool", lambda: nc.gpsimd.tensor_copy(out=Ebc, in_=Ebcf), writes=[Bc])
        S.op("pool", lambda: nc.gpsimd.memset(onescol, 1.0), writes=[Bc])
        S.op("pool", lambda: nc.gpsimd.memset(onesrow, 1.0), writes=[Bc])
        CONST = [Bc]

        Sf = T([128, 4, 256], F32)
        Sb = T([128, 4, 256], BF16)
        BSf = S.bufs(4, "Sf")
        BSb = S.bufs(4, "Sb")
        S.op("dve", lambda: nc.vector.memset(Sf, 0.0), writes=BSf)
        S.op("dve", lambda: nc.vector.memset(Sb, 0.0), writes=BSb)

        def load_gain(gt, Bg, vec, n):
            S.dma("sp", gt[:, 0:n], vec.partition_broadcast(128), writes=[Bg])

        class NormWS:
            def __init__(self, tag, side="l"):
                self.xn = [T([128, D], BF16, side) for i in range(2)]
                self.ss = [T([128, 2], F32, side) for i in range(2)]
                self.Bxn = S.bufs(2, "xn" + tag)
                self.Bss = S.bufs(2, "nss" + tag)
                self.i = 0

        def norm_p1(src, Bsrc, gbc, Bg, W):
            i = W.i
            W.i ^= 1
            xn, Bxn, ss, Bss = W.xn[i], W.Bxn[i], W.ss[i], W.Bss[i]
            S.op("act", lambda: nc.scalar.activation(out=xn, in_=src, func=AF.Square, accum_out=ss[:, 0:1]),
                 reads=Bsrc, writes=[Bxn, Bss], cost=1.9)
            S.op("act", lambda: nc.scalar.activation(out=ss[:, 1:2], in_=ss[:, 0:1], func=AF.Sqrt, scale=1.0 / D, bias=EPS),
                 reads=[Bss], writes=[Bss])
            S.op("dve", lambda: nc.vector.reciprocal(out=ss[:, 1:2], in_=ss[:, 1:2]), reads=[Bss], writes=[Bss])
            S.op("dve", lambda: nc.vector.scalar_tensor_tensor(out=xn, in0=src, scalar=ss[:, 1:2], in1=gbc,
                                                              op0=ALU.mult, op1=ALU.mult),
                 reads=list(Bsrc) + [Bss, Bg], writes=[Bxn], cost=2.4)
            return i

        def norm_p2(i, dstT, c0, Bdst, W):
            xn, Bxn = W.xn[i], W.Bxn[i]
            for half in range(2):
                bank, Bb = nextpb()

                def tr(bank=bank, half=half):
                    for j in range(8):
                        k = half * 8 + j
                        ins = nc.tensor.transpose(out=bank[:, j * 128:(j + 1) * 128], in_=xn[:, k * 128:(k + 1) * 128],
                                                  identity=ident)
                    return ins
                S.op("pe", tr, reads=[Bxn, Bc], writes=[Bb])
                dst = dstT[:, half * 8:(half + 1) * 8, c0:c0 + 128]
                src_ps = bank[:].rearrange("p (k t) -> p k t", k=8)
                if half == 0:
                    S.op("act", lambda: nc.scalar.copy(out=dst, in_=src_ps), reads=[Bb], writes=[Bdst[half]])
                else:
                    S.op("dve", lambda: nc.vector.tensor_copy(out=dst, in_=src_ps), reads=[Bb], writes=[Bdst[half]])

        def norm_T_pipe(items, gbc, Bg, dstT, W):
            prev = None
            for (src, Bsrc, c0, Bdst, pre) in items:
                if pre is not None:
                    pre()
                i = norm_p1(src, Bsrc, gbc, Bg, W)
                if prev is not None:
                    norm_p2(prev[0], dstT, prev[1], prev[2], W)
                prev = (i, c0, Bdst)
            norm_p2(prev[0], dstT, prev[1], prev[2], W)

        def transpose_cols(src_tok, Bsrc, nchunk, dst, Bdst, eng):
            bank, Bb = nextpb()

            def tr():
                for j in range(nchunk):
                    ins = nc.tensor.transpose(out=bank[:, j * 128:(j + 1) * 128], in_=src_tok[:, j * 128:(j + 1) * 128],
                                              identity=ident)
                return ins
            S.op("pe", tr, reads=list(Bsrc) + [Bc], writes=[Bb])
            src_ps = bank[:, 0:nchunk * 128].rearrange("p (k t) -> p k t", k=nchunk)
            if eng == "act":
                S.op("act", lambda: nc.scalar.copy(out=dst, in_=src_ps), reads=[Bb], writes=Bdst)
            else:
                S.op("dve", lambda: nc.vector.tensor_copy(out=dst, in_=src_ps), reads=[Bb], writes=Bdst)

        class WStream:
            def __init__(self, tag, shape, n=2, side="l"):
                self.slots = [T(shape, BF16, side) for i in range(n)]
                self.B = S.bufs(n, "w" + tag)
                self.n = n
                self.i = 0

            def load(self, fn):
                i = self.i
                self.i = (i + 1) % self.n
                fn(self.slots[i], self.B[i])
                return self.slots[i], self.B[i]

        def dense16(lhsT_fn, rhs_fn, out_ap, nk=16):
            def f():
                for k in range(nk):
                    ins = nc.tensor.matmul(out_ap, lhsT=lhsT_fn(k), rhs=rhs_fn(k), start=(k == 0), stop=(k == nk - 1))
                return ins
            f.cost = nk * 0.215
            return f

        def gelu_evac(bank, Bb, out_ap, Bout, tmp, Btmp, accum=None, Baccum=None, pool_affine=False):
            if accum is None:
                S.op("act", lambda: nc.scalar.activation(out=out_ap, in_=bank, func=AF.Gelu_apprx_tanh),
                     reads=[Bb], writes=Bout, cost=0.7)
            else:
                S.op("act", lambda: nc.scalar.activation(out=out_ap, in_=bank, func=AF.Gelu_apprx_tanh, accum_out=accum),
                     reads=[Bb], writes=list(Bout) + [Baccum], cost=0.7)

        S.barrier()
        AR.r = AR.n
        catT = T([128, 16, NTOK], BF16, "r")
        BcatTo = S.bufs(NT, "catTo")
        BcatTm = S.bufs(NT, "catTm")
        def mixer_group(xsrc, gt0, ntile, kinds):
            m_grp = AR.mark()
            ntok = ntile * 128
            main = kinds[0] != "prev"
            has_sample = "sample" in kinds
            aT = T([128, 16, ntok], BF16)
            BaT = [S.bufs(2, "aT%d_" % t) for t in range(ntile)]
            ws = WStream("in", [128, 16, 512])

            def loader(col0):
                def f(slot, Bs):
                    S.dma("pool", slot, w_in_v[:, :, col0:col0 + 512], writes=[Bs])
                return f
            pre = [ws.load(loader(C_V)), ws.load(loader(C_V + 512))]
            if main:
                gelu_u = T([128, ntile, 1024], BF16)
                Bgu = [S.bufs(2, "gelu_u%d_" % t) for t in range(ntile)]
                gtmp = [T([128, 512], F32) for i in range(2)]
                Bgtmp = S.bufs(2, "gtmp")
            m_a = AR.mark()
            xs = [T([128, D], F32) for i in range(2)]
            Bxs = S.bufs(2, "xs")
            gt = T([128, D], F32)
            Bg = S.buf("gmixbc")
            load_gain(gt, Bg, g_mix, D)
            W = NormWS("a")

            def mk_pre(t):
                return lambda: S.dma("sp", xs[t % 2], xsrc[(gt0 + t) * 128:(gt0 + t + 1) * 128, :], writes=[Bxs[t % 2]])
            norm_T_pipe([(xs[t % 2], [Bxs[t % 2]], t * 128, BaT[t], mk_pre(t)) for t in range(ntile)], gt, Bg, aT, W)
            if main:
                S.soft_barrier()
                AR.release(m_a)
            m_gla = AR.mark()
            walow = T([128, 16, 16], BF16)
            Bwalow = S.buf("walow")
            wa2 = T([16, 512], F32)
            barow = T([1, 512], F32)
            Bwa = S.buf("wa2")
            Bba = S.buf("barow")
            S.dma("sp", wa2, w_a2, writes=[Bwa])
            S.dma("sp", barow, b_a, writes=[Bba])
            alowT = T([16, ntok], F32)
            BalowT = S.bufs((ntok + 511) // 512, "alowT")
            ek = T([128, ntile, 512], BF16)
            Bek = S.bufs(ntile, "ek")
            dec = T([128, ntile, 64], F32)
            Bdec = S.bufs(ntile, "dec")
            ktok = T([128, ntile, 512], BF16)
            Bktok = S.bufs(ntile, "ktok")
            vtok = T([128, ntile, 1024], BF16)
            Bvtok = [S.bufs(2, "vtok%d_" % t) for t in range(ntile)]
            lp = [T([128, 512], F32) for i in range(2)]
            Blp = S.bufs(2, "lp")
            Tt = T([128, 4, 256], F32)
            BTt = S.bufs(2, "Tt")
            if main:
                eq = T([128, ntile, 512], BF16)
                Beq = S.bufs(ntile, "eq")
                qT = T([128, 4, ntok], BF16)
                BqT = S.bufs(ntile, "qT")
                kT = T([128, 4, ntok], BF16)
                BkT = S.bufs(ntile, "kT")
                silur = T([128, ntile, 1024], BF16)
                Bsilur = [S.bufs(2, "silur%d_" % t) for t in range(ntile)]
                qtmp = [T([128, 512], BF16) for i in range(2)]
                Bqtmp = S.bufs(2, "qtmp")
                stmp = [T([128, 512], F32) for i in range(2)]
                Bstmp = S.bufs(2, "stmp")
                ggla = T([128, 1024], F32)
                Bggla = S.buf("gglabc")
                load_gain(ggla, Bggla, g_gla, 1024)
                attb = [T([128, 4, 128], BF16) for i in range(2)]
                Battb = S.bufs(2, "attb")
                on = [T([128, 1024], BF16) for i in range(2)]
                Bon = S.bufs(2, "on")
                ojunk = T([128, 256], BF16)
                Bojunk = S.buf("ojunk")
                oss = [T([128, 8], F32) for i in range(2)]
                Boss = S.bufs(2, "oss")
                if has_sample:
                    s0f = [T([128, 4, 256], F32) for i in range(2)]
                    Bs0f = S.bufs(2, "s0f")
                    s0b = [T([128, 4, 256], BF16) for i in range(2)]
                    Bs0b = S.bufs(2, "s0b")
                    qTm = [T([128, 4, 128], BF16) for i in range(2)]
                    BqTm = S.bufs(2, "qTm")
                    ktm = [T([128, 512], BF16) for i in range(2)]
                    Bktm = S.bufs(2, "ktm")
            flags = {"decay": False, "gla_in": False}
            kdone = [False] * ntile

            S.dma("pool", walow, w_in_v[:, :, C_A:C_A + 16], writes=[Bwalow])
            for gi, g0 in enumerate(range(0, ntok, 512)):
                n = min(512, ntok - g0)
                bank, Bb = nextpf()
                tl = list(range(g0 // 128, (g0 + n) // 128))
                S.op("pe", dense16(lambda k: walow[:, k, :], lambda k: aT[:, k, g0:g0 + n], bank[0:16, 0:n]),
                     reads=[Bwalow] + [b for t in tl for b in BaT[t]], writes=[Bb])
                S.op("act", lambda: nc.scalar.copy(out=alowT[:, g0:g0 + n], in_=bank[0:16, 0:n]), reads=[Bb], writes=[BalowT[gi]])

            def decay_gen():
                for t in range(ntile):
                    c0 = t * 128
                    sample = kinds[t] == "sample"
                    M = blk if sample else tri
                    nseg = 16 if sample else 1
                    seg = ET if sample else onescol
                    bank, Bb = nextpf()

                    def mm_la(bank=bank, c0=c0):
                        nc.tensor.matmul(bank[:], lhsT=alowT[:, c0:c0 + 128], rhs=wa2, start=True, stop=False)
                        return nc.tensor.matmul(bank[:], lhsT=onesrow, rhs=barow, start=False, stop=True)
                    S.op("pe", mm_la, reads=[BalowT[c0 // 512], Bwa, Bba] + CONST, writes=[Bb])
                    l, Bl = lp[t % 2], Blp[t % 2]
                    S.op("act", lambda: nc.scalar.activation(out=l, in_=bank[:], func=AF.Exp, scale=-1.0), reads=[Bb], writes=[Bl])
                    S.op("act", lambda: nc.scalar.activation(out=l, in_=l, func=AF.Ln, bias=1.0), reads=[Bl], writes=[Bl])
                    yield
                    bank2, Bb2 = nextpf()
                    S.op("pe", lambda: nc.tensor.matmul(bank2[:], lhsT=M, rhs=l, start=True, stop=True),
                         reads=[Bl] + CONST, writes=[Bb2])
                    bank3, Bb3 = nextpf()

                    def mm_cl(bank3=bank3, l=l, seg=seg, nseg=nseg):
                        for h in range(4):
                            ins = nc.tensor.matmul(bank3[:, h * nseg:(h + 1) * nseg], lhsT=l[:, h * 128:(h + 1) * 128],
                                                   rhs=seg, start=True, stop=True)
                        return ins
                    S.op("pe", mm_cl, reads=[Bl] + CONST, writes=[Bb3])
                    if main:
                        S.op("act", lambda: nc.scalar.activation(out=eq[:, t, :], in_=bank2[:], func=AF.Exp, scale=-1.0 / 16),
                             reads=[Bb2], writes=[Beq[t]])
                    S.op("act", lambda: nc.scalar.activation(out=ek[:, t, :], in_=bank2[:], func=AF.Exp, scale=1.0 / 16),
                         reads=[Bb2], writes=[Bek[t]])
                    S.op("act", lambda: nc.scalar.activation(out=dec[:, t, 0:4 * nseg], in_=bank3[:, 0:4 * nseg], func=AF.Exp,
                                                             scale=-1.0 / 16), reads=[Bb3], writes=[Bdec[t]])
                    yield
                flags["decay"] = True

            def ev_k(t, bank, Bb):
                S.op("dve", lambda: nc.vector.tensor_tensor(out=ktok[:, t, :], in0=bank[:], in1=ek[:, t, :], op=ALU.mult),
                     reads=[Bb, Bek[t]], writes=[Bktok[t]])
                if main:
                    transpose_cols(ktok[:, t, :], [Bktok[t]], 4, kT[:, :, t * 128:(t + 1) * 128], [BkT[t]], "act")
                kdone[t] = True

            def ev_q(t, bank, Bb):
                i = t % 2
                S.op("dve", lambda: nc.vector.scalar_tensor_tensor(out=qtmp[i], in0=bank[:], scalar=128.0 ** -0.5,
                                                                  in1=eq[:, t, :], op0=ALU.mult, op1=ALU.mult),
                     reads=[Bb, Beq[t]], writes=[Bqtmp[i]])
                transpose_cols(qtmp[i], [Bqtmp[i]], 4, qT[:, :, t * 128:(t + 1) * 128], [BqT[t]], "act")

            def ev_v(half):
                def f(t, bank, Bb):
                    S.op("act", lambda: nc.scalar.copy(out=vtok[:, t, half * 512:(half + 1) * 512], in_=bank[:]),
                         reads=[Bb], writes=[Bvtok[t][half]])
                return f

            def ev_r(half):
                def f(t, bank, Bb):
                    i = t % 2
                    S.op("act", lambda: nc.scalar.activation(out=stmp[i], in_=bank[:], func=AF.Silu),
                         reads=[Bb], writes=[Bstmp[i]])
                    S.op("dve", lambda: nc.vector.tensor_tensor(out=silur[:, t, half * 512:(half + 1) * 512], in0=stmp[i],
                                                               in1=ggla[:, half * 512:(half + 1) * 512], op=ALU.mult),
                         reads=[Bstmp[i], Bggla], writes=[Bsilur[t][half]])
                return f
            ucnt = [0]

            def ev_u(half):
                def f(t, bank, Bb):
                    i = ucnt[0] % 2
                    ucnt[0] += 1
                    gelu_evac(bank[:], Bb, gelu_u[:, t, half * 512:(half + 1) * 512], [Bgu[t][half]], gtmp[i], Bgtmp[i])
                return f

            specs = [(C_V, ev_v(0), False), (C_V + 512, ev_v(1), False), (C_K, ev_k, True)]
            if main:
                specs += [(C_Q, ev_q, True), (C_R, ev_r(0), False), (C_R + 512, ev_r(1), False),
                          (C_U, ev_u(0), False), (C_U + 512, ev_u(1), False)]
            n_gla_in = 3 if not main else 6
            vgslots = []

            def dense_gen():
                q = list(pre)
                nloaded = len(q)
                for bi, (col0, evac, need) in enumerate(specs):
                    if bi == n_gla_in:
                        flags["gla_in"] = True
                    if need:
                        while not flags["decay"]:
                            yield
                    slot, Bs = q.pop(0)
                    if not q and nloaded < len(specs):
                        q.append(ws.load(loader(specs[nloaded][0])))
                        nloaded += 1
                    elif not q and main and not vgslots:
                        vgslots.append(ws.load(loader(C_VG)))
                    for t in range(ntile):
                        bank, Bb = nextpf()
                        S.op("pe", dense16(lambda k: aT[:, k, t * 128:(t + 1) * 128], lambda k: slot[:, k, :], bank[:]),
                             reads=[Bs] + BaT[t], writes=[Bb])
                        evac(t, bank, Bb)
                        yield
                if main:
                    vgslots.append(ws.load(loader(C_VG + 512)))
                flags["gla_in"] = True

            def state_update(t):
                banks = [nextpf(), nextpf()]

                def mm():
                    for h in range(4):
                        bk = banks[h // 2][0]
                        ins = nc.tensor.matmul(bk[:, (h % 2) * 256:(h % 2 + 1) * 256], lhsT=ktok[:, t, h * 128:(h + 1) * 128],
                                               rhs=vtok[:, t, h * 256:(h + 1) * 256], start=True, stop=True)
                    return ins
                S.op("pe", mm, reads=[Bktok[t]] + Bvtok[t], writes=[banks[0][1], banks[1][1]])
                for hh in range(2):
                    bk, Bbk = banks[hh]
                    S.op("dve", lambda: nc.vector.tensor_tensor(out=Tt[:, 2 * hh:2 * hh + 2, :],
                                                               in0=bk[:].rearrange("p (a v) -> p a v", a=2),
                                                               in1=Sf[:, 2 * hh:2 * hh + 2, :], op=ALU.add),
                         reads=[Bbk, BSf[2 * hh], BSf[2 * hh + 1]], writes=[BTt[hh]])
                decb = dec[:, t, 0:4].unsqueeze(2).to_broadcast([128, 4, 256])
                S.op("dve", lambda: nc.vector.tensor_tensor(out=Sb, in0=Tt, in1=decb, op=ALU.mult),
                     reads=BTt + [Bdec[t]], writes=BSb)
                S.op("pool", lambda: nc.gpsimd.tensor_tensor(out=Sf, in0=Tt, in1=decb, op=ALU.mult),
                     reads=BTt + [Bdec[t]], writes=BSf, cost=2.2)

            def gla_gen():
                while main and not flags["gla_in"]:
                    yield
                for t in range(ntile):
                    if kinds[t] == "prev":
                        while not kdone[t]:
                            yield
                        state_update(t)
                        yield
                        continue
                    c0 = t * 128
                    gc0 = (gt0 + t) * 128
                    sample = kinds[t] == "sample"
                    M = blk if sample else tri
                    nob = 4 if sample else 2
                    ob = [nextpf() for _ in range(nob)]
                    Bob = [x[1] for x in ob]
                    for b_ in Bob:
                        held.add(b_)

                    def oreg(h, ob=ob, nob=nob):
                        if nob == 4:
                            return ob[h][0][:, 0:256]
                        return ob[h // 2][0][:, (h % 2) * 256:(h % 2 + 1) * 256]

                    def obuf(h, Bob=Bob, nob=nob):
                        return Bob[h] if nob == 4 else Bob[h // 2]
                    ab, Bab = nextpf()

                    def mm_att(ab=ab, c0=c0):
                        for h in range(4):
                            ins = nc.tensor.matmul(ab[:, h * 128:(h + 1) * 128], lhsT=kT[:, h, c0:c0 + 128],
                                                   rhs=qT[:, h, c0:c0 + 128], start=True, stop=True)
                        return ins
                    S.op("pe", mm_att, reads=[BkT[t], BqT[t]], writes=[Bab])
                    ai = t % 2
                    S.op("dve", lambda: nc.vector.tensor_tensor(out=attb[ai], in0=ab[:].rearrange("p (h t) -> p h t", h=4),
                                                               in1=M.unsqueeze(1).to_broadcast([128, 4, 128]), op=ALU.mult),
                         reads=[Bab] + CONST, writes=[Battb[ai]])
                    if not sample:
                        yield
                    else:
                        def ld_state(i):
                            sl = i % 2
                            S.dma("sp", s0f[sl], st0[i].rearrange("h k v -> k h v"), writes=[Bs0f[sl]])
                            S.dma("pool", s0b[sl], st0[i].rearrange("h k v -> k h v"), writes=[Bs0b[sl]])
                        for i in range(16):
                            sl = i % 2
                            ld_state(i)
                            S.op("dve", lambda: nc.vector.tensor_tensor(out=qTm[sl], in0=qT[:, :, c0:c0 + 128],
                                                                       in1=Ebc[:, i, :].unsqueeze(1).to_broadcast([128, 4, 128]),
                                                                       op=ALU.mult),
                                 reads=[BqT[t], Bc], writes=[BqTm[sl]])
                            S.op("act", lambda: nc.scalar.activation(out=ktm[sl], in_=ktok[:, t, :], func=AF.Copy,
                                                                     scale=ET[:, i:i + 1]),
                                 reads=[Bktok[t], Bc], writes=[Bktm[sl]])

                            def mm_inter_s(i=i, sl=sl, oreg=oreg):
                                for h in range(4):
                                    ins = nc.tensor.matmul(oreg(h), lhsT=qTm[sl][:, h, :], rhs=s0b[sl][:, h, :],
                                                           start=(i == 0), stop=False)
                                return ins
                            S.op("pe", mm_inter_s, reads=[BqTm[sl], Bs0b[sl]], writes=Bob)
                            sb2 = [nextpf(), nextpf()]

                            def mm_s(sl=sl, sb2=sb2):
                                for h in range(4):
                                    ins = nc.tensor.matmul(sb2[h // 2][0][:, (h % 2) * 256:(h % 2 + 1) * 256],
                                                           lhsT=ktm[sl][:, h * 128:(h + 1) * 128],
                                                           rhs=vtok[:, t, h * 256:(h + 1) * 256], start=True, stop=True)
                                return ins
                            S.op("pe", mm_s, reads=[Bktm[sl]] + Bvtok[t], writes=[sb2[0][1], sb2[1][1]])
                            for hh in range(2):
                                S.op("dve", lambda: nc.vector.tensor_tensor(
                                    out=s0f[sl][:, 2 * hh:2 * hh + 2, :], in0=sb2[hh][0][:].rearrange("p (a v) -> p a v", a=2),
                                    in1=s0f[sl][:, 2 * hh:2 * hh + 2, :], op=ALU.add),
                                    reads=[sb2[hh][1], Bs0f[sl]], writes=[Bs0f[sl]])
                            S.op("dve", lambda: nc.vector.tensor_tensor(
                                out=s0f[sl], in0=s0f[sl],
                                in1=dec[:, t, :].rearrange("p (h i) -> p h i", h=4)[:, :, i:i + 1].to_broadcast([128, 4, 256]),
                                op=ALU.mult), reads=[Bs0f[sl], Bdec[t]], writes=[Bs0f[sl]])
                            S.dma("sp", sso[i].rearrange("h k v -> k h v"), s0f[sl], reads=[Bs0f[sl]], key=Bs0f[sl])
                            yield

                    def mm_o(ai=ai, oreg=oreg, sample=sample, c0=c0):
                        for h in range(4):
                            if not sample:
                                nc.tensor.matmul(oreg(h), lhsT=qT[:, h, c0:c0 + 128], rhs=Sb[:, h, :], start=True, stop=False)
                            ins = nc.tensor.matmul(oreg(h), lhsT=attb[ai][:, h, :], rhs=vtok[:, t, h * 256:(h + 1) * 256],
                                                   start=False, stop=True)
                        return ins
                    S.op("pe", mm_o, reads=[Battb[ai], BqT[t]] + BSb + Bvtok[t], writes=Bob, cost=1.2)
                    if not sample:
                        state_update(t)
                    yield
                    os_, Bos = oss[ai], Boss[ai]
                    for h in range(4):
                        S.op("act", lambda: nc.scalar.activation(out=ojunk, in_=oreg(h), func=AF.Square, accum_out=os_[:, h:h + 1]),
                             reads=[obuf(h)], writes=[Bojunk, Bos])
                    S.op("act", lambda: nc.scalar.activation(out=os_[:, 4:8], in_=os_[:, 0:4], func=AF.Sqrt, scale=1.0 / 256, bias=EPS),
                         reads=[Bos], writes=[Bos])
                    S.op("dve", lambda: nc.vector.reciprocal(out=os_[:, 4:8], in_=os_[:, 4:8]), reads=[Bos], writes=[Bos])
                    yield
                    for h in range(4):
                        S.op("dve", lambda: nc.vector.scalar_tensor_tensor(
                            out=on[ai][:, h * 256:(h + 1) * 256], in0=oreg(h),
                            scalar=os_[:, 4 + h:5 + h], in1=silur[:, t, h * 256:(h + 1) * 256], op0=ALU.mult, op1=ALU.mult),
                            reads=[obuf(h), Bos] + Bsilur[t], writes=[Bon[ai]])
                    for b_ in Bob:
                        held.discard(b_)
                    transpose_cols(on[ai], [Bon[ai]], 8, catT[:, 0:8, gc0:gc0 + 128], [BcatTo[gt0 + t]], "act")
                    yield

            run_streams(decay_gen(), dense_gen(), gla_gen())
            S.soft_barrier()
            AR.release(m_gla)
            if main:
                gmlp_group(aT, BaT, vgslots, gt0, ntile, kinds, gelu_u, Bgu, gtmp, Bgtmp)
            S.barrier()
            AR.release(m_grp)

        def gmlp_group(aT, BaT, vslots, gt0, ntile, kinds, gelu_u, Bgu, gtmp, Bgtmp):
            has_sample = "sample" in kinds
            wsb = T([128, 8, 128], BF16)
            Bwsb = S.buf("wsb")
            S.dma("pool", wsb, w_s.rearrange("g t s -> t g s"), writes=[Bwsb])
            WT = T([128, 8, 128], BF16)
            BWT = S.buf("WT")
            bank, Bb = nextpb()

            def trw():
                for g in range(8):
                    ins = nc.tensor.transpose(out=bank[:, g * 128:(g + 1) * 128], in_=wsb[:, g, :], identity=ident)
                return ins
            S.op("pe", trw, reads=[Bwsb, Bc], writes=[Bb])
            S.op("dve", lambda: nc.vector.tensor_tensor(out=WT, in0=bank[:].rearrange("p (g t) -> p g t", g=8),
                                                       in1=tri.unsqueeze(1).to_broadcast([128, 8, 128]), op=ALU.mult),
                 reads=[Bb, Bc], writes=[BWT])
            bs2 = T([2, 1024], F32)
            bhi = T([2, 1024], BF16)
            blo = T([2, 1024], F32)
            bsrow = T([2, 1024], BF16)
            ones2 = T([2, 128], BF16)
            Bbs = S.buf("bsrow")
            S.dma("sp", bs2, b_s[0].partition_broadcast(2), writes=[Bbs])
            S.op("dve", lambda: nc.vector.memset(ones2, 1.0), writes=[Bbs])
            S.op("dve", lambda: nc.vector.tensor_copy(out=bhi, in_=bs2), reads=[Bbs], writes=[Bbs])
            S.op("dve", lambda: nc.vector.tensor_tensor(out=blo, in0=bs2, in1=bhi, op=ALU.subtract), reads=[Bbs], writes=[Bbs])
            S.op("dve", lambda: nc.vector.tensor_scalar(out=bs2, in0=bhi, scalar1=ident[0:2, 0:1], scalar2=None, op0=ALU.mult),
                 reads=[Bbs, Bc], writes=[Bbs])
            S.op("dve", lambda: nc.vector.scalar_tensor_tensor(out=bsrow, in0=blo, scalar=ident[0:2, 1:2], in1=bs2,
                                                              op0=ALU.mult, op1=ALU.add), reads=[Bbs, Bc], writes=[Bbs])
            if has_sample:
                Ball = T([8, 8, 16, 8], BF16)
                Arep = T([8, 16, 8], BF16)
                BBall = S.buf("Ball")
                S.op("dve", lambda: nc.vector.tensor_copy(out=Ball, in_=WT[0:8, :, 0:8].unsqueeze(2).to_broadcast([8, 8, 16, 8])),
                     reads=[BWT], writes=[BBall])
                S.op("dve", lambda: nc.vector.tensor_copy(out=Arep, in_=ident[0:8, 0:8].unsqueeze(1).to_broadcast([8, 16, 8])),
                     reads=[Bc], writes=[BBall])
                Wblk = T([128, 8, 128], BF16)
                BWblk = S.bufs(2, "Wblk")
                for half in range(2):
                    bk, Bbk = nextpf()

                    def mmw(bk=bk, half=half):
                        for g4 in range(4):
                            g = half * 4 + g4
                            ins = nc.tensor.matmul(bk[:, g4 * 128:(g4 + 1) * 128], lhsT=Arep.rearrange("p i s -> p (i s)"),
                                                   rhs=Ball[:, g, :, :].rearrange("p j t -> p (j t)"), start=True, stop=True)
                        return ins
                    S.op("pe", mmw, reads=[BBall], writes=[Bbk])
                    S.op("dve", lambda: nc.vector.tensor_tensor(out=Wblk[:, half * 4:(half + 1) * 4, :],
                                                               in0=bk[:].rearrange("p (g t) -> p g t", g=4),
                                                               in1=blk.unsqueeze(1).to_broadcast([128, 4, 128]), op=ALU.mult),
                         reads=[Bbk, Bc], writes=[BWblk[half]])
                bsrow_s = T([2, 1024], BF16)
                Bbss = S.buf("bsrow_s")
                S.op("dve", lambda: nc.vector.tensor_copy(
                    out=bsrow_s.rearrange("p (g j t) -> p g j t", g=8, j=16),
                    in_=bsrow.rearrange("p (g t) -> p g t", g=8)[:, :, 0:8].unsqueeze(2).to_broadcast([2, 8, 16, 8])),
                    reads=[Bbs], writes=[Bbss])
            vn = T([128, ntile, 1024], BF16)
            Bvn = S.bufs(ntile, "vn")
            gv = [T([128, 1024], F32) for i in range(2)]
            Bgv = [S.bufs(2, "gv%d_" % i) for i in range(2)]
            glnbc = T([128, 1024], F32)
            blnbc = T([128, 1024], F32)
            goutbc = T([128, 1024], F32)
            Bgl = S.bufs(3, "lngain")
            load_gain(glnbc, Bgl[0], g_ln, 1024)
            load_gain(blnbc, Bgl[1], b_ln, 1024)
            load_gain(goutbc, Bgl[2], g_gout, 1024)
            st = [T([128, 8], F32) for i in range(2)]
            Bst = S.bufs(2, "lnst")
            Bsth = [S.bufs(2, "lnsth%d_" % i) for i in range(2)]
            st2 = [T([128, 2], F32) for i in range(2)]
            Bst2 = S.bufs(2, "mst")
            junk = T([128, 1024], BF16)
            Bjunk = S.buf("junk")
            mr = [T([128, 1024], F32) for i in range(2)]
            Bmr = [S.bufs(2, "mr%d_" % i) for i in range(2)]
            mn = [T([128, 1024], BF16) for i in range(2)]
            Bmn = S.bufs(2, "mn")
            vout = T([128, 1024], F32)
            Bvout = S.buf("vout")
            cnt = [0]

            def stage_a(t):
                gi = t % 2
                for half in range(2):
                    slot, Bs = vslots[half]
                    bk, Bbk = nextpf()
                    S.op("pe", dense16(lambda k: aT[:, k, t * 128:(t + 1) * 128], lambda k: slot[:, k, :], bk[:]),
                         reads=[Bs] + BaT[t], writes=[Bbk])
                    i = cnt[0] % 2
                    cnt[0] += 1
                    gelu_evac(bk[:], Bbk, gv[gi][:, half * 512:(half + 1) * 512], [Bgv[gi][half]], gtmp[i], Bgtmp[i],
                              accum=st[gi][:, half:half + 1], Baccum=Bsth[gi][half], pool_affine=True)
                    yield

            def stage_b(t):
                sample = kinds[t] == "sample"
                gi = t % 2
                s_, Bs_ = st[gi], Bst[gi]
                g_ = gv[gi]
                Bg_ = Bgv[gi]
                S.op("act", lambda: nc.scalar.activation(out=junk, in_=g_, func=AF.Square, scale=1.0 / 32, accum_out=s_[:, 2:3]),
                     reads=Bg_, writes=[Bjunk, Bs_])
                S.op("dve", lambda: nc.vector.tensor_scalar(out=s_[:, 3:4], in0=s_[:, 0:1], scalar1=s_[:, 1:2], scalar2=1.0 / 1024,
                                                           op0=ALU.add, op1=ALU.mult), reads=Bsth[gi], writes=[Bs_])
                yield
                S.op("dve", lambda: nc.vector.tensor_scalar(out=s_[:, 4:5], in0=s_[:, 3:4], scalar1=s_[:, 3:4], scalar2=-1.0,
                                                           op0=ALU.mult, op1=ALU.mult), reads=[Bs_], writes=[Bs_])
                S.op("dve", lambda: nc.vector.tensor_tensor(out=s_[:, 5:6], in0=s_[:, 4:5], in1=s_[:, 2:3], op=ALU.add),
                     reads=[Bs_], writes=[Bs_])
                S.op("act", lambda: nc.scalar.activation(out=s_[:, 6:7], in_=s_[:, 5:6], func=AF.Sqrt, scale=1.0, bias=EPS),
                     reads=[Bs_], writes=[Bs_])
                S.op("dve", lambda: nc.vector.reciprocal(out=s_[:, 6:7], in_=s_[:, 6:7]), reads=[Bs_], writes=[Bs_])
                yield
                S.op("dve", lambda: nc.vector.tensor_scalar(out=g_, in0=g_, scalar1=s_[:, 3:4], scalar2=s_[:, 6:7],
                                                           op0=ALU.subtract, op1=ALU.mult), reads=Bg_ + [Bs_], writes=Bg_)
                yield
                S.op("pool", lambda: nc.gpsimd.tensor_tensor(out=g_, in0=g_, in1=glnbc, op=ALU.mult), reads=Bg_ + [Bgl[0]], writes=Bg_)
                yield
                if sample:
                    S.op("pool", lambda: nc.gpsimd.tensor_tensor(out=vout, in0=g_, in1=blnbc, op=ALU.add),
                         reads=Bg_ + [Bgl[1]], writes=[Bvout])
                    S.dma("sp", vso, vout, reads=[Bvout], key=Bvout)
                    S.op("act", lambda: nc.scalar.copy(out=vn[:, t, :], in_=vout), reads=[Bvout], writes=[Bvn[t]])
                else:
                    S.op("pool", lambda: nc.gpsimd.tensor_tensor(out=vn[:, t, :], in0=g_, in1=blnbc, op=ALU.add),
                         reads=Bg_ + [Bgl[1]], writes=[Bvn[t]])
                yield

            def stage_c(t):
                sample = kinds[t] == "sample"
                gc0 = (gt0 + t) * 128
                Wm = Wblk if sample else WT
                BWm = BWblk if sample else [BWT]
                br = bsrow_s if sample else bsrow
                Bbr = Bbss if sample else Bbs
                mb = [nextpf(), nextpf()]
                mi = t % 2

                def mmx(mb=mb, Wm=Wm, br=br, t=t):
                    for g in range(8):
                        reg = mb[g // 4][0][:, (g % 4) * 128:(g % 4 + 1) * 128]
                        nc.tensor.matmul(reg, lhsT=Wm[:, g, :], rhs=vn[:, t, g * 128:(g + 1) * 128], start=True, stop=False)
                        ins = nc.tensor.matmul(reg, lhsT=br[:, g * 128:(g + 1) * 128], rhs=ones2, start=False, stop=True)
                    return ins
                S.op("pe", mmx, reads=BWm + [Bvn[t], Bbr, Bbs], writes=[mb[0][1], mb[1][1]])
                for half in range(2):
                    S.op("dve", lambda: nc.vector.tensor_tensor(out=mr[mi][:, half * 512:(half + 1) * 512], in0=mb[half][0][:],
                                                               in1=gelu_u[:, t, half * 512:(half + 1) * 512], op=ALU.mult),
                         reads=[mb[half][1], Bgu[t][half]], writes=[Bmr[mi][half]])
                yield
                s_, Bs_ = st2[mi], Bst2[mi]
                S.op("act", lambda: nc.scalar.activation(out=junk, in_=mr[mi], func=AF.Square, scale=1.0 / 32, accum_out=s_[:, 0:1]),
                     reads=Bmr[mi], writes=[Bjunk, Bs_])
                S.op("act", lambda: nc.scalar.activation(out=s_[:, 1:2], in_=s_[:, 0:1], func=AF.Sqrt, scale=1.0, bias=EPS),
                     reads=[Bs_], writes=[Bs_])
                S.op("dve", lambda: nc.vector.reciprocal(out=s_[:, 1:2], in_=s_[:, 1:2]), reads=[Bs_], writes=[Bs_])
                S.op("dve", lambda: nc.vector.scalar_tensor_tensor(out=mn[mi], in0=mr[mi], scalar=s_[:, 1:2], in1=goutbc,
                                                                  op0=ALU.mult, op1=ALU.mult),
                     reads=Bmr[mi] + [Bs_, Bgl[2]], writes=[Bmn[mi]])
                yield
                transpose_cols(mn[mi], [Bmn[mi]], 8, catT[:, 8:16, gc0:gc0 + 128], [BcatTm[gt0 + t]], "act")
                yield

            def G_(fn, t):
                return fn(t) if 0 <= t < ntile else None
            run_streams(stage_a(0))
            run_streams(G_(stage_a, 1), stage_b(0))
            for t in range(ntile):
                run_streams(G_(stage_a, t + 2), G_(stage_b, t + 1), stage_c(t))

        mixer_group(xp, 0, 8, ["prev"] * 8)
        mixer_group(xm, 0, 3, ["prompt"] * 3)
        mixer_group(xm, 3, 3, ["prompt"] * 3)
        mixer_group(xm, 6, 3, ["prompt", "prompt", "sample"])
        Bspo = S.buf("spo")
        S.dma("sp", spo.rearrange("h k v -> k h v"), Sf, reads=BSf, key=Bspo)

        h = T([128, NT, D], F32)
        Bh = [S.bufs(4, "h%d_" % t) for t in range(NT)]
        for t in range(NT):
            S.dma("sp", h[:, t, :], xm[t * 128:(t + 1) * 128, :], writes=Bh[t])
        m_w = AR.mark()
        ws = WStream("out", [128, 16, 512])

        def ld_out(db):
            def f(slot, Bs):
                S.dma("pool", slot, w_out_v[:, :, db * 512:(db + 1) * 512], writes=[Bs])
            return f
        nxt = ws.load(ld_out(0))
        for db in range(4):
            slot, Bs = nxt
            if db < 3:
                nxt = ws.load(ld_out(db + 1))
            for t in range(NT):
                bk, Bbk = nextpf()
                S.op("pe", dense16(lambda k: catT[:, k, t * 128:(t + 1) * 128], lambda k: slot[:, k, :], bk[:]),
                     reads=[Bs, BcatTo[t], BcatTm[t]], writes=[Bbk])
                hs = h[:, t, db * 512:(db + 1) * 512]
                S.op("dve", lambda: nc.vector.tensor_tensor(out=hs, in0=bk[:], in1=hs, op=ALU.add),
                     reads=[Bbk, Bh[t][db]], writes=[Bh[t][db]])
        S.barrier()
        AR.release(m_w)
        AR.r = AR.n

        wple = T([128, 2, D], BF16)
        pT = T([128, 2, NTOK], BF16)
        rpe = T([128, NT], F32)
        ps1 = T([128, 256], F32)
        pbf1 = T([128, 256], BF16)
        junk2 = T([128, 512], BF16)
        pss = [T([128, 8], F32) for i in range(2)]
        m_f = AR.mark()
        fT = T([128, 16, NTOK], BF16)
        BfT = [S.bufs(2, "fT%d_" % t) for t in range(NT)]
        m_fa = AR.mark()
        gt = T([128, D], F32)
        Bg = S.buf("gffnbc")
        load_gain(gt, Bg, g_ffn, D)
        W = NormWS("f")
        norm_T_pipe([(h[:, t, :], Bh[t], t * 128, BfT[t], None) for t in range(NT)], gt, Bg, fT, W)
        S.barrier()
        AR.release(m_fa)
        w1 = WStream("f1", [128, 2, 16, 256])
        w1.B = [S.bufs(2, "wf1_%d_" % i) for i in range(2)]
        w2 = WStream("f2", [128, 4, D], n=1)
        hT = [T([128, 4, NTOK], BF16) for i in range(2)]
        BhT = [S.bufs(4, "hT%d_" % i) for i in range(2)]
        sg = [T([128, 512], BF16) for i in range(2)]
        Bsg = S.bufs(2, "sg")
        groups = [(0, 512), (512, 512), (1024, 128)]
        NFB = DFF // 512

        def ld_f1(u):
            def f(slot, Bs):
                S.dma("pool", slot[:, 0, :, :], w_f1_v[:, :, u * 256:(u + 1) * 256], writes=[Bs[0]])
                S.dma("pool", slot[:, 1, :, :], w_f1_v[:, :, DFF + u * 256:DFF + (u + 1) * 256], writes=[Bs[1]])
            return f

        def ld_f2(fb):
            def f(slot, Bs):
                S.dma("pool", slot, w_f2_v[:, fb * 4:(fb + 1) * 4, :], writes=[Bs])
            return f
        cnt = [0]
        f1_next = [None]

        def ffn1(fb):
            hi = fb % 2
            for uu in range(2):
                u = fb * 2 + uu
                slot, Bs = f1_next[0]
                if u + 1 < 2 * NFB:
                    f1_next[0] = w1.load(ld_f1(u + 1))
                for c in range(2):
                    j = uu * 2 + c
                    for (g0, n) in groups:
                        tl = list(range(g0 // 128, (g0 + n) // 128))
                        rd = Bs + [b for t in tl for b in BfT[t]]
                        gb, Bgb = nextpf()
                        S.op("pe", dense16(lambda k: slot[:, 0, k, c * 128:(c + 1) * 128], lambda k: fT[:, k, g0:g0 + n], gb[:, 0:n]),
                             reads=rd, writes=[Bgb])
                        ub, Bub = nextpf()
                        S.op("pe", dense16(lambda k: slot[:, 1, k, c * 128:(c + 1) * 128], lambda k: fT[:, k, g0:g0 + n], ub[:, 0:n]),
                             reads=rd, writes=[Bub])
                        i = cnt[0] % 2
                        cnt[0] += 1
                        S.op("act", lambda: nc.scalar.activation(out=sg[i][:, 0:n], in_=gb[:, 0:n], func=AF.Silu),
                             reads=[Bgb], writes=[Bsg[i]])
                        S.op("dve", lambda: nc.vector.tensor_tensor(out=hT[hi][:, j, g0:g0 + n], in0=ub[:, 0:n], in1=sg[i][:, 0:n],
                                                                   op=ALU.mult),
                             reads=[Bub, Bsg[i]], writes=[BhT[hi][j]])

        def ffn2(fb, slot2, Bs2):
            hi = fb % 2
            for t in range(NT):
                for db in range(4):
                    bk, Bbk = nextpf()

                    def mm(bk=bk, db=db, t=t):
                        for j in range(4):
                            ins = nc.tensor.matmul(bk[:], lhsT=hT[hi][:, j, t * 128:(t + 1) * 128],
                                                   rhs=slot2[:, j, db * 512:(db + 1) * 512], start=(j == 0), stop=(j == 3))
                        return ins
                    S.op("pe", mm, reads=[Bs2] + BhT[hi], writes=[Bbk])
                    hs = h[:, t, db * 512:(db + 1) * 512]
                    S.op("dve", lambda: nc.vector.tensor_tensor(out=hs, in0=bk[:], in1=hs, op=ALU.add),
                         reads=[Bbk, Bh[t][db]], writes=[Bh[t][db]])

        f1_next[0] = w1.load(ld_f1(0))
        cur2 = w2.load(ld_f2(0))
        Bwple = S.buf("wple")
        BpT = S.bufs(NT, "pT")
        Brpe = S.bufs(NT, "rpe")
        Bps1, Bpbf1, Bjunk2 = S.buf("ps1"), S.buf("pbf1"), S.buf("junk2")
        Bpss = S.bufs(2, "pss")
        S.dma("pool", wple, w_ple_v, writes=[Bwple])

        def pe_mm(bk, t, db):
            def f():
                for kk in range(2):
                    ins = nc.tensor.matmul(bk[:], lhsT=pT[:, kk, t * 128:(t + 1) * 128], rhs=wple[:, kk, db * 512:(db + 1) * 512],
                                           start=(kk == 0), stop=(kk == 1))
                return ins
            return f
        for t in range(NT):
            i = t % 2
            S.dma("sp", ps1, pm[t * 128:(t + 1) * 128, :], writes=[Bps1])
            S.op("dve", lambda: nc.vector.tensor_copy(out=pbf1, in_=ps1), reads=[Bps1], writes=[Bpbf1], cost=0.3)
            transpose_cols(pbf1, [Bpbf1], 2, pT[:, :, t * 128:(t + 1) * 128], [BpT[t]], "act")
            for db in range(4):
                bk, Bbk = nextpf()
                S.op("pe", pe_mm(bk, t, db), reads=[BpT[t], Bwple], writes=[Bbk], cost=0.5)
                S.op("act", lambda: nc.scalar.activation(out=junk2, in_=bk[:], func=AF.Square, accum_out=pss[i][:, db:db + 1]),
                     reads=[Bbk], writes=[Bjunk2, Bpss[i]], cost=0.7)
            S.op("dve", lambda: nc.vector.reduce_sum(out=pss[i][:, 4:5], in_=pss[i][:, 0:4], axis=AX.X),
                 reads=[Bpss[i]], writes=[Bpss[i]], cost=0.2)
            S.op("act", lambda: nc.scalar.activation(out=pss[i][:, 5:6], in_=pss[i][:, 4:5], func=AF.Sqrt, scale=1.0 / D, bias=EPS),
                 reads=[Bpss[i]], writes=[Bpss[i]], cost=0.3)
            S.op("dve", lambda: nc.vector.reciprocal(out=rpe[:, t:t + 1], in_=pss[i][:, 5:6]), reads=[Bpss[i]], writes=[Brpe[t]], cost=0.2)
        ffn1(0)
        for fb in range(NFB):
            if fb + 1 < NFB:
                ffn1(fb + 1)
            ffn2(fb, cur2[0], cur2[1])
            if fb + 1 < NFB:
                cur2 = w2.load(ld_f2(fb + 1))
        S.barrier()
        AR.release(m_f)

        gT = T([128, 16, NTOK], BF16)
        BgT = [S.bufs(2, "gT%d_" % t) for t in range(NT)]
        gplebc = T([128, D], F32)
        Bgp = S.bufs(2, "gplefin")
        load_gain(gplebc, Bgp[0], g_ple, D)
        m_g2 = AR.mark()
        ws = WStream("pg", [128, 16, 512])

        def ld_pg(db):
            def f(slot, Bs):
                S.dma("pool", slot, w_pg_v[:, :, db * 512:(db + 1) * 512], writes=[Bs])
            return f
        pgq = [ws.load(ld_pg(0)), ws.load(ld_pg(1))]
        m_ga = AR.mark()
        gt = T([128, D], F32)
        Bg = S.buf("gpgbc")
        load_gain(gt, Bg, g_pg, D)
        W = NormWS("g")
        def gT_gen():
            prev = None
            for t in range(NT):
                i = norm_p1(h[:, t, :], Bh[t], gt, Bg, W)
                yield
                if prev is not None:
                    norm_p2(prev[0], gT, prev[1], prev[2], W)
                    yield
                prev = (i, t * 128, BgT[t])
            norm_p2(prev[0], gT, prev[1], prev[2], W)
            yield

        run_streams(gT_gen())
        S.soft_barrier()
        AR.release(m_ga)
        sig = [T([128, 512], F32) for i in range(2)]
        Bsig = S.bufs(2, "sig")
        pet = [T([128, 512], F32) for i in range(2)]
        Bpet = S.bufs(2, "pet")
        cnt = [0]
        for db in range(4):
            slot, Bs = pgq.pop(0)
            for t in range(NT):
                i = cnt[0] % 2
                cnt[0] += 1
                gb, Bgb = nextpf()
                S.op("pe", dense16(lambda k: gT[:, k, t * 128:(t + 1) * 128], lambda k: slot[:, k, :], gb[:]),
                     reads=[Bs] + BgT[t], writes=[Bgb])
                pk, Bpk = nextpf()
                S.op("pe", pe_mm(pk, t, db), reads=[BpT[t], Bwple], writes=[Bpk])
                S.op("act", lambda: nc.scalar.activation(out=sig[i], in_=gb[:], func=AF.Sigmoid), reads=[Bgb], writes=[Bsig[i]])
                S.op("dve", lambda: nc.vector.scalar_tensor_tensor(out=pet[i], in0=pk[:], scalar=rpe[:, t:t + 1],
                                                                  in1=gplebc[:, db * 512:(db + 1) * 512], op0=ALU.mult, op1=ALU.mult),
                     reads=[Bpk, Brpe[t], Bgp[0]], writes=[Bpet[i]])
                S.op("dve", lambda: nc.vector.tensor_tensor(out=pet[i], in0=pet[i], in1=sig[i], op=ALU.mult),
                     reads=[Bpet[i], Bsig[i]], writes=[Bpet[i]])
                hs = h[:, t, db * 512:(db + 1) * 512]
                S.op("dve", lambda: nc.vector.tensor_tensor(out=hs, in0=hs, in1=pet[i], op=ALU.add),
                     reads=[Bpet[i], Bh[t][db]], writes=[Bh[t][db]])
            if db + 2 < 4:
                pgq.append(ws.load(ld_pg(db + 2)))
        S.soft_barrier()
        AR.release(m_g2)
        gfinbc = T([128, D], F32)
        Bgfin = S.buf("gfinbc")
        load_gain(gfinbc, Bgfin, g_fin, D)
        fss = [T([128, 2], F32) for i in range(2)]
        Bfss = S.bufs(2, "fss")
        junk3 = T([128, D], BF16)
        Bjunk3 = S.buf("junk3")

        def fin_p1(t):
            i = t % 2
            hs = h[:, t, :]
            S.op("act", lambda: nc.scalar.activation(out=junk3, in_=hs, func=AF.Square, accum_out=fss[i][:, 0:1]),
                 reads=Bh[t], writes=[Bjunk3, Bfss[i]], cost=1.9)
            S.op("act", lambda: nc.scalar.activation(out=fss[i][:, 1:2], in_=fss[i][:, 0:1], func=AF.Sqrt, scale=1.0 / D, bias=EPS),
                 reads=[Bfss[i]], writes=[Bfss[i]])
            S.op("dve", lambda: nc.vector.reciprocal(out=fss[i][:, 1:2], in_=fss[i][:, 1:2]), reads=[Bfss[i]], writes=[Bfss[i]])

        def fin_p2(t):
            i = t % 2
            hs = h[:, t, :]
            S.op("dve", lambda: nc.vector.scalar_tensor_tensor(out=hs, in0=hs, scalar=fss[i][:, 1:2], in1=gfinbc,
                                                              op0=ALU.mult, op1=ALU.mult),
                 reads=Bh[t] + [Bfss[i], Bgfin], writes=Bh[t], cost=2.4)
            S.dma("sp", y[t * 128:(t + 1) * 128, :], hs, reads=Bh[t], key=Bh[t][0])
        fin_p1(0)
        for t in range(NT):
            if t + 1 < NT:
                fin_p1(t + 1)
            fin_p2(t)
        S.finish()
        print("[kernel] SBUF arena peak words %d / %d ; engine op counts %s" % (AR.peak, AR.n, S.cnt))
    return nc


_NC_CACHE = {}


def kernel(x_prompt, x_sample, state_gla, p_prompt, p_sample, g_mix, w_in, w_a2, b_a, g_gla_norm, g_gmlp_ln, b_gmlp_ln,
           w_s, b_s, g_gmlp_out, w_out, g_ffn, w_ffn_in, w_ffn_out, w_ple, g_ple, g_ple_gate, w_ple_gate, g_final):
    f = lambda a: np.ascontiguousarray(np.asarray(a, dtype=np.float32))
    x_prompt, x_sample, state_gla, p_prompt, p_sample = f(x_prompt), f(x_sample), f(state_gla), f(p_prompt), f(p_sample)
    shared = {
        "g_mix": f(g_mix).reshape(D), "w_in": f(w_in).reshape(D, 5136), "w_a2": f(w_a2).reshape(16, 512),
        "b_a": f(b_a).reshape(1, 512), "g_gla": f(g_gla_norm).reshape(1024), "g_ln": f(g_gmlp_ln).reshape(1024),
        "b_ln": f(b_gmlp_ln).reshape(1024), "w_s": f(w_s).reshape(8, 128, 128), "b_s": f(b_s).reshape(1, 1024),
        "g_gout": f(g_gmlp_out).reshape(1024), "w_out": f(w_out).reshape(D, D), "g_ffn": f(g_ffn).reshape(D),
        "w_f1": f(w_ffn_in).reshape(D, 2 * DFF), "w_f2": f(w_ffn_out).reshape(DFF, D), "w_ple": f(w_ple).reshape(256, D),
        "g_ple": f(g_ple).reshape(D), "g_pg": f(g_ple_gate).reshape(D), "w_pg": f(w_ple_gate).reshape(D, D),
        "g_fin": f(g_final).reshape(D),
    }
    in_maps = []
    for c in range(8):
        b, half = c // 2, c % 2
        xs = x_sample[16 * c:16 * c + 16].reshape(128, D)
        ps = p_sample[0, 16 * c:16 * c + 16].reshape(128, 256)
        xm = np.concatenate([x_prompt[b, half * 1024:(half + 1) * 1024], xs], axis=0)
        pmm = np.concatenate([p_prompt[0, b, half * 1024:(half + 1) * 1024], ps], axis=0)
        xp = x_prompt[b, 0:1024] if half == 1 else np.zeros((1024, D), np.float32)
        m = {"xm": np.ascontiguousarray(xm), "xp": np.ascontiguousarray(xp), "pm": np.ascontiguousarray(pmm),
             "st0": np.ascontiguousarray(state_gla[0, 16 * c:16 * c + 16])}
        m.update(shared)
        in_maps.append(m)
    if "nc" not in _NC_CACHE:
        _NC_CACHE["nc"] = build_nc()
    nc = _NC_CACHE["nc"]
    res = run_bass_kernel_spmd(nc, in_maps, core_ids=list(range(8)))
    r = res.results
    y_prompt = np.zeros((4, 2048, D), np.float32)
    y_sample = np.zeros((128, 8, D), np.float32)
    sp = np.zeros((1, 4, 4, 128, 256), np.float32)
    ss = np.zeros((1, 128, 4, 128, 256), np.float32)
    vs = np.zeros((1, 128, 8, 1024), np.float32)
    for c in range(8):
        b, half = c // 2, c % 2
        yc = np.asarray(r[c]["y"])
        y_prompt[b, half * 1024:(half + 1) * 1024] = yc[0:1024]
        y_sample[16 * c:16 * c + 16] = yc[1024:1152].reshape(16, 8, D)
        if half == 1:
            sp[0, b] = np.asarray(r[c]["spo"])
        ss[0, 16 * c:16 * c + 16] = np.asarray(r[c]["sso"])
        vs[0, 16 * c:16 * c + 16] = np.asarray(r[c]["vso"]).reshape(16, 8, 1024)
    return (y_prompt, y_sample, sp, ss, vs)
```

```python
import numpy as np
import concourse.bass as bass
import concourse.mybir as mybir
from concourse.bass_utils import run_bass_kernel_spmd
from contextlib import ExitStack
import types

F32 = mybir.dt.float32
BF16 = mybir.dt.bfloat16
AF = mybir.ActivationFunctionType
ALU = mybir.AluOpType
AX = mybir.AxisListType

D = 2048
NPT = 8
NT = 9
NTOK = NT * 128
NPREV = 8
DFF = 5632
EPS = 1e-6
C_Q, C_K, C_V, C_R, C_A, C_U, C_VG = 0, 512, 1024, 2048, 3072, 3088, 4112
GELU_C = 1.5957691216057308


REORDER = True
SLACK = 1.0
LAT0 = 0.4


def _snap(fn, depth=0):
    if not isinstance(fn, types.FunctionType) or depth > 4:
        return fn
    cl = fn.__closure__
    if not cl:
        return fn
    cells = []
    for c in cl:
        try:
            v = c.cell_contents
        except ValueError:
            cells.append(c)
            continue
        if isinstance(v, types.FunctionType):
            v = _snap(v, depth + 1)
        cells.append(types.CellType(v))
    g = types.FunctionType(fn.__code__, fn.__globals__, fn.__name__, fn.__defaults__, tuple(cells))
    g.__kwdefaults__ = fn.__kwdefaults__
    return g


class Buf:
    __slots__ = ("name", "w", "r", "dsem", "dcnt", "dcls")

    def __init__(self, name):
        self.name = name
        self.w = {}
        self.r = {}
        self.dsem = None
        self.dcnt = 0
        self.dcls = None


class Sched:
    def __init__(self, nc, es):
        self.nc = nc
        self.es = es
        self.eng = {"pe": nc.tensor, "act": nc.scalar, "dve": nc.vector, "pool": nc.gpsimd, "sp": nc.sync}
        self.sem = {e: es.enter_context(nc.semaphore("sem_" + e)) for e in self.eng}
        self.cnt = {e: 0 for e in self.eng}
        self.waited = {e: {} for e in self.eng}
        self.dsems = []
        self.pool = {"sw": [], "hw": []}
        self.pend = []
        self.efree = {}
        self.cur_fence = None
        self.epoch = {}
        self.nsem = 0
        self.nb = 0

    def buf(self, name=None):
        self.nb += 1
        b = Buf(name or ("b%d" % self.nb))
        b.w = dict(self.epoch)
        if self.cur_fence is not None:
            self.cur_fence.append(b)
        return b

    def soft_barrier(self):
        self.flush()
        fb = []
        self.cur_fence = fb
        self.pend.append(("fence", None, None, [], [], None, 0.0, fb))

    def _emit_fence(self, fb):
        ep = {self.sem[e]: self.cnt[e] for e in self.eng if self.cnt[e] > 0}
        for kb in self.dsems:
            ep[kb.dsem] = kb.dcnt
        for s_, v in self.epoch.items():
            if ep.get(s_, 0) < v:
                ep[s_] = v
        self.epoch = ep
        for b in fb:
            for s_, v in ep.items():
                if b.w.get(s_, 0) < v:
                    b.w[s_] = v

    def bufs(self, n, name="b"):
        return [self.buf("%s%d" % (name, i)) for i in range(n)]

    def _wait(self, e, deps):
        w = self.waited[e]
        for sem, val in deps.items():
            if w.get(sem, 0) < val:
                self.eng[e].wait_ge(sem, val)
                w[sem] = val

    def _deps(self, e, reads, writes):
        deps = {}

        def add(d):
            for s, v in d.items():
                if deps.get(s, 0) < v:
                    deps[s] = v
        for b in reads:
            add(b.w)
        for b in writes:
            add(b.w)
            add(b.r)
        if e == "pe":
            deps.pop(self.sem["pe"], None)
        return deps

    def _commit(self, tok, reads, writes):
        s, v = tok
        for b in reads:
            if b.r.get(s, 0) < v:
                b.r[s] = v
        for b in writes:
            b.w = {s: v}
            b.r = {}

    def op(self, e, fn, reads=(), writes=(), cost=None):
        if cost is None:
            cost = getattr(fn, "cost", None)
        if cost is None:
            cost = {"pe": 0.6, "act": 0.8, "dve": 0.8, "pool": 1.2, "sp": 0.2}[e]
        self.pend.append(("op", e, _snap(fn), list(reads), list(writes), None, float(cost)))

    def dma(self, e, out, in_, reads=(), writes=(), key=None):
        kb = key if key is not None else (writes[0] if writes else reads[0])
        try:
            nb = float(out.nbytes())
        except Exception:
            nb = 1e6
        self.pend.append(("dma", e, (out, in_), list(reads), list(writes), kb, 2.0 + nb / 3.0e5))

    def _emit_op(self, e, fn, reads, writes):
        self._wait(e, self._deps(e, reads, writes))
        ins = fn()
        self.cnt[e] += 1
        ins.then_inc(self.sem[e], 1)
        self._commit((self.sem[e], self.cnt[e]), reads, writes)

    def _emit_dma(self, e, out, in_, reads, writes, kb):
        cls = "sw" if e == "pool" else "hw"
        if kb.dsem is not None and kb.dcls != cls:
            raise RuntimeError("buffer %s used as DMA key from both DGE kinds" % kb.name)
        if kb.dsem is None:
            if self.pool[cls]:
                kb.dsem, kb.dcnt = self.pool[cls].pop()
            else:
                self.nsem += 1
                kb.dsem = self.es.enter_context(self.nc.semaphore("ds%s%d" % (cls, self.nsem)))
                kb.dcnt = 0
            kb.dcls = cls
            self.dsems.append(kb)
        self._wait(e, self._deps(e, reads, writes))
        kb.dcnt += 16
        self.eng[e].dma_start(out=out, in_=in_).then_inc(kb.dsem, 16)
        self._commit((kb.dsem, kb.dcnt), reads, writes)

    def flush(self):
        ops = self.pend
        self.pend = []
        n = len(ops)
        if n == 0:
            return
        lastw, readers, lastkey = {}, {}, {}
        deps = [set() for _ in range(n)]
        fence_of = {}
        prev_all = []
        for i, op in enumerate(ops):
            kind, e, fn, reads, writes, kb, cost = op[:7]
            if kind == "fence":
                deps[i].update(prev_all)
                for b in op[7]:
                    fence_of[id(b)] = i
                prev_all = [i]
                continue
            prev_all.append(i)
            for b in list(reads) + list(writes):
                if id(b) in fence_of:
                    deps[i].add(fence_of[id(b)])
            for b in reads:
                if id(b) in lastw:
                    deps[i].add(lastw[id(b)])
            for b in writes:
                if id(b) in lastw:
                    deps[i].add(lastw[id(b)])
                deps[i].update(readers.get(id(b), ()))
            if kb is not None:
                if id(kb) in lastkey:
                    deps[i].add(lastkey[id(kb)])
                lastkey[id(kb)] = i
            for b in reads:
                readers.setdefault(id(b), []).append(i)
            for b in writes:
                lastw[id(b)] = i
                readers[id(b)] = []
            deps[i].discard(i)
        if not REORDER:
            order = list(range(n))
        else:
            users = [[] for _ in range(n)]
            ndep = [len(d) for d in deps]
            for i, d in enumerate(deps):
                for j in d:
                    users[j].append(i)
            ready = [i for i in range(n) if ndep[i] == 0]
            bl = [0.0] * n
            for i in range(n - 1, -1, -1):
                m = 0.0
                for u in users[i]:
                    if bl[u] > m:
                        m = bl[u]
                bl[i] = ops[i][6] + m + (LAT0 if users[i] else 0.0)
            efree = dict(self.efree)
            fin = [0.0] * n
            order = []
            LAT = 0.4
            while ready:
                best, bt = None, None
                cand = []
                for i in ready:
                    kind, e, fn, reads, writes, kb, cost = ops[i][:7]
                    t = efree.get(e, 0.0) if e is not None else 0.0
                    for j in deps[i]:
                        tj = fin[j] + (0.0 if (ops[j][1] == e and ops[j][0] == "op") or ops[j][0] == "fence" else LAT)
                        if tj > t:
                            t = tj
                    cand.append((t, i))
                    if bt is None or t < bt:
                        bt = t
                best, bbl = None, None
                for (t, i) in cand:
                    if t <= bt + SLACK and (bbl is None or bl[i] > bbl + 1e-9 or (abs(bl[i] - bbl) <= 1e-9 and i < best)):
                        best, bbl, tb = i, bl[i], t
                bt = tb
                i = best
                ready.remove(i)
                kind, e, fn, reads, writes, kb, cost = ops[i][:7]
                if kind == "fence":
                    fin[i] = bt
                elif kind == "dma":
                    efree[e] = bt + (1.4 if e == "pool" else 0.15)
                    fin[i] = bt + cost
                else:
                    efree[e] = bt + cost
                    fin[i] = bt + cost
                order.append(i)
                for u in users[i]:
                    ndep[u] -= 1
                    if ndep[u] == 0:
                        ready.append(u)
            tmax = max(fin) if fin else 0.0
            self.efree = {e: max(0.0, efree.get(e, 0.0) - tmax) for e in self.eng}
            assert len(order) == n
        for i in order:
            kind, e, fn, reads, writes, kb, cost = ops[i][:7]
            if kind == "op":
                self._emit_op(e, fn, reads, writes)
            elif kind == "fence":
                self._emit_fence(ops[i][7])
            else:
                self._emit_dma(e, fn[0], fn[1], reads, writes, kb)
        self.cur_fence = None

    def barrier(self):
        self.flush()
        deps = {self.sem[e]: self.cnt[e] for e in self.eng if self.cnt[e] > 0}
        for kb in self.dsems:
            deps[kb.dsem] = kb.dcnt
        for e in self.eng:
            self._wait(e, dict(deps))
        for kb in self.dsems:
            self.pool[kb.dcls].append((kb.dsem, kb.dcnt))
            kb.dsem = None
        self.dsems = []
        self.epoch = {}

    def finish(self):
        self.barrier()


class Arena:
    def __init__(self, nc, nbytes):
        self.n = nbytes // 4
        self.t = nc.alloc_sbuf_tensor("arena", [128, self.n], F32)
        self.l = 0
        self.r = self.n
        self.peak = 0

    def alloc(self, shape, dt, side="l"):
        p = shape[0]
        n = int(np.prod(shape[1:]))
        words = (n * (2 if dt == BF16 else 4) + 3) // 4
        words = (words + 15) // 16 * 16
        if side == "l":
            off = self.l
            self.l += words
        else:
            self.r -= words
            off = self.r
        assert self.l <= self.r, "SBUF arena overflow l=%d r=%d" % (self.l, self.r)
        self.peak = max(self.peak, self.l + self.n - self.r)
        ap = self.t[0:p, off:off + words]
        if dt == BF16:
            ap = ap.bitcast(BF16)
        ap = ap[:, 0:n]
        if len(shape) == 3:
            ap = ap.rearrange("p (a b) -> p a b", a=shape[1])
        elif len(shape) == 4:
            ap = ap.rearrange("p (a b c) -> p a b c", a=shape[1], b=shape[2])
        return ap

    def mark(self):
        return (self.l, self.r)

    def release(self, m):
        self.l, self.r = m


def build_nc():
    nc = bass.Bass("TRN2", target_bir_lowering=False)

    def din(name, shape):
        return nc.dram_tensor(name, list(shape), F32, kind="ExternalInput").ap()

    def dout(name, shape):
        return nc.dram_tensor(name, list(shape), F32, kind="ExternalOutput").ap()

    xm = din("xm", [NTOK, D])
    xp = din("xp", [NPREV * 128, D])
    pm = din("pm", [NTOK, 256])
    st0 = din("st0", [16, 4, 128, 256])
    g_mix = din("g_mix", [D])
    w_in = din("w_in", [D, 5136])
    w_a2 = din("w_a2", [16, 512])
    b_a = din("b_a", [1, 512])
    g_gla = din("g_gla", [1024])
    g_ln = din("g_ln", [1024])
    b_ln = din("b_ln", [1024])
    w_s = din("w_s", [8, 128, 128])
    b_s = din("b_s", [1, 1024])
    g_gout = din("g_gout", [1024])
    w_out = din("w_out", [D, D])
    g_ffn = din("g_ffn", [D])
    w_f1 = din("w_f1", [D, 2 * DFF])
    w_f2 = din("w_f2", [DFF, D])
    w_ple = din("w_ple", [256, D])
    g_ple = din("g_ple", [D])
    g_pg = din("g_pg", [D])
    w_pg = din("w_pg", [D, D])
    g_fin = din("g_fin", [D])
    y = dout("y", [NTOK, D])
    spo = dout("spo", [4, 128, 256])
    sso = dout("sso", [16, 4, 128, 256])
    vso = dout("vso", [128, 1024])

    w_in_v = w_in.rearrange("(k p) n -> p k n", p=128)
    w_out_v = w_out.rearrange("(k p) n -> p k n", p=128)
    w_f1_v = w_f1.rearrange("(k p) n -> p k n", p=128)
    w_f2_v = w_f2.rearrange("(j p) n -> p j n", p=128)
    w_ple_v = w_ple.rearrange("(k p) n -> p k n", p=128)
    w_pg_v = w_pg.rearrange("(k p) n -> p k n", p=128)

    with ExitStack() as es:
        S = Sched(nc, es)
        AR = Arena(nc, 204 * 1024)
        T = AR.alloc

        pf = [es.enter_context(nc.psum_tensor("pf%d" % i, [128, 512], F32)) for i in range(6)]
        pb = [es.enter_context(nc.psum_tensor("pb%d" % i, [128, 1024], BF16)) for i in range(2)]
        Bpf = S.bufs(6, "pf")
        Bpb = S.bufs(2, "pb")
        rr = {"pf": 0, "pb": 0}
        held = set()

        def nextpf():
            while True:
                i = rr["pf"]
                rr["pf"] = (i + 1) % 6
                if Bpf[i] not in held:
                    return pf[i], Bpf[i]

        def run_streams(*gens):
            gl = []
            for g in gens:
                if g is None:
                    continue
                gl.append(list(g) if isinstance(g, tuple) else [g, 1])
            while gl:
                for item in list(gl):
                    for _ in range(item[1]):
                        try:
                            next(item[0])
                        except StopIteration:
                            gl.remove(item)
                            break

        def nextpb():
            i = rr["pb"]
            rr["pb"] = (i + 1) % 2
            return pb[i], Bpb[i]

        ident = T([128, 128], BF16)
        identf = T([128, 128], F32, "r")
        tri = T([128, 128], F32)
        blk = T([128, 128], F32)
        ET = T([128, 16], F32)
        Ebc = T([128, 16, 128], BF16)
        Ebcf = T([128, 16, 128], F32, "r")
        onescol = T([128, 1], F32)
        onesrow = T([1, 128], F32)
        Bc = S.buf("consts")

        def asel(ap, pattern, base, cm, op=ALU.is_ge, fill=0.0):
            S.op("pool", lambda: nc.gpsimd.affine_select(out=ap, in_=ap, pattern=pattern, compare_op=op, fill=fill,
                                                          base=base, channel_multiplier=cm), writes=[Bc])

        S.op("pool", lambda: nc.gpsimd.memset(identf, 0.0), writes=[Bc])
        asel(identf, [[-1, 128]], 0, 1, op=ALU.not_equal, fill=1.0)
        S.op("pool", lambda: nc.gpsimd.tensor_copy(out=ident, in_=identf), writes=[Bc])
        S.op("pool", lambda: nc.gpsimd.memset(tri, 1.0), writes=[Bc])
        asel(tri, [[1, 128]], 0, -1)
        S.op("pool", lambda: nc.gpsimd.memset(blk, 1.0), writes=[Bc])
        asel(blk, [[1, 128]], 0, -1)
        blk3 = blk.rearrange("p (i s) -> p i s", i=16)
        asel(blk3, [[-8, 16], [0, 8]], 0, 1)
        asel(blk3, [[8, 16], [0, 8]], 7, -1)
        S.op("pool", lambda: nc.gpsimd.memset(ET, 1.0), writes=[Bc])
        asel(ET, [[-8, 16]], 0, 1)
        asel(ET, [[8, 16]], 7, -1)
        S.op("pool", lambda: nc.gpsimd.memset(Ebcf, 1.0), writes=[Bc])
        asel(Ebcf, [[-8, 16], [1, 128]], 0, 0)
        asel(Ebcf, [[8, 16], [-1, 128]], 7, 0)
        S.op("pool", lambda: nc.gpsimd.tensor_copy(out=Ebc, in_=Ebcf), writes=[Bc])
        S.op("pool", lambda: nc.gpsimd.memset(onescol, 1.0), writes=[Bc])
        S.op("pool", lambda: nc.gpsimd.memset(onesrow, 1.0), writes=[Bc])
        CONST = [Bc]

        Sf = T([128, 4, 256], F32)
        Sb = T([128, 4, 256], BF16)
        BSf = S.bufs(4, "Sf")
        BSb = S.bufs(4, "Sb")
        S.op("dve", lambda: nc.vector.memset(Sf, 0.0), writes=BSf)
        S.op("dve", lambda: nc.vector.memset(Sb, 0.0), writes=BSb)

        def load_gain(gt, Bg, vec, n):
            S.dma("sp", gt[:, 0:n], vec.partition_broadcast(128), writes=[Bg])

        class NormWS:
            def __init__(self, tag, side="l"):
                self.xn = [T([128, D], BF16, side) for i in range(2)]
                self.ss = [T([128, 2], F32, side) for i in range(2)]
                self.Bxn = S.bufs(2, "xn" + tag)
                self.Bss = S.bufs(2, "nss" + tag)
                self.i = 0

        def norm_p1(src, Bsrc, gbc, Bg, W):
            i = W.i
            W.i ^= 1
            xn, Bxn, ss, Bss = W.xn[i], W.Bxn[i], W.ss[i], W.Bss[i]
            S.op("act", lambda: nc.scalar.activation(out=xn, in_=src, func=AF.Square, accum_out=ss[:, 0:1]),
                 reads=Bsrc, writes=[Bxn, Bss], cost=1.9)
            S.op("act", lambda: nc.scalar.activation(out=ss[:, 1:2], in_=ss[:, 0:1], func=AF.Sqrt, scale=1.0 / D, bias=EPS),
                 reads=[Bss], writes=[Bss])
            S.op("dve", lambda: nc.vector.reciprocal(out=ss[:, 1:2], in_=ss[:, 1:2]), reads=[Bss], writes=[Bss])
            S.op("dve", lambda: nc.vector.scalar_tensor_tensor(out=xn, in0=src, scalar=ss[:, 1:2], in1=gbc,
                                                              op0=ALU.mult, op1=ALU.mult),
                 reads=list(Bsrc) + [Bss, Bg], writes=[Bxn], cost=2.4)
            return i

        def norm_p2(i, dstT, c0, Bdst, W):
            xn, Bxn = W.xn[i], W.Bxn[i]
            for half in range(2):
                bank, Bb = nextpb()

                def tr(bank=bank, half=half):
                    for j in range(8):
                        k = half * 8 + j
                        ins = nc.tensor.transpose(out=bank[:, j * 128:(j + 1) * 128], in_=xn[:, k * 128:(k + 1) * 128],
                                                  identity=ident)
                    return ins
                S.op("pe", tr, reads=[Bxn, Bc], writes=[Bb])
                dst = dstT[:, half * 8:(half + 1) * 8, c0:c0 + 128]
                src_ps = bank[:].rearrange("p (k t) -> p k t", k=8)
                if half == 0:
                    S.op("act", lambda: nc.scalar.copy(out=dst, in_=src_ps), reads=[Bb], writes=[Bdst[half]])
                else:
                    S.op("dve", lambda: nc.vector.tensor_copy(out=dst, in_=src_ps), reads=[Bb], writes=[Bdst[half]])

        def norm_T_pipe(items, gbc, Bg, dstT, W):
            prev = None
            for (src, Bsrc, c0, Bdst, pre) in items:
                if pre is not None:
                    pre()
                i = norm_p1(src, Bsrc, gbc, Bg, W)
                if prev is not None:
                    norm_p2(prev[0], dstT, prev[1], prev[2], W)
                prev = (i, c0, Bdst)
            norm_p2(prev[0], dstT, prev[1], prev[2], W)

        def transpose_cols(src_tok, Bsrc, nchunk, dst, Bdst, eng):
            bank, Bb = nextpb()

            def tr():
                for j in range(nchunk):
                    ins = nc.tensor.transpose(out=bank[:, j * 128:(j + 1) * 128], in_=src_tok[:, j * 128:(j + 1) * 128],
                                              identity=ident)
                return ins
            S.op("pe", tr, reads=list(Bsrc) + [Bc], writes=[Bb])
            src_ps = bank[:, 0:nchunk * 128].rearrange("p (k t) -> p k t", k=nchunk)
            if eng == "act":
                S.op("act", lambda: nc.scalar.copy(out=dst, in_=src_ps), reads=[Bb], writes=Bdst)
            else:
                S.op("dve", lambda: nc.vector.tensor_copy(out=dst, in_=src_ps), reads=[Bb], writes=Bdst)

        class WStream:
            def __init__(self, tag, shape, n=2, side="l"):
                self.slots = [T(shape, BF16, side) for i in range(n)]
                self.B = S.bufs(n, "w" + tag)
                self.n = n
                self.i = 0

            def load(self, fn):
                i = self.i
                self.i = (i + 1) % self.n
                fn(self.slots[i], self.B[i])
                return self.slots[i], self.B[i]

        def dense16(lhsT_fn, rhs_fn, out_ap, nk=16):
            def f():
                for k in range(nk):
                    ins = nc.tensor.matmul(out_ap, lhsT=lhsT_fn(k), rhs=rhs_fn(k), start=(k == 0), stop=(k == nk - 1))
                return ins
            f.cost = nk * 0.215
            return f

        def gelu_evac(bank, Bb, out_ap, Bout, tmp, Btmp, accum=None, Baccum=None, pool_affine=False):
            if accum is None:
                S.op("act", lambda: nc.scalar.activation(out=out_ap, in_=bank, func=AF.Gelu_apprx_tanh),
                     reads=[Bb], writes=Bout, cost=0.7)
            else:
                S.op("act", lambda: nc.scalar.activation(out=out_ap, in_=bank, func=AF.Gelu_apprx_tanh, accum_out=accum),
                     reads=[Bb], writes=list(Bout) + [Baccum], cost=0.7)

        S.barrier()
        AR.r = AR.n
        catT = T([128, 16, NTOK], BF16, "r")
        BcatTo = S.bufs(NT, "catTo")
        BcatTm = S.bufs(NT, "catTm")
        def mixer_group(xsrc, gt0, ntile, kinds):
            m_grp = AR.mark()
            ntok = ntile * 128
            main = kinds[0] != "prev"
            has_sample = "sample" in kinds
            aT = T([128, 16, ntok], BF16)
            BaT = [S.bufs(2, "aT%d_" % t) for t in range(ntile)]
            ws = WStream("in", [128, 16, 512])

            def loader(col0):
                def f(slot, Bs):
                    S.dma("pool", slot, w_in_v[:, :, col0:col0 + 512], writes=[Bs])
                return f
            pre = [ws.load(loader(C_V)), ws.load(loader(C_V + 512))]
            if main:
                gelu_u = T([128, ntile, 1024], BF16)
                Bgu = [S.bufs(2, "gelu_u%d_" % t) for t in range(ntile)]
                gtmp = [T([128, 512], F32) for i in range(2)]
                Bgtmp = S.bufs(2, "gtmp")
            m_a = AR.mark()
            xs = [T([128, D], F32) for i in range(2)]
            Bxs = S.bufs(2, "xs")
            gt = T([128, D], F32)
            Bg = S.buf("gmixbc")
            load_gain(gt, Bg, g_mix, D)
            W = NormWS("a")

            def mk_pre(t):
                return lambda: S.dma("sp", xs[t % 2], xsrc[(gt0 + t) * 128:(gt0 + t + 1) * 128, :], writes=[Bxs[t % 2]])
            norm_T_pipe([(xs[t % 2], [Bxs[t % 2]], t * 128, BaT[t], mk_pre(t)) for t in range(ntile)], gt, Bg, aT, W)
            if main:
                S.soft_barrier()
                AR.release(m_a)
            m_gla = AR.mark()
            walow = T([128, 16, 16], BF16)
            Bwalow = S.buf("walow")
            wa2 = T([16, 512], F32)
            barow = T([1, 512], F32)
            Bwa = S.buf("wa2")
            Bba = S.buf("barow")
            S.dma("sp", wa2, w_a2, writes=[Bwa])
            S.dma("sp", barow, b_a, writes=[Bba])
            alowT = T([16, ntok], F32)
            BalowT = S.bufs((ntok + 511) // 512, "alowT")
            ek = T([128, ntile, 512], BF16)
            Bek = S.bufs(ntile, "ek")
            dec = T([128, ntile, 64], F32)
            Bdec = S.bufs(ntile, "dec")
            ktok = T([128, ntile, 512], BF16)
            Bktok = S.bufs(ntile, "ktok")
            vtok = T([128, ntile, 1024], BF16)
            Bvtok = [S.bufs(2, "vtok%d_" % t) for t in range(ntile)]
            lp = [T([128, 512], F32) for i in range(2)]
            Blp = S.bufs(2, "lp")
            Tt = T([128, 4, 256], F32)
            BTt = S.bufs(2, "Tt")
            if main:
                eq = T([128, ntile, 512], BF16)
                Beq = S.bufs(ntile, "eq")
                qT = T([128, 4, ntok], BF16)
                BqT = S.bufs(ntile, "qT")
                kT = T([128, 4, ntok], BF16)
                BkT = S.bufs(ntile, "kT")
                silur = T([128, ntile, 1024], BF16)
                Bsilur = [S.bufs(2, "silur%d_" % t) for t in range(ntile)]
                qtmp = [T([128, 512], BF16) for i in range(2)]
                Bqtmp = S.bufs(2, "qtmp")
                stmp = [T([128, 512], F32) for i in range(2)]
                Bstmp = S.bufs(2, "stmp")
                ggla = T([128, 1024], F32)
                Bggla = S.buf("gglabc")
                load_gain(ggla, Bggla, g_gla, 1024)
                attb = [T([128, 4, 128], BF16) for i in range(2)]
                Battb = S.bufs(2, "attb")
                on = [T([128, 1024], BF16) for i in range(2)]
                Bon = S.bufs(2, "on")
                ojunk = T([128, 256], BF16)
                Bojunk = S.buf("ojunk")
                oss = [T([128, 8], F32) for i in range(2)]
                Boss = S.bufs(2, "oss")
                if has_sample:
                    s0f = [T([128, 4, 256], F32) for i in range(2)]
                    Bs0f = S.bufs(2, "s0f")
                    s0b = [T([128, 4, 256], BF16) for i in range(2)]
                    Bs0b = S.bufs(2, "s0b")
                    qTm = [T([128, 4, 128], BF16) for i in range(2)]
                    BqTm = S.bufs(2, "qTm")
                    ktm = [T([128, 512], BF16) for i in range(2)]
                    Bktm = S.bufs(2, "ktm")
            flags = {"decay": False, "gla_in": False}
            kdone = [False] * ntile

            S.dma("pool", walow, w_in_v[:, :, C_A:C_A + 16], writes=[Bwalow])
            for gi, g0 in enumerate(range(0, ntok, 512)):
                n = min(512, ntok - g0)
                bank, Bb = nextpf()
                tl = list(range(g0 // 128, (g0 + n) // 128))
                S.op("pe", dense16(lambda k: walow[:, k, :], lambda k: aT[:, k, g0:g0 + n], bank[0:16, 0:n]),
                     reads=[Bwalow] + [b for t in tl for b in BaT[t]], writes=[Bb])
                S.op("act", lambda: nc.scalar.copy(out=alowT[:, g0:g0 + n], in_=bank[0:16, 0:n]), reads=[Bb], writes=[BalowT[gi]])

            def decay_gen():
                for t in range(ntile):
                    c0 = t * 128
                    sample = kinds[t] == "sample"
                    M = blk if sample else tri
                    nseg = 16 if sample else 1
                    seg = ET if sample else onescol
                    bank, Bb = nextpf()

                    def mm_la(bank=bank, c0=c0):
                        nc.tensor.matmul(bank[:], lhsT=alowT[:, c0:c0 + 128], rhs=wa2, start=True, stop=False)
                        return nc.tensor.matmul(bank[:], lhsT=onesrow, rhs=barow, start=False, stop=True)
                    S.op("pe", mm_la, reads=[BalowT[c0 // 512], Bwa, Bba] + CONST, writes=[Bb])
                    l, Bl = lp[t % 2], Blp[t % 2]
                    S.op("act", lambda: nc.scalar.activation(out=l, in_=bank[:], func=AF.Exp, scale=-1.0), reads=[Bb], writes=[Bl])
                    S.op("act", lambda: nc.scalar.activation(out=l, in_=l, func=AF.Ln, bias=1.0), reads=[Bl], writes=[Bl])
                    yield
                    bank2, Bb2 = nextpf()
                    S.op("pe", lambda: nc.tensor.matmul(bank2[:], lhsT=M, rhs=l, start=True, stop=True),
                         reads=[Bl] + CONST, writes=[Bb2])
                    bank3, Bb3 = nextpf()

                    def mm_cl(bank3=bank3, l=l, seg=seg, nseg=nseg):
                        for h in range(4):
                            ins = nc.tensor.matmul(bank3[:, h * nseg:(h + 1) * nseg], lhsT=l[:, h * 128:(h + 1) * 128],
                                                   rhs=seg, start=True, stop=True)
                        return ins
                    S.op("pe", mm_cl, reads=[Bl] + CONST, writes=[Bb3])
                    if main:
                        S.op("act", lambda: nc.scalar.activation(out=eq[:, t, :], in_=bank2[:], func=AF.Exp, scale=-1.0 / 16),
                             reads=[Bb2], writes=[Beq[t]])
                    S.op("act", lambda: nc.scalar.activation(out=ek[:, t, :], in_=bank2[:], func=AF.Exp, scale=1.0 / 16),
                         reads=[Bb2], writes=[Bek[t]])
                    S.op("act", lambda: nc.scalar.activation(out=dec[:, t, 0:4 * nseg], in_=bank3[:, 0:4 * nseg], func=AF.Exp,
                                                             scale=-1.0 / 16), reads=[Bb3], writes=[Bdec[t]])
                    yield
                flags["decay"] = True

            def ev_k(t, bank, Bb):
                S.op("dve", lambda: nc.vector.tensor_tensor(out=ktok[:, t, :], in0=bank[:], in1=ek[:, t, :], op=ALU.mult),
                     reads=[Bb, Bek[t]], writes=[Bktok[t]])
                if main:
                    transpose_cols(ktok[:, t, :], [Bktok[t]], 4, kT[:, :, t * 128:(t + 1) * 128], [BkT[t]], "act")
                kdone[t] = True

            def ev_q(t, bank, Bb):
                i = t % 2
                S.op("dve", lambda: nc.vector.scalar_tensor_tensor(out=qtmp[i], in0=bank[:], scalar=128.0 ** -0.5,
                                                                  in1=eq[:, t, :], op0=ALU.mult, op1=ALU.mult),
                     reads=[Bb, Beq[t]], writes=[Bqtmp[i]])
                transpose_cols(qtmp[i], [Bqtmp[i]], 4, qT[:, :, t * 128:(t + 1) * 128], [BqT[t]], "act")

            def ev_v(half):
                def f(t, bank, Bb):
                    S.op("act", lambda: nc.scalar.copy(out=vtok[:, t, half * 512:(half + 1) * 512], in_=bank[:]),
                         reads=[Bb], writes=[Bvtok[t][half]])
                return f

            def ev_r(half):
                def f(t, bank, Bb):
                    i = t % 2
                    S.op("act", lambda: nc.scalar.activation(out=stmp[i], in_=bank[:], func=AF.Silu),
                         reads=[Bb], writes=[Bstmp[i]])
                    S.op("dve", lambda: nc.vector.tensor_tensor(out=silur[:, t, half * 512:(half + 1) * 512], in0=stmp[i],
                                                               in1=ggla[:, half * 512:(half + 1) * 512], op=ALU.mult),
                         reads=[Bstmp[i], Bggla], writes=[Bsilur[t][half]])
                return f
            ucnt = [0]

            def ev_u(half):
                def f(t, bank, Bb):
                    i = ucnt[0] % 2
                    ucnt[0] += 1
                    gelu_evac(bank[:], Bb, gelu_u[:, t, half * 512:(half + 1) * 512], [Bgu[t][half]], gtmp[i], Bgtmp[i])
                return f

            specs = [(C_V, ev_v(0), False), (C_V + 512, ev_v(1), False), (C_K, ev_k, True)]
            if main:
                specs += [(C_Q, ev_q, True), (C_R, ev_r(0), False), (C_R + 512, ev_r(1), False),
                          (C_U, ev_u(0), False), (C_U + 512, ev_u(1), False)]
            n_gla_in = 3 if not main else 6
            vgslots = []

            def dense_gen():
                q = list(pre)
                nloaded = len(q)
                for bi, (col0, evac, need) in enumerate(specs):
                    if bi == n_gla_in:
                        flags["gla_in"] = True
                    if need:
                        while not flags["decay"]:
                            yield
                    slot, Bs = q.pop(0)
                    if not q and nloaded < len(specs):
                        q.append(ws.load(loader(specs[nloaded][0])))
                        nloaded += 1
                    elif not q and main and not vgslots:
                        vgslots.append(ws.load(loader(C_VG)))
                    for t in range(ntile):
                        bank, Bb = nextpf()
                        S.op("pe", dense16(lambda k: aT[:, k, t * 128:(t + 1) * 128], lambda k: slot[:, k, :], bank[:]),
                             reads=[Bs] + BaT[t], writes=[Bb])
                        evac(t, bank, Bb)
                        yield
                if main:
                    vgslots.append(ws.load(loader(C_VG + 512)))
                flags["gla_in"] = True

            def state_update(t):
                banks = [nextpf(), nextpf()]

                def mm():
                    for h in range(4):
                        bk = banks[h // 2][0]
                        ins = nc.tensor.matmul(bk[:, (h % 2) * 256:(h % 2 + 1) * 256], lhsT=ktok[:, t, h * 128:(h + 1) * 128],
                                               rhs=vtok[:, t, h * 256:(h + 1) * 256], start=True, stop=True)
                    return ins
                S.op("pe", mm, reads=[Bktok[t]] + Bvtok[t], writes=[banks[0][1], banks[1][1]])
                for hh in range(2):
                    bk, Bbk = banks[hh]
                    S.op("dve", lambda: nc.vector.tensor_tensor(out=Tt[:, 2 * hh:2 * hh + 2, :],
                                                               in0=bk[:].rearrange("p (a v) -> p a v", a=2),
                                                               in1=Sf[:, 2 * hh:2 * hh + 2, :], op=ALU.add),
                         reads=[Bbk, BSf[2 * hh], BSf[2 * hh + 1]], writes=[BTt[hh]])
                decb = dec[:, t, 0:4].unsqueeze(2).to_broadcast([128, 4, 256])
                S.op("dve", lambda: nc.vector.tensor_tensor(out=Sb, in0=Tt, in1=decb, op=ALU.mult),
                     reads=BTt + [Bdec[t]], writes=BSb)
                S.op("pool", lambda: nc.gpsimd.tensor_tensor(out=Sf, in0=Tt, in1=decb, op=ALU.mult),
                     reads=BTt + [Bdec[t]], writes=BSf, cost=2.2)

            def gla_gen():
                while main and not flags["gla_in"]:
                    yield
                for t in range(ntile):
                    if kinds[t] == "prev":
                        while not kdone[t]:
                            yield
                        state_update(t)
                        yield
                        continue
                    c0 = t * 128
                    gc0 = (gt0 + t) * 128
                    sample = kinds[t] == "sample"
                    M = blk if sample else tri
                    nob = 4 if sample else 2
                    ob = [nextpf() for _ in range(nob)]
                    Bob = [x[1] for x in ob]
                    for b_ in Bob:
                        held.add(b_)

                    def oreg(h, ob=ob, nob=nob):
                        if nob == 4:
                            return ob[h][0][:, 0:256]
                        return ob[h // 2][0][:, (h % 2) * 256:(h % 2 + 1) * 256]

                    def obuf(h, Bob=Bob, nob=nob):
                        return Bob[h] if nob == 4 else Bob[h // 2]
                    ab, Bab = nextpf()

                    def mm_att(ab=ab, c0=c0):
                        for h in range(4):
                            ins = nc.tensor.matmul(ab[:, h * 128:(h + 1) * 128], lhsT=kT[:, h, c0:c0 + 128],
                                                   rhs=qT[:, h, c0:c0 + 128], start=True, stop=True)
                        return ins
                    S.op("pe", mm_att, reads=[BkT[t], BqT[t]], writes=[Bab])
                    ai = t % 2
                    S.op("dve", lambda: nc.vector.tensor_tensor(out=attb[ai], in0=ab[:].rearrange("p (h t) -> p h t", h=4),
                                                               in1=M.unsqueeze(1).to_broadcast([128, 4, 128]), op=ALU.mult),
                         reads=[Bab] + CONST, writes=[Battb[ai]])
                    if not sample:
                        yield
                    else:
                        def ld_state(i):
                            sl = i % 2
                            S.dma("sp", s0f[sl], st0[i].rearrange("h k v -> k h v"), writes=[Bs0f[sl]])
                            S.dma("pool", s0b[sl], st0[i].rearrange("h k v -> k h v"), writes=[Bs0b[sl]])
                        for i in range(16):
                            sl = i % 2
                            ld_state(i)
                            S.op("dve", lambda: nc.vector.tensor_tensor(out=qTm[sl], in0=qT[:, :, c0:c0 + 128],
                                                                       in1=Ebc[:, i, :].unsqueeze(1).to_broadcast([128, 4, 128]),
                                                                       op=ALU.mult),
                                 reads=[BqT[t], Bc], writes=[BqTm[sl]])
                            S.op("act", lambda: nc.scalar.activation(out=ktm[sl], in_=ktok[:, t, :], func=AF.Copy,
                                                                     scale=ET[:, i:i + 1]),
                                 reads=[Bktok[t], Bc], writes=[Bktm[sl]])

                            def mm_inter_s(i=i, sl=sl, oreg=oreg):
                                for h in range(4):
                                    ins = nc.tensor.matmul(oreg(h), lhsT=qTm[sl][:, h, :], rhs=s0b[sl][:, h, :],
                                                           start=(i == 0), stop=False)
                                return ins
                            S.op("pe", mm_inter_s, reads=[BqTm[sl], Bs0b[sl]], writes=Bob)
                            sb2 = [nextpf(), nextpf()]

                            def mm_s(sl=sl, sb2=sb2):
                                for h in range(4):
                                    ins = nc.tensor.matmul(sb2[h // 2][0][:, (h % 2) * 256:(h % 2 + 1) * 256],
                                                           lhsT=ktm[sl][:, h * 128:(h + 1) * 128],
                                                           rhs=vtok[:, t, h * 256:(h + 1) * 256], start=True, stop=True)
                                return ins
                            S.op("pe", mm_s, reads=[Bktm[sl]] + Bvtok[t], writes=[sb2[0][1], sb2[1][1]])
                            for hh in range(2):
                                S.op("dve", lambda: nc.vector.tensor_tensor(
                                    out=s0f[sl][:, 2 * hh:2 * hh + 2, :], in0=sb2[hh][0][:].rearrange("p (a v) -> p a v", a=2),
                                    in1=s0f[sl][:, 2 * hh:2 * hh + 2, :], op=ALU.add),
                                    reads=[sb2[hh][1], Bs0f[sl]], writes=[Bs0f[sl]])
                            S.op("dve", lambda: nc.vector.tensor_tensor(
                                out=s0f[sl], in0=s0f[sl],
                                in1=dec[:, t, :].rearrange("p (h i) -> p h i", h=4)[:, :, i:i + 1].to_broadcast([128, 4, 256]),
                                op=ALU.mult), reads=[Bs0f[sl], Bdec[t]], writes=[Bs0f[sl]])
                            S.dma("sp", sso[i].rearrange("h k v -> k h v"), s0f[sl], reads=[Bs0f[sl]], key=Bs0f[sl])
                            yield

                    def mm_o(ai=ai, oreg=oreg, sample=sample, c0=c0):
                        for h in range(4):
                            if not sample:
                                nc.tensor.matmul(oreg(h), lhsT=qT[:, h, c0:c0 + 128], rhs=Sb[:, h, :], start=True, stop=False)
                            ins = nc.tensor.matmul(oreg(h), lhsT=attb[ai][:, h, :], rhs=vtok[:, t, h * 256:(h + 1) * 256],
                                                   start=False, stop=True)
                        return ins
                    S.op("pe", mm_o, reads=[Battb[ai], BqT[t]] + BSb + Bvtok[t], writes=Bob, cost=1.2)
                    if not sample:
                        state_update(t)
                    yield
                    os_, Bos = oss[ai], Boss[ai]
                    for h in range(4):
                        S.op("act", lambda: nc.scalar.activation(out=ojunk, in_=oreg(h), func=AF.Square, accum_out=os_[:, h:h + 1]),
                             reads=[obuf(h)], writes=[Bojunk, Bos])
                    S.op("act", lambda: nc.scalar.activation(out=os_[:, 4:8], in_=os_[:, 0:4], func=AF.Sqrt, scale=1.0 / 256, bias=EPS),
                         reads=[Bos], writes=[Bos])
                    S.op("dve", lambda: nc.vector.reciprocal(out=os_[:, 4:8], in_=os_[:, 4:8]), reads=[Bos], writes=[Bos])
                    yield
                    for h in range(4):
                        S.op("dve", lambda: nc.vector.scalar_tensor_tensor(
                            out=on[ai][:, h * 256:(h + 1) * 256], in0=oreg(h),
                            scalar=os_[:, 4 + h:5 + h], in1=silur[:, t, h * 256:(h + 1) * 256], op0=ALU.mult, op1=ALU.mult),
                            reads=[obuf(h), Bos] + Bsilur[t], writes=[Bon[ai]])
                    for b_ in Bob:
                        held.discard(b_)
                    transpose_cols(on[ai], [Bon[ai]], 8, catT[:, 0:8, gc0:gc0 + 128], [BcatTo[gt0 + t]], "act")
                    yield

            run_streams(decay_gen(), dense_gen(), gla_gen())
            S.soft_barrier()
            AR.release(m_gla)
            if main:
                gmlp_group(aT, BaT, vgslots, gt0, ntile, kinds, gelu_u, Bgu, gtmp, Bgtmp)
            S.barrier()
            AR.release(m_grp)

        def gmlp_group(aT, BaT, vslots, gt0, ntile, kinds, gelu_u, Bgu, gtmp, Bgtmp):
            has_sample = "sample" in kinds
            wsb = T([128, 8, 128], BF16)
            Bwsb = S.buf("wsb")
            S.dma("pool", wsb, w_s.rearrange("g t s -> t g s"), writes=[Bwsb])
            WT = T([128, 8, 128], BF16)
            BWT = S.buf("WT")
            bank, Bb = nextpb()

            def trw():
                for g in range(8):
                    ins = nc.tensor.transpose(out=bank[:, g * 128:(g + 1) * 128], in_=wsb[:, g, :], identity=ident)
                return ins
            S.op("pe", trw, reads=[Bwsb, Bc], writes=[Bb])
            S.op("dve", lambda: nc.vector.tensor_tensor(out=WT, in0=bank[:].rearrange("p (g t) -> p g t", g=8),
                                                       in1=tri.unsqueeze(1).to_broadcast([128, 8, 128]), op=ALU.mult),
                 reads=[Bb, Bc], writes=[BWT])
            bs2 = T([2, 1024], F32)
            bhi = T([2, 1024], BF16)
            blo = T([2, 1024], F32)
            bsrow = T([2, 1024], BF16)
            ones2 = T([2, 128], BF16)
            Bbs = S.buf("bsrow")
            S.dma("sp", bs2, b_s[0].partition_broadcast(2), writes=[Bbs])
            S.op("dve", lambda: nc.vector.memset(ones2, 1.0), writes=[Bbs])
            S.op("dve", lambda: nc.vector.tensor_copy(out=bhi, in_=bs2), reads=[Bbs], writes=[Bbs])
            S.op("dve", lambda: nc.vector.tensor_tensor(out=blo, in0=bs2, in1=bhi, op=ALU.subtract), reads=[Bbs], writes=[Bbs])
            S.op("dve", lambda: nc.vector.tensor_scalar(out=bs2, in0=bhi, scalar1=ident[0:2, 0:1], scalar2=None, op0=ALU.mult),
                 reads=[Bbs, Bc], writes=[Bbs])
            S.op("dve", lambda: nc.vector.scalar_tensor_tensor(out=bsrow, in0=blo, scalar=ident[0:2, 1:2], in1=bs2,
                                                              op0=ALU.mult, op1=ALU.add), reads=[Bbs, Bc], writes=[Bbs])
            if has_sample:
                Ball = T([8, 8, 16, 8], BF16)
                Arep = T([8, 16, 8], BF16)
                BBall = S.buf("Ball")
                S.op("dve", lambda: nc.vector.tensor_copy(out=Ball, in_=WT[0:8, :, 0:8].unsqueeze(2).to_broadcast([8, 8, 16, 8])),
                     reads=[BWT], writes=[BBall])
                S.op("dve", lambda: nc.vector.tensor_copy(out=Arep, in_=ident[0:8, 0:8].unsqueeze(1).to_broadcast([8, 16, 8])),
                     reads=[Bc], writes=[BBall])
                Wblk = T([128, 8, 128], BF16)
                BWblk = S.bufs(2, "Wblk")
                for half in range(2):
                    bk, Bbk = nextpf()

                    def mmw(bk=bk, half=half):
                        for g4 in range(4):
                            g = half * 4 + g4
                            ins = nc.tensor.matmul(bk[:, g4 * 128:(g4 + 1) * 128], lhsT=Arep.rearrange("p i s -> p (i s)"),
                                                   rhs=Ball[:, g, :, :].rearrange("p j t -> p (j t)"), start=True, stop=True)
                        return ins
                    S.op("pe", mmw, reads=[BBall], writes=[Bbk])
                    S.op("dve", lambda: nc.vector.tensor_tensor(out=Wblk[:, half * 4:(half + 1) * 4, :],
                                                               in0=bk[:].rearrange("p (g t) -> p g t", g=4),
                                                               in1=blk.unsqueeze(1).to_broadcast([128, 4, 128]), op=ALU.mult),
                         reads=[Bbk, Bc], writes=[BWblk[half]])
                bsrow_s = T([2, 1024], BF16)
                Bbss = S.buf("bsrow_s")
                S.op("dve", lambda: nc.vector.tensor_copy(
                    out=bsrow_s.rearrange("p (g j t) -> p g j t", g=8, j=16),
                    in_=bsrow.rearrange("p (g t) -> p g t", g=8)[:, :, 0:8].unsqueeze(2).to_broadcast([2, 8, 16, 8])),
                    reads=[Bbs], writes=[Bbss])
            vn = T([128, ntile, 1024], BF16)
            Bvn = S.bufs(ntile, "vn")
            gv = [T([128, 1024], F32) for i in range(2)]
            Bgv = [S.bufs(2, "gv%d_" % i) for i in range(2)]
            glnbc = T([128, 1024], F32)
            blnbc = T([128, 1024], F32)
            goutbc = T([128, 1024], F32)
            Bgl = S.bufs(3, "lngain")
            load_gain(glnbc, Bgl[0], g_ln, 1024)
            load_gain(blnbc, Bgl[1], b_ln, 1024)
            load_gain(goutbc, Bgl[2], g_gout, 1024)
            st = [T([128, 8], F32) for i in range(2)]
            Bst = S.bufs(2, "lnst")
            Bsth = [S.bufs(2, "lnsth%d_" % i) for i in range(2)]
            st2 = [T([128, 2], F32) for i in range(2)]
            Bst2 = S.bufs(2, "mst")
            junk = T([128, 1024], BF16)
            Bjunk = S.buf("junk")
            mr = [T([128, 1024], F32) for i in range(2)]
            Bmr = [S.bufs(2, "mr%d_" % i) for i in range(2)]
            mn = [T([128, 1024], BF16) for i in range(2)]
            Bmn = S.bufs(2, "mn")
            vout = T([128, 1024], F32)
            Bvout = S.buf("vout")
            cnt = [0]

            def stage_a(t):
                gi = t % 2
                for half in range(2):
                    slot, Bs = vslots[half]
                    bk, Bbk = nextpf()
                    S.op("pe", dense16(lambda k: aT[:, k, t * 128:(t + 1) * 128], lambda k: slot[:, k, :], bk[:]),
                         reads=[Bs] + BaT[t], writes=[Bbk])
                    i = cnt[0] % 2
                    cnt[0] += 1
                    gelu_evac(bk[:], Bbk, gv[gi][:, half * 512:(half + 1) * 512], [Bgv[gi][half]], gtmp[i], Bgtmp[i],
                              accum=st[gi][:, half:half + 1], Baccum=Bsth[gi][half], pool_affine=True)
                    yield

            def stage_b(t):
                sample = kinds[t] == "sample"
                gi = t % 2
                s_, Bs_ = st[gi], Bst[gi]
                g_ = gv[gi]
                Bg_ = Bgv[gi]
                S.op("act", lambda: nc.scalar.activation(out=junk, in_=g_, func=AF.Square, scale=1.0 / 32, accum_out=s_[:, 2:3]),
                     reads=Bg_, writes=[Bjunk, Bs_])
                S.op("dve", lambda: nc.vector.tensor_scalar(out=s_[:, 3:4], in0=s_[:, 0:1], scalar1=s_[:, 1:2], scalar2=1.0 / 1024,
                                                           op0=ALU.add, op1=ALU.mult), reads=Bsth[gi], writes=[Bs_])
                yield
                S.op("dve", lambda: nc.vector.tensor_scalar(out=s_[:, 4:5], in0=s_[:, 3:4], scalar1=s_[:, 3:4], scalar2=-1.0,
                                                           op0=ALU.mult, op1=ALU.mult), reads=[Bs_], writes=[Bs_])
                S.op("dve", lambda: nc.vector.tensor_tensor(out=s_[:, 5:6], in0=s_[:, 4:5], in1=s_[:, 2:3], op=ALU.add),
                     reads=[Bs_], writes=[Bs_])
                S.op("act", lambda: nc.scalar.activation(out=s_[:, 6:7], in_=s_[:, 5:6], func=AF.Sqrt, scale=1.0, bias=EPS),
                     reads=[Bs_], writes=[Bs_])
                S.op("dve", lambda: nc.vector.reciprocal(out=s_[:, 6:7], in_=s_[:, 6:7]), reads=[Bs_], writes=[Bs_])
                yield
                S.op("dve", lambda: nc.vector.tensor_scalar(out=g_, in0=g_, scalar1=s_[:, 3:4], scalar2=s_[:, 6:7],
                                                           op0=ALU.subtract, op1=ALU.mult), reads=Bg_ + [Bs_], writes=Bg_)
                yield
                S.op("dve", lambda: nc.vector.tensor_tensor(out=g_, in0=g_, in1=glnbc, op=ALU.mult), reads=Bg_ + [Bgl[0]], writes=Bg_)
                yield
                if sample:
                    S.op("dve", lambda: nc.vector.tensor_tensor(out=vout, in0=g_, in1=blnbc, op=ALU.add),
                         reads=Bg_ + [Bgl[1]], writes=[Bvout])
                    S.dma("sp", vso, vout, reads=[Bvout], key=Bvout)
                    S.op("act", lambda: nc.scalar.copy(out=vn[:, t, :], in_=vout), reads=[Bvout], writes=[Bvn[t]])
                else:
                    S.op("dve", lambda: nc.vector.tensor_tensor(out=vn[:, t, :], in0=g_, in1=blnbc, op=ALU.add),
                         reads=Bg_ + [Bgl[1]], writes=[Bvn[t]])
                yield

            def stage_c(t):
                sample = kinds[t] == "sample"
                gc0 = (gt0 + t) * 128
                Wm = Wblk if sample else WT
                BWm = BWblk if sample else [BWT]
                br = bsrow_s if sample else bsrow
                Bbr = Bbss if sample else Bbs
                mb = [nextpf(), nextpf()]
                mi = t % 2

                def mmx(mb=mb, Wm=Wm, br=br, t=t):
                    for g in range(8):
                        reg = mb[g // 4][0][:, (g % 4) * 128:(g % 4 + 1) * 128]
                        nc.tensor.matmul(reg, lhsT=Wm[:, g, :], rhs=vn[:, t, g * 128:(g + 1) * 128], start=True, stop=False)
                        ins = nc.tensor.matmul(reg, lhsT=br[:, g * 128:(g + 1) * 128], rhs=ones2, start=False, stop=True)
                    return ins
                S.op("pe", mmx, reads=BWm + [Bvn[t], Bbr, Bbs], writes=[mb[0][1], mb[1][1]])
                for half in range(2):
                    S.op("dve", lambda: nc.vector.tensor_tensor(out=mr[mi][:, half * 512:(half + 1) * 512], in0=mb[half][0][:],
                                                               in1=gelu_u[:, t, half * 512:(half + 1) * 512], op=ALU.mult),
                         reads=[mb[half][1], Bgu[t][half]], writes=[Bmr[mi][half]])
                yield
                s_, Bs_ = st2[mi], Bst2[mi]
                S.op("act", lambda: nc.scalar.activation(out=junk, in_=mr[mi], func=AF.Square, scale=1.0 / 32, accum_out=s_[:, 0:1]),
                     reads=Bmr[mi], writes=[Bjunk, Bs_])
                S.op("act", lambda: nc.scalar.activation(out=s_[:, 1:2], in_=s_[:, 0:1], func=AF.Sqrt, scale=1.0, bias=EPS),
                     reads=[Bs_], writes=[Bs_])
                S.op("dve", lambda: nc.vector.reciprocal(out=s_[:, 1:2], in_=s_[:, 1:2]), reads=[Bs_], writes=[Bs_])
                S.op("dve", lambda: nc.vector.scalar_tensor_tensor(out=mn[mi], in0=mr[mi], scalar=s_[:, 1:2], in1=goutbc,
                                                                  op0=ALU.mult, op1=ALU.mult),
                     reads=Bmr[mi] + [Bs_, Bgl[2]], writes=[Bmn[mi]])
                yield
                transpose_cols(mn[mi], [Bmn[mi]], 8, catT[:, 8:16, gc0:gc0 + 128], [BcatTm[gt0 + t]], "act")
                yield

            def G_(fn, t):
                return fn(t) if 0 <= t < ntile else None
            run_streams(stage_a(0))
            run_streams(G_(stage_a, 1), stage_b(0))
            for t in range(ntile):
                run_streams(G_(stage_a, t + 2), G_(stage_b, t + 1), stage_c(t))

        mixer_group(xp, 0, 8, ["prev"] * 8)
        mixer_group(xm, 0, 3, ["prompt"] * 3)
        mixer_group(xm, 3, 3, ["prompt"] * 3)
        mixer_group(xm, 6, 3, ["prompt", "prompt", "sample"])
        Bspo = S.buf("spo")
        S.dma("sp", spo.rearrange("h k v -> k h v"), Sf, reads=BSf, key=Bspo)

        h = T([128, NT, D], F32)
        Bh = [S.bufs(4, "h%d_" % t) for t in range(NT)]
        for t in range(NT):
            S.dma("sp", h[:, t, :], xm[t * 128:(t + 1) * 128, :], writes=Bh[t])
        m_w = AR.mark()
        ws = WStream("out", [128, 16, 512])

        def ld_out(db):
            def f(slot, Bs):
                S.dma("pool", slot, w_out_v[:, :, db * 512:(db + 1) * 512], writes=[Bs])
            return f
        nxt = ws.load(ld_out(0))
        for db in range(4):
            slot, Bs = nxt
            if db < 3:
                nxt = ws.load(ld_out(db + 1))
            for t in range(NT):
                bk, Bbk = nextpf()
                S.op("pe", dense16(lambda k: catT[:, k, t * 128:(t + 1) * 128], lambda k: slot[:, k, :], bk[:]),
                     reads=[Bs, BcatTo[t], BcatTm[t]], writes=[Bbk])
                hs = h[:, t, db * 512:(db + 1) * 512]
                S.op("dve", lambda: nc.vector.tensor_tensor(out=hs, in0=bk[:], in1=hs, op=ALU.add),
                     reads=[Bbk, Bh[t][db]], writes=[Bh[t][db]])
        S.barrier()
        AR.release(m_w)
        AR.r = AR.n

        wple = T([128, 2, D], BF16)
        pT = T([128, 2, NTOK], BF16)
        rpe = T([128, NT], F32)
        ps1 = T([128, 256], F32)
        pbf1 = T([128, 256], BF16)
        junk2 = T([128, 512], BF16)
        pss = [T([128, 8], F32) for i in range(2)]
        m_f = AR.mark()
        fT = T([128, 16, NTOK], BF16)
        BfT = [S.bufs(2, "fT%d_" % t) for t in range(NT)]
        m_fa = AR.mark()
        gt = T([128, D], F32)
        Bg = S.buf("gffnbc")
        load_gain(gt, Bg, g_ffn, D)
        W = NormWS("f")
        norm_T_pipe([(h[:, t, :], Bh[t], t * 128, BfT[t], None) for t in range(NT)], gt, Bg, fT, W)
        S.barrier()
        AR.release(m_fa)
        w1 = WStream("f1", [128, 2, 16, 256])
        w1.B = [S.bufs(2, "wf1_%d_" % i) for i in range(2)]
        w2 = WStream("f2", [128, 4, D], n=1)
        hT = [T([128, 4, NTOK], BF16) for i in range(2)]
        BhT = [S.bufs(4, "hT%d_" % i) for i in range(2)]
        sg = [T([128, 512], BF16) for i in range(2)]
        Bsg = S.bufs(2, "sg")
        groups = [(0, 512), (512, 512), (1024, 128)]
        NFB = DFF // 512

        def ld_f1(u):
            def f(slot, Bs):
                S.dma("pool", slot[:, 0, :, :], w_f1_v[:, :, u * 256:(u + 1) * 256], writes=[Bs[0]])
                S.dma("pool", slot[:, 1, :, :], w_f1_v[:, :, DFF + u * 256:DFF + (u + 1) * 256], writes=[Bs[1]])
            return f

        def ld_f2(fb):
            def f(slot, Bs):
                S.dma("pool", slot, w_f2_v[:, fb * 4:(fb + 1) * 4, :], writes=[Bs])
            return f
        cnt = [0]
        f1_next = [None]

        def ffn1(fb):
            hi = fb % 2
            for uu in range(2):
                u = fb * 2 + uu
                slot, Bs = f1_next[0]
                if u + 1 < 2 * NFB:
                    f1_next[0] = w1.load(ld_f1(u + 1))
                for c in range(2):
                    j = uu * 2 + c
                    for (g0, n) in groups:
                        tl = list(range(g0 // 128, (g0 + n) // 128))
                        rd = Bs + [b for t in tl for b in BfT[t]]
                        gb, Bgb = nextpf()
                        S.op("pe", dense16(lambda k: slot[:, 0, k, c * 128:(c + 1) * 128], lambda k: fT[:, k, g0:g0 + n], gb[:, 0:n]),
                             reads=rd, writes=[Bgb])
                        ub, Bub = nextpf()
                        S.op("pe", dense16(lambda k: slot[:, 1, k, c * 128:(c + 1) * 128], lambda k: fT[:, k, g0:g0 + n], ub[:, 0:n]),
                             reads=rd, writes=[Bub])
                        i = cnt[0] % 2
                        cnt[0] += 1
                        S.op("act", lambda: nc.scalar.activation(out=sg[i][:, 0:n], in_=gb[:, 0:n], func=AF.Silu),
                             reads=[Bgb], writes=[Bsg[i]])
                        S.op("dve", lambda: nc.vector.tensor_tensor(out=hT[hi][:, j, g0:g0 + n], in0=ub[:, 0:n], in1=sg[i][:, 0:n],
                                                                   op=ALU.mult),
                             reads=[Bub, Bsg[i]], writes=[BhT[hi][j]])

        def ffn2(fb, slot2, Bs2):
            hi = fb % 2
            for t in range(NT):
                for db in range(4):
                    bk, Bbk = nextpf()

                    def mm(bk=bk, db=db, t=t):
                        for j in range(4):
                            ins = nc.tensor.matmul(bk[:], lhsT=hT[hi][:, j, t * 128:(t + 1) * 128],
                                                   rhs=slot2[:, j, db * 512:(db + 1) * 512], start=(j == 0), stop=(j == 3))
                        return ins
                    S.op("pe", mm, reads=[Bs2] + BhT[hi], writes=[Bbk])
                    hs = h[:, t, db * 512:(db + 1) * 512]
                    S.op("dve", lambda: nc.vector.tensor_tensor(out=hs, in0=bk[:], in1=hs, op=ALU.add),
                         reads=[Bbk, Bh[t][db]], writes=[Bh[t][db]])

        f1_next[0] = w1.load(ld_f1(0))
        cur2 = w2.load(ld_f2(0))
        Bwple = S.buf("wple")
        BpT = S.bufs(NT, "pT")
        Brpe = S.bufs(NT, "rpe")
        Bps1, Bpbf1, Bjunk2 = S.buf("ps1"), S.buf("pbf1"), S.buf("junk2")
        Bpss = S.bufs(2, "pss")
        S.dma("pool", wple, w_ple_v, writes=[Bwple])

        def pe_mm(bk, t, db):
            def f():
                for kk in range(2):
                    ins = nc.tensor.matmul(bk[:], lhsT=pT[:, kk, t * 128:(t + 1) * 128], rhs=wple[:, kk, db * 512:(db + 1) * 512],
                                           start=(kk == 0), stop=(kk == 1))
                return ins
            return f
        for t in range(NT):
            i = t % 2
            S.dma("sp", ps1, pm[t * 128:(t + 1) * 128, :], writes=[Bps1])
            S.op("dve", lambda: nc.vector.tensor_copy(out=pbf1, in_=ps1), reads=[Bps1], writes=[Bpbf1], cost=0.3)
            transpose_cols(pbf1, [Bpbf1], 2, pT[:, :, t * 128:(t + 1) * 128], [BpT[t]], "act")
            for db in range(4):
                bk, Bbk = nextpf()
                S.op("pe", pe_mm(bk, t, db), reads=[BpT[t], Bwple], writes=[Bbk], cost=0.5)
                S.op("act", lambda: nc.scalar.activation(out=junk2, in_=bk[:], func=AF.Square, accum_out=pss[i][:, db:db + 1]),
                     reads=[Bbk], writes=[Bjunk2, Bpss[i]], cost=0.7)
            S.op("dve", lambda: nc.vector.reduce_sum(out=pss[i][:, 4:5], in_=pss[i][:, 0:4], axis=AX.X),
                 reads=[Bpss[i]], writes=[Bpss[i]], cost=0.2)
            S.op("act", lambda: nc.scalar.activation(out=pss[i][:, 5:6], in_=pss[i][:, 4:5], func=AF.Sqrt, scale=1.0 / D, bias=EPS),
                 reads=[Bpss[i]], writes=[Bpss[i]], cost=0.3)
            S.op("dve", lambda: nc.vector.reciprocal(out=rpe[:, t:t + 1], in_=pss[i][:, 5:6]), reads=[Bpss[i]], writes=[Brpe[t]], cost=0.2)
        ffn1(0)
        for fb in range(NFB):
            if fb + 1 < NFB:
                ffn1(fb + 1)
            ffn2(fb, cur2[0], cur2[1])
            if fb + 1 < NFB:
                cur2 = w2.load(ld_f2(fb + 1))
        S.barrier()
        AR.release(m_f)

        gT = T([128, 16, NTOK], BF16)
        BgT = [S.bufs(2, "gT%d_" % t) for t in range(NT)]
        gplebc = T([128, D], F32)
        Bgp = S.bufs(2, "gplefin")
        load_gain(gplebc, Bgp[0], g_ple, D)
        m_g2 = AR.mark()
        ws = WStream("pg", [128, 16, 512])

        def ld_pg(db):
            def f(slot, Bs):
                S.dma("pool", slot, w_pg_v[:, :, db * 512:(db + 1) * 512], writes=[Bs])
            return f
        pgq = [ws.load(ld_pg(0)), ws.load(ld_pg(1))]
        m_ga = AR.mark()
        gt = T([128, D], F32)
        Bg = S.buf("gpgbc")
        load_gain(gt, Bg, g_pg, D)
        W = NormWS("g")
        def gT_gen():
            prev = None
            for t in range(NT):
                i = norm_p1(h[:, t, :], Bh[t], gt, Bg, W)
                yield
                if prev is not None:
                    norm_p2(prev[0], gT, prev[1], prev[2], W)
                    yield
                prev = (i, t * 128, BgT[t])
            norm_p2(prev[0], gT, prev[1], prev[2], W)
            yield

        run_streams(gT_gen())
        S.soft_barrier()
        AR.release(m_ga)
        sig = [T([128, 512], F32) for i in range(2)]
        Bsig = S.bufs(2, "sig")
        pet = [T([128, 512], F32) for i in range(2)]
        Bpet = S.bufs(2, "pet")
        cnt = [0]
        for db in range(4):
            slot, Bs = pgq.pop(0)
            for t in range(NT):
                i = cnt[0] % 2
                cnt[0] += 1
                gb, Bgb = nextpf()
                S.op("pe", dense16(lambda k: gT[:, k, t * 128:(t + 1) * 128], lambda k: slot[:, k, :], gb[:]),
                     reads=[Bs] + BgT[t], writes=[Bgb])
                pk, Bpk = nextpf()
                S.op("pe", pe_mm(pk, t, db), reads=[BpT[t], Bwple], writes=[Bpk])
                S.op("act", lambda: nc.scalar.activation(out=sig[i], in_=gb[:], func=AF.Sigmoid), reads=[Bgb], writes=[Bsig[i]])
                S.op("dve", lambda: nc.vector.scalar_tensor_tensor(out=pet[i], in0=pk[:], scalar=rpe[:, t:t + 1],
                                                                  in1=gplebc[:, db * 512:(db + 1) * 512], op0=ALU.mult, op1=ALU.mult),
                     reads=[Bpk, Brpe[t], Bgp[0]], writes=[Bpet[i]])
                S.op("dve", lambda: nc.vector.tensor_tensor(out=pet[i], in0=pet[i], in1=sig[i], op=ALU.mult),
                     reads=[Bpet[i], Bsig[i]], writes=[Bpet[i]])
                hs = h[:, t, db * 512:(db + 1) * 512]
                S.op("dve", lambda: nc.vector.tensor_tensor(out=hs, in0=hs, in1=pet[i], op=ALU.add),
                     reads=[Bpet[i], Bh[t][db]], writes=[Bh[t][db]])
            if db + 2 < 4:
                pgq.append(ws.load(ld_pg(db + 2)))
        S.soft_barrier()
        AR.release(m_g2)
        gfinbc = T([128, D], F32)
        Bgfin = S.buf("gfinbc")
        load_gain(gfinbc, Bgfin, g_fin, D)
        fss = [T([128, 2], F32) for i in range(2)]
        Bfss = S.bufs(2, "fss")
        junk3 = T([128, D], BF16)
        Bjunk3 = S.buf("junk3")

        def fin_p1(t):
            i = t % 2
            hs = h[:, t, :]
            S.op("act", lambda: nc.scalar.activation(out=junk3, in_=hs, func=AF.Square, accum_out=fss[i][:, 0:1]),
                 reads=Bh[t], writes=[Bjunk3, Bfss[i]], cost=1.9)
            S.op("act", lambda: nc.scalar.activation(out=fss[i][:, 1:2], in_=fss[i][:, 0:1], func=AF.Sqrt, scale=1.0 / D, bias=EPS),
                 reads=[Bfss[i]], writes=[Bfss[i]])
            S.op("dve", lambda: nc.vector.reciprocal(out=fss[i][:, 1:2], in_=fss[i][:, 1:2]), reads=[Bfss[i]], writes=[Bfss[i]])

        def fin_p2(t):
            i = t % 2
            hs = h[:, t, :]
            S.op("dve", lambda: nc.vector.scalar_tensor_tensor(out=hs, in0=hs, scalar=fss[i][:, 1:2], in1=gfinbc,
                                                              op0=ALU.mult, op1=ALU.mult),
                 reads=Bh[t] + [Bfss[i], Bgfin], writes=Bh[t], cost=2.4)
            S.dma("sp", y[t * 128:(t + 1) * 128, :], hs, reads=Bh[t], key=Bh[t][0])
        fin_p1(0)
        for t in range(NT):
            if t + 1 < NT:
                fin_p1(t + 1)
            fin_p2(t)
        S.finish()
        print("[kernel] SBUF arena peak words %d / %d ; engine op counts %s" % (AR.peak, AR.n, S.cnt))
    return nc


_NC_CACHE = {}


def kernel(x_prompt, x_sample, state_gla, p_prompt, p_sample, g_mix, w_in, w_a2, b_a, g_gla_norm, g_gmlp_ln, b_gmlp_ln,
           w_s, b_s, g_gmlp_out, w_out, g_ffn, w_ffn_in, w_ffn_out, w_ple, g_ple, g_ple_gate, w_ple_gate, g_final):
    f = lambda a: np.ascontiguousarray(np.asarray(a, dtype=np.float32))
    x_prompt, x_sample, state_gla, p_prompt, p_sample = f(x_prompt), f(x_sample), f(state_gla), f(p_prompt), f(p_sample)
    shared = {
        "g_mix": f(g_mix).reshape(D), "w_in": f(w_in).reshape(D, 5136), "w_a2": f(w_a2).reshape(16, 512),
        "b_a": f(b_a).reshape(1, 512), "g_gla": f(g_gla_norm).reshape(1024), "g_ln": f(g_gmlp_ln).reshape(1024),
        "b_ln": f(b_gmlp_ln).reshape(1024), "w_s": f(w_s).reshape(8, 128, 128), "b_s": f(b_s).reshape(1, 1024),
        "g_gout": f(g_gmlp_out).reshape(1024), "w_out": f(w_out).reshape(D, D), "g_ffn": f(g_ffn).reshape(D),
        "w_f1": f(w_ffn_in).reshape(D, 2 * DFF), "w_f2": f(w_ffn_out).reshape(DFF, D), "w_ple": f(w_ple).reshape(256, D),
        "g_ple": f(g_ple).reshape(D), "g_pg": f(g_ple_gate).reshape(D), "w_pg": f(w_ple_gate).reshape(D, D),
        "g_fin": f(g_final).reshape(D),
    }
    in_maps = []
    for c in range(8):
        b, half = c // 2, c % 2
        xs = x_sample[16 * c:16 * c + 16].reshape(128, D)
        ps = p_sample[0, 16 * c:16 * c + 16].reshape(128, 256)
        xm = np.concatenate([x_prompt[b, half * 1024:(half + 1) * 1024], xs], axis=0)
        pmm = np.concatenate([p_prompt[0, b, half * 1024:(half + 1) * 1024], ps], axis=0)
        xp = x_prompt[b, 0:1024] if half == 1 else np.zeros((1024, D), np.float32)
        m = {"xm": np.ascontiguousarray(xm), "xp": np.ascontiguousarray(xp), "pm": np.ascontiguousarray(pmm),
             "st0": np.ascontiguousarray(state_gla[0, 16 * c:16 * c + 16])}
        m.update(shared)
        in_maps.append(m)
    if "nc" not in _NC_CACHE:
        _NC_CACHE["nc"] = build_nc()
    nc = _NC_CACHE["nc"]
    res = run_bass_kernel_spmd(nc, in_maps, core_ids=list(range(8)))
    r = res.results
    y_prompt = np.zeros((4, 2048, D), np.float32)
    y_sample = np.zeros((128, 8, D), np.float32)
    sp = np.zeros((1, 4, 4, 128, 256), np.float32)
    ss = np.zeros((1, 128, 4, 128, 256), np.float32)
    vs = np.zeros((1, 128, 8, 1024), np.float32)
    for c in range(8):
        b, half = c // 2, c % 2
        yc = np.asarray(r[c]["y"])
        y_prompt[b, half * 1024:(half + 1) * 1024] = yc[0:1024]
        y_sample[16 * c:16 * c + 16] = yc[1024:1152].reshape(16, 8, D)
        if half == 1:
            sp[0, b] = np.asarray(r[c]["spo"])
        ss[0, 16 * c:16 * c + 16] = np.asarray(r[c]["sso"])
        vs[0, 16 * c:16 * c + 16] = np.asarray(r[c]["vso"]).reshape(16, 8, 1024)
    return (y_prompt, y_sample, sp, ss, vs)
```

```python
import numpy as np
import concourse.bass as bass
import concourse.mybir as mybir
from concourse.bass_utils import run_bass_kernel_spmd
from contextlib import ExitStack
import types

F32 = mybir.dt.float32
BF16 = mybir.dt.bfloat16
AF = mybir.ActivationFunctionType
ALU = mybir.AluOpType
AX = mybir.AxisListType

D = 2048
NPT = 8
NT = 9
NTOK = NT * 128
NPREV = 8
DFF = 5632
EPS = 1e-6
C_Q, C_K, C_V, C_R, C_A, C_U, C_VG = 0, 512, 1024, 2048, 3072, 3088, 4112
GELU_C = 1.5957691216057308


REORDER = True
SLACK = 1.0
LAT0 = 0.4


def _snap(fn, depth=0):
    if not isinstance(fn, types.FunctionType) or depth > 4:
        return fn
    cl = fn.__closure__
    if not cl:
        return fn
    cells = []
    for c in cl:
        try:
            v = c.cell_contents
        except ValueError:
            cells.append(c)
            continue
        if isinstance(v, types.FunctionType):
            v = _snap(v, depth + 1)
        cells.append(types.CellType(v))
    g = types.FunctionType(fn.__code__, fn.__globals__, fn.__name__, fn.__defaults__, tuple(cells))
    g.__kwdefaults__ = fn.__kwdefaults__
    return g


class Buf:
    __slots__ = ("name", "w", "r", "dsem", "dcnt", "dcls")

    def __init__(self, name):
        self.name = name
        self.w = {}
        self.r = {}
        self.dsem = None
        self.dcnt = 0
        self.dcls = None


class Sched:
    def __init__(self, nc, es):
        self.nc = nc
        self.es = es
        self.eng = {"pe": nc.tensor, "act": nc.scalar, "dve": nc.vector, "pool": nc.gpsimd, "sp": nc.sync}
        self.sem = {e: es.enter_context(nc.semaphore("sem_" + e)) for e in self.eng}
        self.cnt = {e: 0 for e in self.eng}
        self.waited = {e: {} for e in self.eng}
        self.dsems = []
        self.pool = {"sw": [], "hw": []}
        self.pend = []
        self.efree = {}
        self.cur_fence = None
        self.epoch = {}
        self.nsem = 0
        self.nb = 0

    def buf(self, name=None):
        self.nb += 1
        b = Buf(name or ("b%d" % self.nb))
        b.w = dict(self.epoch)
        if self.cur_fence is not None:
            self.cur_fence.append(b)
        return b

    def soft_barrier(self):
        self.flush()
        fb = []
        self.cur_fence = fb
        self.pend.append(("fence", None, None, [], [], None, 0.0, fb))

    def _emit_fence(self, fb):
        ep = {self.sem[e]: self.cnt[e] for e in self.eng if self.cnt[e] > 0}
        for kb in self.dsems:
            ep[kb.dsem] = kb.dcnt
        for s_, v in self.epoch.items():
            if ep.get(s_, 0) < v:
                ep[s_] = v
        self.epoch = ep
        for b in fb:
            for s_, v in ep.items():
                if b.w.get(s_, 0) < v:
                    b.w[s_] = v

    def bufs(self, n, name="b"):
        return [self.buf("%s%d" % (name, i)) for i in range(n)]

    def _wait(self, e, deps):
        w = self.waited[e]
        for sem, val in deps.items():
            if w.get(sem, 0) < val:
                self.eng[e].wait_ge(sem, val)
                w[sem] = val

    def _deps(self, e, reads, writes):
        deps = {}

        def add(d):
            for s, v in d.items():
                if deps.get(s, 0) < v:
                    deps[s] = v
        for b in reads:
            add(b.w)
        for b in writes:
            add(b.w)
            add(b.r)
        if e == "pe":
            deps.pop(self.sem["pe"], None)
        return deps

    def _commit(self, tok, reads, writes):
        s, v = tok
        for b in reads:
            if b.r.get(s, 0) < v:
                b.r[s] = v
        for b in writes:
            b.w = {s: v}
            b.r = {}

    def op(self, e, fn, reads=(), writes=(), cost=None):
        if cost is None:
            cost = getattr(fn, "cost", None)
        if cost is None:
            cost = {"pe": 0.6, "act": 0.8, "dve": 0.8, "pool": 1.2, "sp": 0.2}[e]
        self.pend.append(("op", e, _snap(fn), list(reads), list(writes), None, float(cost)))

    def dma(self, e, out, in_, reads=(), writes=(), key=None):
        kb = key if key is not None else (writes[0] if writes else reads[0])
        try:
            nb = float(out.nbytes())
        except Exception:
            nb = 1e6
        self.pend.append(("dma", e, (out, in_), list(reads), list(writes), kb, 2.0 + nb / 3.0e5))

    def _emit_op(self, e, fn, reads, writes):
        self._wait(e, self._deps(e, reads, writes))
        ins = fn()
        self.cnt[e] += 1
        ins.then_inc(self.sem[e], 1)
        self._commit((self.sem[e], self.cnt[e]), reads, writes)

    def _emit_dma(self, e, out, in_, reads, writes, kb):
        cls = "sw" if e == "pool" else "hw"
        if kb.dsem is not None and kb.dcls != cls:
            raise RuntimeError("buffer %s used as DMA key from both DGE kinds" % kb.name)
        if kb.dsem is None:
            if self.pool[cls]:
                kb.dsem, kb.dcnt = self.pool[cls].pop()
            else:
                self.nsem += 1
                kb.dsem = self.es.enter_context(self.nc.semaphore("ds%s%d" % (cls, self.nsem)))
                kb.dcnt = 0
            kb.dcls = cls
            self.dsems.append(kb)
        self._wait(e, self._deps(e, reads, writes))
        kb.dcnt += 16
        self.eng[e].dma_start(out=out, in_=in_).then_inc(kb.dsem, 16)
        self._commit((kb.dsem, kb.dcnt), reads, writes)

    def flush(self):
        ops = self.pend
        self.pend = []
        n = len(ops)
        if n == 0:
            return
        lastw, readers, lastkey = {}, {}, {}
        deps = [set() for _ in range(n)]
        fence_of = {}
        prev_all = []
        for i, op in enumerate(ops):
            kind, e, fn, reads, writes, kb, cost = op[:7]
            if kind == "fence":
                deps[i].update(prev_all)
                for b in op[7]:
                    fence_of[id(b)] = i
                prev_all = [i]
                continue
            prev_all.append(i)
            for b in list(reads) + list(writes):
                if id(b) in fence_of:
                    deps[i].add(fence_of[id(b)])
            for b in reads:
                if id(b) in lastw:
                    deps[i].add(lastw[id(b)])
            for b in writes:
                if id(b) in lastw:
                    deps[i].add(lastw[id(b)])
                deps[i].update(readers.get(id(b), ()))
            if kb is not None:
                if id(kb) in lastkey:
                    deps[i].add(lastkey[id(kb)])
                lastkey[id(kb)] = i
            for b in reads:
                readers.setdefault(id(b), []).append(i)
            for b in writes:
                lastw[id(b)] = i
                readers[id(b)] = []
            deps[i].discard(i)
        if not REORDER:
            order = list(range(n))
        else:
            users = [[] for _ in range(n)]
            ndep = [len(d) for d in deps]
            for i, d in enumerate(deps):
                for j in d:
                    users[j].append(i)
            ready = [i for i in range(n) if ndep[i] == 0]
            bl = [0.0] * n
            for i in range(n - 1, -1, -1):
                m = 0.0
                for u in users[i]:
                    if bl[u] > m:
                        m = bl[u]
                bl[i] = ops[i][6] + m + (LAT0 if users[i] else 0.0)
            efree = dict(self.efree)
            fin = [0.0] * n
            order = []
            LAT = 0.4
            while ready:
                best, bt = None, None
                cand = []
                for i in ready:
                    kind, e, fn, reads, writes, kb, cost = ops[i][:7]
                    t = efree.get(e, 0.0) if e is not None else 0.0
                    for j in deps[i]:
                        tj = fin[j] + (0.0 if (ops[j][1] == e and ops[j][0] == "op") or ops[j][0] == "fence" else LAT)
                        if tj > t:
                            t = tj
                    cand.append((t, i))
                    if bt is None or t < bt:
                        bt = t
                best, bbl = None, None
                for (t, i) in cand:
                    if t <= bt + SLACK and (bbl is None or bl[i] > bbl + 1e-9 or (abs(bl[i] - bbl) <= 1e-9 and i < best)):
                        best, bbl, tb = i, bl[i], t
                bt = tb
                i = best
                ready.remove(i)
                kind, e, fn, reads, writes, kb, cost = ops[i][:7]
                if kind == "fence":
                    fin[i] = bt
                elif kind == "dma":
                    efree[e] = bt + (1.4 if e == "pool" else 0.15)
                    fin[i] = bt + cost
                else:
                    efree[e] = bt + cost
                    fin[i] = bt + cost
                order.append(i)
                for u in users[i]:
                    ndep[u] -= 1
                    if ndep[u] == 0:
                        ready.append(u)
            tmax = max(fin) if fin else 0.0
            self.efree = {e: max(0.0, efree.get(e, 0.0) - tmax) for e in self.eng}
            assert len(order) == n
        for i in order:
            kind, e, fn, reads, writes, kb, cost = ops[i][:7]
            if kind == "op":
                self._emit_op(e, fn, reads, writes)
            elif kind == "fence":
                self._emit_fence(ops[i][7])
            else:
                self._emit_dma(e, fn[0], fn[1], reads, writes, kb)
        self.cur_fence = None

    def barrier(self):
        self.flush()
        deps = {self.sem[e]: self.cnt[e] for e in self.eng if self.cnt[e] > 0}
        for kb in self.dsems:
            deps[kb.dsem] = kb.dcnt
        for e in self.eng:
            self._wait(e, dict(deps))
        for kb in self.dsems:
            self.pool[kb.dcls].append((kb.dsem, kb.dcnt))
            kb.dsem = None
        self.dsems = []
        self.epoch = {}

    def finish(self):
        self.barrier()


class Arena:
    def __init__(self, nc, nbytes):
        self.n = nbytes // 4
        self.t = nc.alloc_sbuf_tensor("arena", [128, self.n], F32)
        self.l = 0
        self.r = self.n
        self.peak = 0

    def alloc(self, shape, dt, side="l"):
        p = shape[0]
        n = int(np.prod(shape[1:]))
        words = (n * (2 if dt == BF16 else 4) + 3) // 4
        words = (words + 15) // 16 * 16
        if side == "l":
            off = self.l
            self.l += words
        else:
            self.r -= words
            off = self.r
        assert self.l <= self.r, "SBUF arena overflow l=%d r=%d" % (self.l, self.r)
        self.peak = max(self.peak, self.l + self.n - self.r)
        ap = self.t[0:p, off:off + words]
        if dt == BF16:
            ap = ap.bitcast(BF16)
        ap = ap[:, 0:n]
        if len(shape) == 3:
            ap = ap.rearrange("p (a b) -> p a b", a=shape[1])
        elif len(shape) == 4:
            ap = ap.rearrange("p (a b c) -> p a b c", a=shape[1], b=shape[2])
        return ap

    def mark(self):
        return (self.l, self.r)

    def release(self, m):
        self.l, self.r = m


def build_nc():
    nc = bass.Bass("TRN2", target_bir_lowering=False)

    def din(name, shape):
        return nc.dram_tensor(name, list(shape), F32, kind="ExternalInput").ap()

    def dout(name, shape):
        return nc.dram_tensor(name, list(shape), F32, kind="ExternalOutput").ap()

    xm = din("xm", [NTOK, D])
    xp = din("xp", [NPREV * 128, D])
    pm = din("pm", [NTOK, 256])
    st0 = din("st0", [16, 4, 128, 256])
    g_mix = din("g_mix", [D])
    w_in = din("w_in", [D, 5136])
    w_a2 = din("w_a2", [16, 512])
    b_a = din("b_a", [1, 512])
    g_gla = din("g_gla", [1024])
    g_ln = din("g_ln", [1024])
    b_ln = din("b_ln", [1024])
    w_s = din("w_s", [8, 128, 128])
    b_s = din("b_s", [1, 1024])
    g_gout = din("g_gout", [1024])
    w_out = din("w_out", [D, D])
    g_ffn = din("g_ffn", [D])
    w_f1 = din("w_f1", [D, 2 * DFF])
    w_f2 = din("w_f2", [DFF, D])
    w_ple = din("w_ple", [256, D])
    g_ple = din("g_ple", [D])
    g_pg = din("g_pg", [D])
    w_pg = din("w_pg", [D, D])
    g_fin = din("g_fin", [D])
    y = dout("y", [NTOK, D])
    spo = dout("spo", [4, 128, 256])
    sso = dout("sso", [16, 4, 128, 256])
    vso = dout("vso", [128, 1024])

    w_in_v = w_in.rearrange("(k p) n -> p k n", p=128)
    w_out_v = w_out.rearrange("(k p) n -> p k n", p=128)
    w_f1_v = w_f1.rearrange("(k p) n -> p k n", p=128)
    w_f2_v = w_f2.rearrange("(j p) n -> p j n", p=128)
    w_ple_v = w_ple.rearrange("(k p) n -> p k n", p=128)
    w_pg_v = w_pg.rearrange("(k p) n -> p k n", p=128)

    with ExitStack() as es:
        S = Sched(nc, es)
        AR = Arena(nc, 204 * 1024)
        T = AR.alloc

        pf = [es.enter_context(nc.psum_tensor("pf%d" % i, [128, 512], F32)) for i in range(6)]
        pb = [es.enter_context(nc.psum_tensor("pb%d" % i, [128, 1024], BF16)) for i in range(2)]
        Bpf = S.bufs(6, "pf")
        Bpb = S.bufs(2, "pb")
        rr = {"pf": 0, "pb": 0}
        held = set()

        def nextpf():
            while True:
                i = rr["pf"]
                rr["pf"] = (i + 1) % 6
                if Bpf[i] not in held:
                    return pf[i], Bpf[i]

        def run_streams(*gens):
            gl = []
            for g in gens:
                if g is None:
                    continue
                gl.append(list(g) if isinstance(g, tuple) else [g, 1])
            while gl:
                for item in list(gl):
                    for _ in range(item[1]):
                        try:
                            next(item[0])
                        except StopIteration:
                            gl.remove(item)
                            break

        def nextpb():
            i = rr["pb"]
            rr["pb"] = (i + 1) % 2
            return pb[i], Bpb[i]

        ident = T([128, 128], BF16)
        identf = T([128, 128], F32, "r")
        tri = T([128, 128], F32)
        blk = T([128, 128], F32)
        ET = T([128, 16], F32)
        Ebc = T([128, 16, 128], BF16)
        Ebcf = T([128, 16, 128], F32, "r")
        onescol = T([128, 1], F32)
        onesrow = T([1, 128], F32)
        Bc = S.buf("consts")

        def asel(ap, pattern, base, cm, op=ALU.is_ge, fill=0.0):
            S.op("pool", lambda: nc.gpsimd.affine_select(out=ap, in_=ap, pattern=pattern, compare_op=op, fill=fill,
                                                          base=base, channel_multiplier=cm), writes=[Bc])

        S.op("pool", lambda: nc.gpsimd.memset(identf, 0.0), writes=[Bc])
        asel(identf, [[-1, 128]], 0, 1, op=ALU.not_equal, fill=1.0)
        S.op("pool", lambda: nc.gpsimd.tensor_copy(out=ident, in_=identf), writes=[Bc])
        S.op("pool", lambda: nc.gpsimd.memset(tri, 1.0), writes=[Bc])
        asel(tri, [[1, 128]], 0, -1)
        S.op("pool", lambda: nc.gpsimd.memset(blk, 1.0), writes=[Bc])
        asel(blk, [[1, 128]], 0, -1)
        blk3 = blk.rearrange("p (i s) -> p i s", i=16)
        asel(blk3, [[-8, 16], [0, 8]], 0, 1)
        asel(blk3, [[8, 16], [0, 8]], 7, -1)
        S.op("pool", lambda: nc.gpsimd.memset(ET, 1.0), writes=[Bc])
        asel(ET, [[-8, 16]], 0, 1)
        asel(ET, [[8, 16]], 7, -1)
        S.op("pool", lambda: nc.gpsimd.memset(Ebcf, 1.0), writes=[Bc])
        asel(Ebcf, [[-8, 16], [1, 128]], 0, 0)
        asel(Ebcf, [[8, 16], [-1, 128]], 7, 0)
        S.op("pool", lambda: nc.gpsimd.tensor_copy(out=Ebc, in_=Ebcf), writes=[Bc])
        S.op("pool", lambda: nc.gpsimd.memset(onescol, 1.0), writes=[Bc])
        S.op("pool", lambda: nc.gpsimd.memset(onesrow, 1.0), writes=[Bc])
        CONST = [Bc]

        Sf = T([128, 4, 256], F32)
        Sb = T([128, 4, 256], BF16)
        BSf = S.bufs(4, "Sf")
        BSb = S.bufs(4, "Sb")
        walow = T([128, 16, 16], BF16)
        Bwalow = S.buf("walow")
        S.dma("pool", walow, w_in_v[:, :, C_A:C_A + 16], writes=[Bwalow])
        S.op("dve", lambda: nc.vector.memset(Sf, 0.0), writes=BSf)
        S.op("dve", lambda: nc.vector.memset(Sb, 0.0), writes=BSb)

        def load_gain(gt, Bg, vec, n):
            S.dma("sp", gt[:, 0:n], vec.partition_broadcast(128), writes=[Bg])

        class NormWS:
            def __init__(self, tag, side="l"):
                self.xn = [T([128, D], BF16, side) for i in range(2)]
                self.ss = [T([128, 2], F32, side) for i in range(2)]
                self.Bxn = S.bufs(2, "xn" + tag)
                self.Bss = S.bufs(2, "nss" + tag)
                self.i = 0

        def norm_p1(src, Bsrc, gbc, Bg, W):
            i = W.i
            W.i ^= 1
            xn, Bxn, ss, Bss = W.xn[i], W.Bxn[i], W.ss[i], W.Bss[i]
            S.op("act", lambda: nc.scalar.activation(out=xn, in_=src, func=AF.Square, accum_out=ss[:, 0:1]),
                 reads=Bsrc, writes=[Bxn, Bss], cost=1.9)
            S.op("act", lambda: nc.scalar.activation(out=ss[:, 1:2], in_=ss[:, 0:1], func=AF.Sqrt, scale=1.0 / D, bias=EPS),
                 reads=[Bss], writes=[Bss])
            S.op("dve", lambda: nc.vector.reciprocal(out=ss[:, 1:2], in_=ss[:, 1:2]), reads=[Bss], writes=[Bss])
            S.op("dve", lambda: nc.vector.scalar_tensor_tensor(out=xn, in0=src, scalar=ss[:, 1:2], in1=gbc,
                                                              op0=ALU.mult, op1=ALU.mult),
                 reads=list(Bsrc) + [Bss, Bg], writes=[Bxn], cost=2.4)
            return i

        def norm_p2(i, dstT, c0, Bdst, W):
            xn, Bxn = W.xn[i], W.Bxn[i]
            for half in range(2):
                bank, Bb = nextpb()

                def tr(bank=bank, half=half):
                    for j in range(8):
                        k = half * 8 + j
                        ins = nc.tensor.transpose(out=bank[:, j * 128:(j + 1) * 128], in_=xn[:, k * 128:(k + 1) * 128],
                                                  identity=ident)
                    return ins
                S.op("pe", tr, reads=[Bxn, Bc], writes=[Bb])
                dst = dstT[:, half * 8:(half + 1) * 8, c0:c0 + 128]
                src_ps = bank[:].rearrange("p (k t) -> p k t", k=8)
                if half == 0:
                    S.op("act", lambda: nc.scalar.copy(out=dst, in_=src_ps), reads=[Bb], writes=[Bdst[half]])
                else:
                    S.op("dve", lambda: nc.vector.tensor_copy(out=dst, in_=src_ps), reads=[Bb], writes=[Bdst[half]])

        def norm_T_pipe(items, gbc, Bg, dstT, W):
            prev = None
            for (src, Bsrc, c0, Bdst, pre) in items:
                if pre is not None:
                    pre()
                i = norm_p1(src, Bsrc, gbc, Bg, W)
                if prev is not None:
                    norm_p2(prev[0], dstT, prev[1], prev[2], W)
                prev = (i, c0, Bdst)
            norm_p2(prev[0], dstT, prev[1], prev[2], W)

        def transpose_cols(src_tok, Bsrc, nchunk, dst, Bdst, eng):
            bank, Bb = nextpb()

            def tr():
                for j in range(nchunk):
                    ins = nc.tensor.transpose(out=bank[:, j * 128:(j + 1) * 128], in_=src_tok[:, j * 128:(j + 1) * 128],
                                              identity=ident)
                return ins
            S.op("pe", tr, reads=list(Bsrc) + [Bc], writes=[Bb])
            src_ps = bank[:, 0:nchunk * 128].rearrange("p (k t) -> p k t", k=nchunk)
            if eng == "act":
                S.op("act", lambda: nc.scalar.copy(out=dst, in_=src_ps), reads=[Bb], writes=Bdst)
            else:
                S.op("dve", lambda: nc.vector.tensor_copy(out=dst, in_=src_ps), reads=[Bb], writes=Bdst)

        class WStream:
            def __init__(self, tag, shape, n=2, side="l"):
                self.slots = [T(shape, BF16, side) for i in range(n)]
                self.B = S.bufs(n, "w" + tag)
                self.n = n
                self.i = 0

            def load(self, fn):
                i = self.i
                self.i = (i + 1) % self.n
                fn(self.slots[i], self.B[i])
                return self.slots[i], self.B[i]

        def dense16(lhsT_fn, rhs_fn, out_ap, nk=16):
            def f():
                for k in range(nk):
                    ins = nc.tensor.matmul(out_ap, lhsT=lhsT_fn(k), rhs=rhs_fn(k), start=(k == 0), stop=(k == nk - 1))
                return ins
            f.cost = nk * 0.215
            return f

        def gelu_evac(bank, Bb, out_ap, Bout, tmp, Btmp, accum=None, Baccum=None, pool_affine=False):
            if accum is None:
                S.op("act", lambda: nc.scalar.activation(out=out_ap, in_=bank, func=AF.Gelu_apprx_tanh),
                     reads=[Bb], writes=Bout, cost=0.7)
            else:
                S.op("act", lambda: nc.scalar.activation(out=out_ap, in_=bank, func=AF.Gelu_apprx_tanh, accum_out=accum),
                     reads=[Bb], writes=list(Bout) + [Baccum], cost=0.7)

        S.barrier()
        AR.r = AR.n
        catT = T([128, 16, NTOK], BF16, "r")
        BcatTo = S.bufs(NT, "catTo")
        BcatTm = S.bufs(NT, "catTm")
        def mixer_group(xsrc, gt0, ntile, kinds):
            m_grp = AR.mark()
            ntok = ntile * 128
            main = kinds[0] != "prev"
            has_sample = "sample" in kinds
            aT = T([128, 16, ntok], BF16)
            BaT = [S.bufs(2, "aT%d_" % t) for t in range(ntile)]
            ws = WStream("in", [128, 16, 512])

            def loader(col0):
                def f(slot, Bs):
                    S.dma("pool", slot, w_in_v[:, :, col0:col0 + 512], writes=[Bs])
                return f
            pre = [ws.load(loader(C_V)), ws.load(loader(C_V + 512))]
            if main:
                gelu_u = T([128, ntile, 1024], BF16)
                Bgu = [S.bufs(2, "gelu_u%d_" % t) for t in range(ntile)]
                gtmp = [T([128, 512], F32) for i in range(2)]
                Bgtmp = S.bufs(2, "gtmp")
            m_a = AR.mark()
            xs = [T([128, D], F32) for i in range(2)]
            Bxs = S.bufs(2, "xs")
            gt = T([128, D], F32)
            Bg = S.buf("gmixbc")
            load_gain(gt, Bg, g_mix, D)
            W = NormWS("a")

            def mk_pre(t):
                return lambda: S.dma("sp", xs[t % 2], xsrc[(gt0 + t) * 128:(gt0 + t + 1) * 128, :], writes=[Bxs[t % 2]])
            norm_T_pipe([(xs[t % 2], [Bxs[t % 2]], t * 128, BaT[t], mk_pre(t)) for t in range(ntile)], gt, Bg, aT, W)
            if main:
                S.soft_barrier()
                AR.release(m_a)
            m_gla = AR.mark()
            wa2 = T([16, 512], F32)
            barow = T([1, 512], F32)
            Bwa = S.buf("wa2")
            Bba = S.buf("barow")
            S.dma("sp", wa2, w_a2, writes=[Bwa])
            S.dma("sp", barow, b_a, writes=[Bba])
            alowT = T([16, ntok], F32)
            BalowT = S.bufs((ntok + 511) // 512, "alowT")
            ek = T([128, ntile, 512], BF16)
            Bek = S.bufs(ntile, "ek")
            dec = T([128, ntile, 64], F32)
            Bdec = S.bufs(ntile, "dec")
            ktok = T([128, ntile, 512], BF16)
            Bktok = S.bufs(ntile, "ktok")
            vtok = T([128, ntile, 1024], BF16)
            Bvtok = [S.bufs(2, "vtok%d_" % t) for t in range(ntile)]
            lp = [T([128, 512], F32) for i in range(2)]
            Blp = S.bufs(2, "lp")
            Tt = T([128, 4, 256], F32)
            BTt = S.bufs(2, "Tt")
            if main:
                eq = T([128, ntile, 512], BF16)
                Beq = S.bufs(ntile, "eq")
                qT = T([128, 4, ntok], BF16)
                BqT = S.bufs(ntile, "qT")
                kT = T([128, 4, ntok], BF16)
                BkT = S.bufs(ntile, "kT")
                silur = T([128, ntile, 1024], BF16)
                Bsilur = [S.bufs(2, "silur%d_" % t) for t in range(ntile)]
                qtmp = [T([128, 512], BF16) for i in range(2)]
                Bqtmp = S.bufs(2, "qtmp")
                stmp = [T([128, 512], F32) for i in range(2)]
                Bstmp = S.bufs(2, "stmp")
                ggla = T([128, 1024], F32)
                Bggla = S.buf("gglabc")
                load_gain(ggla, Bggla, g_gla, 1024)
                attb = [T([128, 4, 128], BF16) for i in range(2)]
                Battb = S.bufs(2, "attb")
                on = [T([128, 1024], BF16) for i in range(2)]
                Bon = S.bufs(2, "on")
                ojunk = T([128, 256], BF16)
                Bojunk = S.buf("ojunk")
                oss = [T([128, 8], F32) for i in range(2)]
                Boss = S.bufs(2, "oss")
                if has_sample:
                    s0f = [T([128, 4, 256], F32) for i in range(2)]
                    Bs0f = S.bufs(2, "s0f")
                    s0b = [T([128, 4, 256], BF16) for i in range(2)]
                    Bs0b = S.bufs(2, "s0b")
                    qTm = [T([128, 4, 128], BF16) for i in range(2)]
                    BqTm = S.bufs(2, "qTm")
                    ktm = [T([128, 512], BF16) for i in range(2)]
                    Bktm = S.bufs(2, "ktm")
            flags = {"decay": False, "gla_in": False}
            kdone = [False] * ntile

            for gi, g0 in enumerate(range(0, ntok, 512)):
                n = min(512, ntok - g0)
                bank, Bb = nextpf()
                tl = list(range(g0 // 128, (g0 + n) // 128))
                S.op("pe", dense16(lambda k: walow[:, k, :], lambda k: aT[:, k, g0:g0 + n], bank[0:16, 0:n]),
                     reads=[Bwalow] + [b for t in tl for b in BaT[t]], writes=[Bb])
                S.op("act", lambda: nc.scalar.copy(out=alowT[:, g0:g0 + n], in_=bank[0:16, 0:n]), reads=[Bb], writes=[BalowT[gi]])

            def decay_gen():
                for t in range(ntile):
                    c0 = t * 128
                    sample = kinds[t] == "sample"
                    M = blk if sample else tri
                    nseg = 16 if sample else 1
                    seg = ET if sample else onescol
                    bank, Bb = nextpf()

                    def mm_la(bank=bank, c0=c0):
                        nc.tensor.matmul(bank[:], lhsT=alowT[:, c0:c0 + 128], rhs=wa2, start=True, stop=False)
                        return nc.tensor.matmul(bank[:], lhsT=onesrow, rhs=barow, start=False, stop=True)
                    S.op("pe", mm_la, reads=[BalowT[c0 // 512], Bwa, Bba] + CONST, writes=[Bb])
                    l, Bl = lp[t % 2], Blp[t % 2]
                    S.op("act", lambda: nc.scalar.activation(out=l, in_=bank[:], func=AF.Exp, scale=-1.0), reads=[Bb], writes=[Bl])
                    S.op("act", lambda: nc.scalar.activation(out=l, in_=l, func=AF.Ln, bias=1.0), reads=[Bl], writes=[Bl])
                    yield
                    bank2, Bb2 = nextpf()
                    S.op("pe", lambda: nc.tensor.matmul(bank2[:], lhsT=M, rhs=l, start=True, stop=True),
                         reads=[Bl] + CONST, writes=[Bb2])
                    bank3, Bb3 = nextpf()

                    def mm_cl(bank3=bank3, l=l, seg=seg, nseg=nseg):
                        for h in range(4):
                            ins = nc.tensor.matmul(bank3[:, h * nseg:(h + 1) * nseg], lhsT=l[:, h * 128:(h + 1) * 128],
                                                   rhs=seg, start=True, stop=True)
                        return ins
                    S.op("pe", mm_cl, reads=[Bl] + CONST, writes=[Bb3])
                    if main:
                        S.op("act", lambda: nc.scalar.activation(out=eq[:, t, :], in_=bank2[:], func=AF.Exp, scale=-1.0 / 16),
                             reads=[Bb2], writes=[Beq[t]])
                    S.op("act", lambda: nc.scalar.activation(out=ek[:, t, :], in_=bank2[:], func=AF.Exp, scale=1.0 / 16),
                         reads=[Bb2], writes=[Bek[t]])
                    S.op("act", lambda: nc.scalar.activation(out=dec[:, t, 0:4 * nseg], in_=bank3[:, 0:4 * nseg], func=AF.Exp,
                                                             scale=-1.0 / 16), reads=[Bb3], writes=[Bdec[t]])
                    yield
                flags["decay"] = True

            def ev_k(t, bank, Bb):
                S.op("dve", lambda: nc.vector.tensor_tensor(out=ktok[:, t, :], in0=bank[:], in1=ek[:, t, :], op=ALU.mult),
                     reads=[Bb, Bek[t]], writes=[Bktok[t]])
                if main:
                    transpose_cols(ktok[:, t, :], [Bktok[t]], 4, kT[:, :, t * 128:(t + 1) * 128], [BkT[t]], "act")
                kdone[t] = True

            def ev_q(t, bank, Bb):
                i = t % 2
                S.op("dve", lambda: nc.vector.scalar_tensor_tensor(out=qtmp[i], in0=bank[:], scalar=128.0 ** -0.5,
                                                                  in1=eq[:, t, :], op0=ALU.mult, op1=ALU.mult),
                     reads=[Bb, Beq[t]], writes=[Bqtmp[i]])
                transpose_cols(qtmp[i], [Bqtmp[i]], 4, qT[:, :, t * 128:(t + 1) * 128], [BqT[t]], "act")

            def ev_v(half):
                def f(t, bank, Bb):
                    S.op("act", lambda: nc.scalar.copy(out=vtok[:, t, half * 512:(half + 1) * 512], in_=bank[:]),
                         reads=[Bb], writes=[Bvtok[t][half]])
                return f

            def ev_r(half):
                def f(t, bank, Bb):
                    i = t % 2
                    S.op("act", lambda: nc.scalar.activation(out=stmp[i], in_=bank[:], func=AF.Silu),
                         reads=[Bb], writes=[Bstmp[i]])
                    S.op("dve", lambda: nc.vector.tensor_tensor(out=silur[:, t, half * 512:(half + 1) * 512], in0=stmp[i],
                                                               in1=ggla[:, half * 512:(half + 1) * 512], op=ALU.mult),
                         reads=[Bstmp[i], Bggla], writes=[Bsilur[t][half]])
                return f
            ucnt = [0]

            def ev_u(half):
                def f(t, bank, Bb):
                    i = ucnt[0] % 2
                    ucnt[0] += 1
                    gelu_evac(bank[:], Bb, gelu_u[:, t, half * 512:(half + 1) * 512], [Bgu[t][half]], gtmp[i], Bgtmp[i])
                return f

            specs = [(C_V, ev_v(0), False), (C_V + 512, ev_v(1), False), (C_K, ev_k, True)]
            if main:
                specs += [(C_Q, ev_q, True), (C_R, ev_r(0), False), (C_R + 512, ev_r(1), False),
                          (C_U, ev_u(0), False), (C_U + 512, ev_u(1), False)]
            n_gla_in = 3 if not main else 6
            vgslots = []

            def dense_gen():
                q = list(pre)
                nloaded = len(q)
                for bi, (col0, evac, need) in enumerate(specs):
                    if bi == n_gla_in:
                        flags["gla_in"] = True
                    if need:
                        while not flags["decay"]:
                            yield
                    slot, Bs = q.pop(0)
                    if not q and nloaded < len(specs):
                        q.append(ws.load(loader(specs[nloaded][0])))
                        nloaded += 1
                    elif not q and main and not vgslots:
                        vgslots.append(ws.load(loader(C_VG)))
                    for t in range(ntile):
                        bank, Bb = nextpf()
                        S.op("pe", dense16(lambda k: aT[:, k, t * 128:(t + 1) * 128], lambda k: slot[:, k, :], bank[:]),
                             reads=[Bs] + BaT[t], writes=[Bb])
                        evac(t, bank, Bb)
                        yield
                if main:
                    vgslots.append(ws.load(loader(C_VG + 512)))
                flags["gla_in"] = True

            def state_update(t):
                banks = [nextpf(), nextpf()]

                def mm():
                    for h in range(4):
                        bk = banks[h // 2][0]
                        ins = nc.tensor.matmul(bk[:, (h % 2) * 256:(h % 2 + 1) * 256], lhsT=ktok[:, t, h * 128:(h + 1) * 128],
                                               rhs=vtok[:, t, h * 256:(h + 1) * 256], start=True, stop=True)
                    return ins
                S.op("pe", mm, reads=[Bktok[t]] + Bvtok[t], writes=[banks[0][1], banks[1][1]])
                for hh in range(2):
                    bk, Bbk = banks[hh]
                    S.op("dve", lambda: nc.vector.tensor_tensor(out=Tt[:, 2 * hh:2 * hh + 2, :],
                                                               in0=bk[:].rearrange("p (a v) -> p a v", a=2),
                                                               in1=Sf[:, 2 * hh:2 * hh + 2, :], op=ALU.add),
                         reads=[Bbk, BSf[2 * hh], BSf[2 * hh + 1]], writes=[BTt[hh]])
                decb = dec[:, t, 0:4].unsqueeze(2).to_broadcast([128, 4, 256])
                S.op("dve", lambda: nc.vector.tensor_tensor(out=Sb, in0=Tt, in1=decb, op=ALU.mult),
                     reads=BTt + [Bdec[t]], writes=BSb)
                S.op("pool", lambda: nc.gpsimd.tensor_tensor(out=Sf, in0=Tt, in1=decb, op=ALU.mult),
                     reads=BTt + [Bdec[t]], writes=BSf, cost=2.2)

            def gla_gen():
                while main and not flags["gla_in"]:
                    yield
                for t in range(ntile):
                    if kinds[t] == "prev":
                        while not kdone[t]:
                            yield
                        state_update(t)
                        yield
                        continue
                    c0 = t * 128
                    gc0 = (gt0 + t) * 128
                    sample = kinds[t] == "sample"
                    M = blk if sample else tri
                    nob = 4 if sample else 2
                    ob = [nextpf() for _ in range(nob)]
                    Bob = [x[1] for x in ob]
                    for b_ in Bob:
                        held.add(b_)

                    def oreg(h, ob=ob, nob=nob):
                        if nob == 4:
                            return ob[h][0][:, 0:256]
                        return ob[h // 2][0][:, (h % 2) * 256:(h % 2 + 1) * 256]

                    def obuf(h, Bob=Bob, nob=nob):
                        return Bob[h] if nob == 4 else Bob[h // 2]
                    ab, Bab = nextpf()

                    def mm_att(ab=ab, c0=c0):
                        for h in range(4):
                            ins = nc.tensor.matmul(ab[:, h * 128:(h + 1) * 128], lhsT=kT[:, h, c0:c0 + 128],
                                                   rhs=qT[:, h, c0:c0 + 128], start=True, stop=True)
                        return ins
                    S.op("pe", mm_att, reads=[BkT[t], BqT[t]], writes=[Bab])
                    ai = t % 2
                    S.op("dve", lambda: nc.vector.tensor_tensor(out=attb[ai], in0=ab[:].rearrange("p (h t) -> p h t", h=4),
                                                               in1=M.unsqueeze(1).to_broadcast([128, 4, 128]), op=ALU.mult),
                         reads=[Bab] + CONST, writes=[Battb[ai]])
                    if not sample:
                        yield
                    else:
                        def ld_state(i):
                            sl = i % 2
                            S.dma("sp", s0f[sl], st0[i].rearrange("h k v -> k h v"), writes=[Bs0f[sl]])
                            S.dma("pool", s0b[sl], st0[i].rearrange("h k v -> k h v"), writes=[Bs0b[sl]])
                        for i in range(16):
                            sl = i % 2
                            ld_state(i)
                            S.op("dve", lambda: nc.vector.tensor_tensor(out=qTm[sl], in0=qT[:, :, c0:c0 + 128],
                                                                       in1=Ebc[:, i, :].unsqueeze(1).to_broadcast([128, 4, 128]),
                                                                       op=ALU.mult),
                                 reads=[BqT[t], Bc], writes=[BqTm[sl]])
                            S.op("act", lambda: nc.scalar.activation(out=ktm[sl], in_=ktok[:, t, :], func=AF.Copy,
                                                                     scale=ET[:, i:i + 1]),
                                 reads=[Bktok[t], Bc], writes=[Bktm[sl]])

                            def mm_inter_s(i=i, sl=sl, oreg=oreg):
                                for h in range(4):
                                    ins = nc.tensor.matmul(oreg(h), lhsT=qTm[sl][:, h, :], rhs=s0b[sl][:, h, :],
                                                           start=(i == 0), stop=False)
                                return ins
                            S.op("pe", mm_inter_s, reads=[BqTm[sl], Bs0b[sl]], writes=Bob)
                            sb2 = [nextpf(), nextpf()]

                            def mm_s(sl=sl, sb2=sb2):
                                for h in range(4):
                                    ins = nc.tensor.matmul(sb2[h // 2][0][:, (h % 2) * 256:(h % 2 + 1) * 256],
                                                           lhsT=ktm[sl][:, h * 128:(h + 1) * 128],
                                                           rhs=vtok[:, t, h * 256:(h + 1) * 256], start=True, stop=True)
                                return ins
                            S.op("pe", mm_s, reads=[Bktm[sl]] + Bvtok[t], writes=[sb2[0][1], sb2[1][1]])
                            for hh in range(2):
                                S.op("dve", lambda: nc.vector.tensor_tensor(
                                    out=s0f[sl][:, 2 * hh:2 * hh + 2, :], in0=sb2[hh][0][:].rearrange("p (a v) -> p a v", a=2),
                                    in1=s0f[sl][:, 2 * hh:2 * hh + 2, :], op=ALU.add),
                                    reads=[sb2[hh][1], Bs0f[sl]], writes=[Bs0f[sl]])
                            S.op("dve", lambda: nc.vector.tensor_tensor(
                                out=s0f[sl], in0=s0f[sl],
                                in1=dec[:, t, :].rearrange("p (h i) -> p h i", h=4)[:, :, i:i + 1].to_broadcast([128, 4, 256]),
                                op=ALU.mult), reads=[Bs0f[sl], Bdec[t]], writes=[Bs0f[sl]])
                            S.dma("sp", sso[i].rearrange("h k v -> k h v"), s0f[sl], reads=[Bs0f[sl]], key=Bs0f[sl])
                            yield

                    def mm_o(ai=ai, oreg=oreg, sample=sample, c0=c0):
                        for h in range(4):
                            if not sample:
                                nc.tensor.matmul(oreg(h), lhsT=qT[:, h, c0:c0 + 128], rhs=Sb[:, h, :], start=True, stop=False)
                            ins = nc.tensor.matmul(oreg(h), lhsT=attb[ai][:, h, :], rhs=vtok[:, t, h * 256:(h + 1) * 256],
                                                   start=False, stop=True)
                        return ins
                    S.op("pe", mm_o, reads=[Battb[ai], BqT[t]] + BSb + Bvtok[t], writes=Bob, cost=1.2)
                    if not sample:
                        state_update(t)
                    yield
                    os_, Bos = oss[ai], Boss[ai]
                    for h in range(4):
                        S.op("act", lambda: nc.scalar.activation(out=ojunk, in_=oreg(h), func=AF.Square, accum_out=os_[:, h:h + 1]),
                             reads=[obuf(h)], writes=[Bojunk, Bos])
                    S.op("act", lambda: nc.scalar.activation(out=os_[:, 4:8], in_=os_[:, 0:4], func=AF.Sqrt, scale=1.0 / 256, bias=EPS),
                         reads=[Bos], writes=[Bos])
                    S.op("dve", lambda: nc.vector.reciprocal(out=os_[:, 4:8], in_=os_[:, 4:8]), reads=[Bos], writes=[Bos])
                    yield
                    for h in range(4):
                        S.op("dve", lambda: nc.vector.scalar_tensor_tensor(
                            out=on[ai][:, h * 256:(h + 1) * 256], in0=oreg(h),
                            scalar=os_[:, 4 + h:5 + h], in1=silur[:, t, h * 256:(h + 1) * 256], op0=ALU.mult, op1=ALU.mult),
                            reads=[obuf(h), Bos] + Bsilur[t], writes=[Bon[ai]])
                    for b_ in Bob:
                        held.discard(b_)
                    transpose_cols(on[ai], [Bon[ai]], 8, catT[:, 0:8, gc0:gc0 + 128], [BcatTo[gt0 + t]], "act")
                    yield

            run_streams(decay_gen(), dense_gen(), gla_gen())
            S.soft_barrier()
            AR.release(m_gla)
            if main:
                gmlp_group(aT, BaT, vgslots, gt0, ntile, kinds, gelu_u, Bgu, gtmp, Bgtmp)
            S.barrier()
            AR.release(m_grp)

        def gmlp_group(aT, BaT, vslots, gt0, ntile, kinds, gelu_u, Bgu, gtmp, Bgtmp):
            has_sample = "sample" in kinds
            wsb = T([128, 8, 128], BF16)
            Bwsb = S.buf("wsb")
            S.dma("pool", wsb, w_s.rearrange("g t s -> t g s"), writes=[Bwsb])
            WT = T([128, 8, 128], BF16)
            BWT = S.buf("WT")
            bank, Bb = nextpb()

            def trw():
                for g in range(8):
                    ins = nc.tensor.transpose(out=bank[:, g * 128:(g + 1) * 128], in_=wsb[:, g, :], identity=ident)
                return ins
            S.op("pe", trw, reads=[Bwsb, Bc], writes=[Bb])
            S.op("dve", lambda: nc.vector.tensor_tensor(out=WT, in0=bank[:].rearrange("p (g t) -> p g t", g=8),
                                                       in1=tri.unsqueeze(1).to_broadcast([128, 8, 128]), op=ALU.mult),
                 reads=[Bb, Bc], writes=[BWT])
            bs2 = T([2, 1024], F32)
            bhi = T([2, 1024], BF16)
            blo = T([2, 1024], F32)
            bsrow = T([2, 1024], BF16)
            ones2 = T([2, 128], BF16)
            Bbs = S.buf("bsrow")
            S.dma("sp", bs2, b_s[0].partition_broadcast(2), writes=[Bbs])
            S.op("dve", lambda: nc.vector.memset(ones2, 1.0), writes=[Bbs])
            S.op("dve", lambda: nc.vector.tensor_copy(out=bhi, in_=bs2), reads=[Bbs], writes=[Bbs])
            S.op("dve", lambda: nc.vector.tensor_tensor(out=blo, in0=bs2, in1=bhi, op=ALU.subtract), reads=[Bbs], writes=[Bbs])
            S.op("dve", lambda: nc.vector.tensor_scalar(out=bs2, in0=bhi, scalar1=ident[0:2, 0:1], scalar2=None, op0=ALU.mult),
                 reads=[Bbs, Bc], writes=[Bbs])
            S.op("dve", lambda: nc.vector.scalar_tensor_tensor(out=bsrow, in0=blo, scalar=ident[0:2, 1:2], in1=bs2,
                                                              op0=ALU.mult, op1=ALU.add), reads=[Bbs, Bc], writes=[Bbs])
            if has_sample:
                Ball = T([8, 8, 16, 8], BF16)
                Arep = T([8, 16, 8], BF16)
                BBall = S.buf("Ball")
                S.op("dve", lambda: nc.vector.tensor_copy(out=Ball, in_=WT[0:8, :, 0:8].unsqueeze(2).to_broadcast([8, 8, 16, 8])),
                     reads=[BWT], writes=[BBall])
                S.op("dve", lambda: nc.vector.tensor_copy(out=Arep, in_=ident[0:8, 0:8].unsqueeze(1).to_broadcast([8, 16, 8])),
                     reads=[Bc], writes=[BBall])
                Wblk = T([128, 8, 128], BF16)
                BWblk = S.bufs(2, "Wblk")
                for half in range(2):
                    bk, Bbk = nextpf()

                    def mmw(bk=bk, half=half):
                        for g4 in range(4):
                            g = half * 4 + g4
                            ins = nc.tensor.matmul(bk[:, g4 * 128:(g4 + 1) * 128], lhsT=Arep.rearrange("p i s -> p (i s)"),
                                                   rhs=Ball[:, g, :, :].rearrange("p j t -> p (j t)"), start=True, stop=True)
                        return ins
                    S.op("pe", mmw, reads=[BBall], writes=[Bbk])
                    S.op("dve", lambda: nc.vector.tensor_tensor(out=Wblk[:, half * 4:(half + 1) * 4, :],
                                                               in0=bk[:].rearrange("p (g t) -> p g t", g=4),
                                                               in1=blk.unsqueeze(1).to_broadcast([128, 4, 128]), op=ALU.mult),
                         reads=[Bbk, Bc], writes=[BWblk[half]])
                bsrow_s = T([2, 1024], BF16)
                Bbss = S.buf("bsrow_s")
                S.op("dve", lambda: nc.vector.tensor_copy(
                    out=bsrow_s.rearrange("p (g j t) -> p g j t", g=8, j=16),
                    in_=bsrow.rearrange("p (g t) -> p g t", g=8)[:, :, 0:8].unsqueeze(2).to_broadcast([2, 8, 16, 8])),
                    reads=[Bbs], writes=[Bbss])
            vn = T([128, ntile, 1024], BF16)
            Bvn = S.bufs(ntile, "vn")
            gv = [T([128, 1024], F32) for i in range(2)]
            Bgv = [S.bufs(2, "gv%d_" % i) for i in range(2)]
            glnbc = T([128, 1024], F32)
            blnbc = T([128, 1024], F32)
            goutbc = T([128, 1024], F32)
            Bgl = S.bufs(3, "lngain")
            load_gain(glnbc, Bgl[0], g_ln, 1024)
            load_gain(blnbc, Bgl[1], b_ln, 1024)
            load_gain(goutbc, Bgl[2], g_gout, 1024)
            st = [T([128, 8], F32) for i in range(2)]
            Bst = S.bufs(2, "lnst")
            Bsth = [S.bufs(2, "lnsth%d_" % i) for i in range(2)]
            st2 = [T([128, 2], F32) for i in range(2)]
            Bst2 = S.bufs(2, "mst")
            junk = T([128, 1024], BF16)
            Bjunk = S.buf("junk")
            mr = [T([128, 1024], F32) for i in range(2)]
            Bmr = [S.bufs(2, "mr%d_" % i) for i in range(2)]
            mn = [T([128, 1024], BF16) for i in range(2)]
            Bmn = S.bufs(2, "mn")
            vout = T([128, 1024], F32)
            Bvout = S.buf("vout")
            cnt = [0]

            def stage_a(t):
                gi = t % 2
                for half in range(2):
                    slot, Bs = vslots[half]
                    bk, Bbk = nextpf()
                    S.op("pe", dense16(lambda k: aT[:, k, t * 128:(t + 1) * 128], lambda k: slot[:, k, :], bk[:]),
                         reads=[Bs] + BaT[t], writes=[Bbk])
                    i = cnt[0] % 2
                    cnt[0] += 1
                    gelu_evac(bk[:], Bbk, gv[gi][:, half * 512:(half + 1) * 512], [Bgv[gi][half]], gtmp[i], Bgtmp[i],
                              accum=st[gi][:, half:half + 1], Baccum=Bsth[gi][half], pool_affine=True)
                    yield

            def stage_b(t):
                sample = kinds[t] == "sample"
                gi = t % 2
                s_, Bs_ = st[gi], Bst[gi]
                g_ = gv[gi]
                Bg_ = Bgv[gi]
                S.op("act", lambda: nc.scalar.activation(out=junk, in_=g_, func=AF.Square, scale=1.0 / 32, accum_out=s_[:, 2:3]),
                     reads=Bg_, writes=[Bjunk, Bs_])
                S.op("dve", lambda: nc.vector.tensor_scalar(out=s_[:, 3:4], in0=s_[:, 0:1], scalar1=s_[:, 1:2], scalar2=1.0 / 1024,
                                                           op0=ALU.add, op1=ALU.mult), reads=Bsth[gi], writes=[Bs_])
                yield
                S.op("dve", lambda: nc.vector.tensor_scalar(out=s_[:, 4:5], in0=s_[:, 3:4], scalar1=s_[:, 3:4], scalar2=-1.0,
                                                           op0=ALU.mult, op1=ALU.mult), reads=[Bs_], writes=[Bs_])
                S.op("dve", lambda: nc.vector.tensor_tensor(out=s_[:, 5:6], in0=s_[:, 4:5], in1=s_[:, 2:3], op=ALU.add),
                     reads=[Bs_], writes=[Bs_])
                S.op("act", lambda: nc.scalar.activation(out=s_[:, 6:7], in_=s_[:, 5:6], func=AF.Sqrt, scale=1.0, bias=EPS),
                     reads=[Bs_], writes=[Bs_])
                S.op("dve", lambda: nc.vector.reciprocal(out=s_[:, 6:7], in_=s_[:, 6:7]), reads=[Bs_], writes=[Bs_])
                yield
                S.op("dve", lambda: nc.vector.tensor_scalar(out=g_, in0=g_, scalar1=s_[:, 3:4], scalar2=s_[:, 6:7],
                                                           op0=ALU.subtract, op1=ALU.mult), reads=Bg_ + [Bs_], writes=Bg_)
                yield
                S.op("dve", lambda: nc.vector.tensor_tensor(out=g_, in0=g_, in1=glnbc, op=ALU.mult), reads=Bg_ + [Bgl[0]], writes=Bg_)
                yield
                if sample:
                    S.op("dve", lambda: nc.vector.tensor_tensor(out=vout, in0=g_, in1=blnbc, op=ALU.add),
                         reads=Bg_ + [Bgl[1]], writes=[Bvout])
                    S.dma("sp", vso, vout, reads=[Bvout], key=Bvout)
                    S.op("act", lambda: nc.scalar.copy(out=vn[:, t, :], in_=vout), reads=[Bvout], writes=[Bvn[t]])
                else:
                    S.op("dve", lambda: nc.vector.tensor_tensor(out=vn[:, t, :], in0=g_, in1=blnbc, op=ALU.add),
                         reads=Bg_ + [Bgl[1]], writes=[Bvn[t]])
                yield

            def stage_c(t):
                sample = kinds[t] == "sample"
                gc0 = (gt0 + t) * 128
                Wm = Wblk if sample else WT
                BWm = BWblk if sample else [BWT]
                br = bsrow_s if sample else bsrow
                Bbr = Bbss if sample else Bbs
                mb = [nextpf(), nextpf()]
                mi = t % 2

                def mmx(mb=mb, Wm=Wm, br=br, t=t):
                    for g in range(8):
                        reg = mb[g // 4][0][:, (g % 4) * 128:(g % 4 + 1) * 128]
                        nc.tensor.matmul(reg, lhsT=Wm[:, g, :], rhs=vn[:, t, g * 128:(g + 1) * 128], start=True, stop=False)
                        ins = nc.tensor.matmul(reg, lhsT=br[:, g * 128:(g + 1) * 128], rhs=ones2, start=False, stop=True)
                    return ins
                S.op("pe", mmx, reads=BWm + [Bvn[t], Bbr, Bbs], writes=[mb[0][1], mb[1][1]])
                for half in range(2):
                    S.op("dve", lambda: nc.vector.tensor_tensor(out=mr[mi][:, half * 512:(half + 1) * 512], in0=mb[half][0][:],
                                                               in1=gelu_u[:, t, half * 512:(half + 1) * 512], op=ALU.mult),
                         reads=[mb[half][1], Bgu[t][half]], writes=[Bmr[mi][half]])
                yield
                s_, Bs_ = st2[mi], Bst2[mi]
                S.op("act", lambda: nc.scalar.activation(out=junk, in_=mr[mi], func=AF.Square, scale=1.0 / 32, accum_out=s_[:, 0:1]),
                     reads=Bmr[mi], writes=[Bjunk, Bs_])
                S.op("act", lambda: nc.scalar.activation(out=s_[:, 1:2], in_=s_[:, 0:1], func=AF.Sqrt, scale=1.0, bias=EPS),
                     reads=[Bs_], writes=[Bs_])
                S.op("dve", lambda: nc.vector.reciprocal(out=s_[:, 1:2], in_=s_[:, 1:2]), reads=[Bs_], writes=[Bs_])
                S.op("dve", lambda: nc.vector.scalar_tensor_tensor(out=mn[mi], in0=mr[mi], scalar=s_[:, 1:2], in1=goutbc,
                                                                  op0=ALU.mult, op1=ALU.mult),
                     reads=Bmr[mi] + [Bs_, Bgl[2]], writes=[Bmn[mi]])
                yield
                transpose_cols(mn[mi], [Bmn[mi]], 8, catT[:, 8:16, gc0:gc0 + 128], [BcatTm[gt0 + t]], "act")
                yield

            def G_(fn, t):
                return fn(t) if 0 <= t < ntile else None
            run_streams(stage_a(0))
            run_streams(G_(stage_a, 1), stage_b(0))
            for t in range(ntile):
                run_streams(G_(stage_a, t + 2), G_(stage_b, t + 1), stage_c(t))

        mixer_group(xp, 0, 8, ["prev"] * 8)
        mixer_group(xm, 0, 3, ["prompt"] * 3)
        mixer_group(xm, 3, 3, ["prompt"] * 3)
        mixer_group(xm, 6, 3, ["prompt", "prompt", "sample"])
        Bspo = S.buf("spo")
        S.dma("sp", spo.rearrange("h k v -> k h v"), Sf, reads=BSf, key=Bspo)

        h = T([128, NT, D], F32)
        Bh = [S.bufs(4, "h%d_" % t) for t in range(NT)]
        for t in range(NT):
            S.dma("sp", h[:, t, :], xm[t * 128:(t + 1) * 128, :], writes=Bh[t])
        m_w = AR.mark()
        ws = WStream("out", [128, 16, 512])

        def ld_out(db):
            def f(slot, Bs):
                S.dma("pool", slot, w_out_v[:, :, db * 512:(db + 1) * 512], writes=[Bs])
            return f
        nxt = ws.load(ld_out(0))
        for db in range(4):
            slot, Bs = nxt
            if db < 3:
                nxt = ws.load(ld_out(db + 1))
            for t in range(NT):
                bk, Bbk = nextpf()
                S.op("pe", dense16(lambda k: catT[:, k, t * 128:(t + 1) * 128], lambda k: slot[:, k, :], bk[:]),
                     reads=[Bs, BcatTo[t], BcatTm[t]], writes=[Bbk])
                hs = h[:, t, db * 512:(db + 1) * 512]
                S.op("dve", lambda: nc.vector.tensor_tensor(out=hs, in0=bk[:], in1=hs, op=ALU.add),
                     reads=[Bbk, Bh[t][db]], writes=[Bh[t][db]])
        S.barrier()
        AR.release(m_w)
        AR.r = AR.n

        wple = T([128, 2, D], BF16)
        pT = T([128, 2, NTOK], BF16)
        rpe = T([128, NT], F32)
        ps1 = T([128, 256], F32)
        pbf1 = T([128, 256], BF16)
        junk2 = T([128, 512], BF16)
        pss = [T([128, 8], F32) for i in range(2)]
        m_f = AR.mark()
        fT = T([128, 16, NTOK], BF16)
        BfT = [S.bufs(2, "fT%d_" % t) for t in range(NT)]
        m_fa = AR.mark()
        gt = T([128, D], F32)
        Bg = S.buf("gffnbc")
        load_gain(gt, Bg, g_ffn, D)
        W = NormWS("f")
        norm_T_pipe([(h[:, t, :], Bh[t], t * 128, BfT[t], None) for t in range(NT)], gt, Bg, fT, W)
        S.barrier()
        AR.release(m_fa)
        w1 = WStream("f1", [128, 2, 16, 256])
        w1.B = [S.bufs(2, "wf1_%d_" % i) for i in range(2)]
        w2 = WStream("f2", [128, 4, D], n=1)
        hT = [T([128, 4, NTOK], BF16) for i in range(2)]
        BhT = [S.bufs(4, "hT%d_" % i) for i in range(2)]
        sg = [T([128, 512], BF16) for i in range(2)]
        Bsg = S.bufs(2, "sg")
        groups = [(0, 512), (512, 512), (1024, 128)]
        NFB = DFF // 512

        def ld_f1(u):
            def f(slot, Bs):
                S.dma("pool", slot[:, 0, :, :], w_f1_v[:, :, u * 256:(u + 1) * 256], writes=[Bs[0]])
                S.dma("pool", slot[:, 1, :, :], w_f1_v[:, :, DFF + u * 256:DFF + (u + 1) * 256], writes=[Bs[1]])
            return f

        def ld_f2(fb):
            def f(slot, Bs):
                S.dma("pool", slot, w_f2_v[:, fb * 4:(fb + 1) * 4, :], writes=[Bs])
            return f
        cnt = [0]
        f1_next = [None]

        def ffn1(fb):
            hi = fb % 2
            for uu in range(2):
                u = fb * 2 + uu
                slot, Bs = f1_next[0]
                if u + 1 < 2 * NFB:
                    f1_next[0] = w1.load(ld_f1(u + 1))
                for c in range(2):
                    j = uu * 2 + c
                    for (g0, n) in groups:
                        tl = list(range(g0 // 128, (g0 + n) // 128))
                        rd = Bs + [b for t in tl for b in BfT[t]]
                        gb, Bgb = nextpf()
                        S.op("pe", dense16(lambda k: slot[:, 0, k, c * 128:(c + 1) * 128], lambda k: fT[:, k, g0:g0 + n], gb[:, 0:n]),
                             reads=rd, writes=[Bgb])
                        ub, Bub = nextpf()
                        S.op("pe", dense16(lambda k: slot[:, 1, k, c * 128:(c + 1) * 128], lambda k: fT[:, k, g0:g0 + n], ub[:, 0:n]),
                             reads=rd, writes=[Bub])
                        i = cnt[0] % 2
                        cnt[0] += 1
                        S.op("act", lambda: nc.scalar.activation(out=sg[i][:, 0:n], in_=gb[:, 0:n], func=AF.Silu),
                             reads=[Bgb], writes=[Bsg[i]])
                        S.op("dve", lambda: nc.vector.tensor_tensor(out=hT[hi][:, j, g0:g0 + n], in0=ub[:, 0:n], in1=sg[i][:, 0:n],
                                                                   op=ALU.mult),
                             reads=[Bub, Bsg[i]], writes=[BhT[hi][j]])

        def ffn2(fb, slot2, Bs2):
            hi = fb % 2
            for t in range(NT):
                for db in range(4):
                    bk, Bbk = nextpf()

                    def mm(bk=bk, db=db, t=t):
                        for j in range(4):
                            ins = nc.tensor.matmul(bk[:], lhsT=hT[hi][:, j, t * 128:(t + 1) * 128],
                                                   rhs=slot2[:, j, db * 512:(db + 1) * 512], start=(j == 0), stop=(j == 3))
                        return ins
                    S.op("pe", mm, reads=[Bs2] + BhT[hi], writes=[Bbk])
                    hs = h[:, t, db * 512:(db + 1) * 512]
                    S.op("dve", lambda: nc.vector.tensor_tensor(out=hs, in0=bk[:], in1=hs, op=ALU.add),
                         reads=[Bbk, Bh[t][db]], writes=[Bh[t][db]])

        f1_next[0] = w1.load(ld_f1(0))
        cur2 = w2.load(ld_f2(0))
        Bwple = S.buf("wple")
        BpT = S.bufs(NT, "pT")
        Brpe = S.bufs(NT, "rpe")
        Bps1, Bpbf1, Bjunk2 = S.buf("ps1"), S.buf("pbf1"), S.buf("junk2")
        Bpss = S.bufs(2, "pss")
        S.dma("pool", wple, w_ple_v, writes=[Bwple])

        def pe_mm(bk, t, db):
            def f():
                for kk in range(2):
                    ins = nc.tensor.matmul(bk[:], lhsT=pT[:, kk, t * 128:(t + 1) * 128], rhs=wple[:, kk, db * 512:(db + 1) * 512],
                                           start=(kk == 0), stop=(kk == 1))
                return ins
            return f
        for t in range(NT):
            i = t % 2
            S.dma("sp", ps1, pm[t * 128:(t + 1) * 128, :], writes=[Bps1])
            S.op("dve", lambda: nc.vector.tensor_copy(out=pbf1, in_=ps1), reads=[Bps1], writes=[Bpbf1], cost=0.3)
            transpose_cols(pbf1, [Bpbf1], 2, pT[:, :, t * 128:(t + 1) * 128], [BpT[t]], "act")
            for db in range(4):
                bk, Bbk = nextpf()
                S.op("pe", pe_mm(bk, t, db), reads=[BpT[t], Bwple], writes=[Bbk], cost=0.5)
                S.op("act", lambda: nc.scalar.activation(out=junk2, in_=bk[:], func=AF.Square, accum_out=pss[i][:, db:db + 1]),
                     reads=[Bbk], writes=[Bjunk2, Bpss[i]], cost=0.7)
            S.op("dve", lambda: nc.vector.reduce_sum(out=pss[i][:, 4:5], in_=pss[i][:, 0:4], axis=AX.X),
                 reads=[Bpss[i]], writes=[Bpss[i]], cost=0.2)
            S.op("act", lambda: nc.scalar.activation(out=pss[i][:, 5:6], in_=pss[i][:, 4:5], func=AF.Sqrt, scale=1.0 / D, bias=EPS),
                 reads=[Bpss[i]], writes=[Bpss[i]], cost=0.3)
            S.op("dve", lambda: nc.vector.reciprocal(out=rpe[:, t:t + 1], in_=pss[i][:, 5:6]), reads=[Bpss[i]], writes=[Brpe[t]], cost=0.2)
        ffn1(0)
        for fb in range(NFB):
            if fb + 1 < NFB:
                ffn1(fb + 1)
            ffn2(fb, cur2[0], cur2[1])
            if fb + 1 < NFB:
                cur2 = w2.load(ld_f2(fb + 1))
        S.barrier()
        AR.release(m_f)

        gT = T([128, 16, NTOK], BF16)
        BgT = [S.bufs(2, "gT%d_" % t) for t in range(NT)]
        gplebc = T([128, D], F32)
        Bgp = S.bufs(2, "gplefin")
        load_gain(gplebc, Bgp[0], g_ple, D)
        m_g2 = AR.mark()
        ws = WStream("pg", [128, 16, 512])

        def ld_pg(db):
            def f(slot, Bs):
                S.dma("pool", slot, w_pg_v[:, :, db * 512:(db + 1) * 512], writes=[Bs])
            return f
        pgq = [ws.load(ld_pg(0)), ws.load(ld_pg(1))]
        m_ga = AR.mark()
        gt = T([128, D], F32)
        Bg = S.buf("gpgbc")
        load_gain(gt, Bg, g_pg, D)
        W = NormWS("g")
        def gT_gen():
            prev = None
            for t in range(NT):
                i = norm_p1(h[:, t, :], Bh[t], gt, Bg, W)
                yield
                if prev is not None:
                    norm_p2(prev[0], gT, prev[1], prev[2], W)
                    yield
                prev = (i, t * 128, BgT[t])
            norm_p2(prev[0], gT, prev[1], prev[2], W)
            yield

        run_streams(gT_gen())
        S.soft_barrier()
        AR.release(m_ga)
        sig = [T([128, 512], F32) for i in range(2)]
        Bsig = S.bufs(2, "sig")
        pet = [T([128, 512], F32) for i in range(2)]
        Bpet = S.bufs(2, "pet")
        cnt = [0]
        for db in range(4):
            slot, Bs = pgq.pop(0)
            for t in range(NT):
                i = cnt[0] % 2
                cnt[0] += 1
                gb, Bgb = nextpf()
                S.op("pe", dense16(lambda k: gT[:, k, t * 128:(t + 1) * 128], lambda k: slot[:, k, :], gb[:]),
                     reads=[Bs] + BgT[t], writes=[Bgb])
                pk, Bpk = nextpf()
                S.op("pe", pe_mm(pk, t, db), reads=[BpT[t], Bwple], writes=[Bpk])
                S.op("act", lambda: nc.scalar.activation(out=sig[i], in_=gb[:], func=AF.Sigmoid), reads=[Bgb], writes=[Bsig[i]])
                S.op("dve", lambda: nc.vector.scalar_tensor_tensor(out=pet[i], in0=pk[:], scalar=rpe[:, t:t + 1],
                                                                  in1=gplebc[:, db * 512:(db + 1) * 512], op0=ALU.mult, op1=ALU.mult),
                     reads=[Bpk, Brpe[t], Bgp[0]], writes=[Bpet[i]])
                S.op("dve", lambda: nc.vector.tensor_tensor(out=pet[i], in0=pet[i], in1=sig[i], op=ALU.mult),
                     reads=[Bpet[i], Bsig[i]], writes=[Bpet[i]])
                hs = h[:, t, db * 512:(db + 1) * 512]
                S.op("dve", lambda: nc.vector.tensor_tensor(out=hs, in0=hs, in1=pet[i], op=ALU.add),
                     reads=[Bpet[i], Bh[t][db]], writes=[Bh[t][db]])
            if db + 2 < 4:
                pgq.append(ws.load(ld_pg(db + 2)))
        S.soft_barrier()
        AR.release(m_g2)
        gfinbc = T([128, D], F32)
        Bgfin = S.buf("gfinbc")
        load_gain(gfinbc, Bgfin, g_fin, D)
        fss = [T([128, 2], F32) for i in range(2)]
        Bfss = S.bufs(2, "fss")
        junk3 = T([128, D], BF16)
        Bjunk3 = S.buf("junk3")

        def fin_p1(t):
            i = t % 2
            hs = h[:, t, :]
            S.op("act", lambda: nc.scalar.activation(out=junk3, in_=hs, func=AF.Square, accum_out=fss[i][:, 0:1]),
                 reads=Bh[t], writes=[Bjunk3, Bfss[i]], cost=1.9)
            S.op("act", lambda: nc.scalar.activation(out=fss[i][:, 1:2], in_=fss[i][:, 0:1], func=AF.Sqrt, scale=1.0 / D, bias=EPS),
                 reads=[Bfss[i]], writes=[Bfss[i]])
            S.op("dve", lambda: nc.vector.reciprocal(out=fss[i][:, 1:2], in_=fss[i][:, 1:2]), reads=[Bfss[i]], writes=[Bfss[i]])

        def fin_p2(t):
            i = t % 2
            hs = h[:, t, :]
            S.op("dve", lambda: nc.vector.scalar_tensor_tensor(out=hs, in0=hs, scalar=fss[i][:, 1:2], in1=gfinbc,
                                                              op0=ALU.mult, op1=ALU.mult),
                 reads=Bh[t] + [Bfss[i], Bgfin], writes=Bh[t], cost=2.4)
            S.dma("sp", y[t * 128:(t + 1) * 128, :], hs, reads=Bh[t], key=Bh[t][0])
        fin_p1(0)
        for t in range(NT):
            if t + 1 < NT:
                fin_p1(t + 1)
            fin_p2(t)
        S.finish()
        print("[kernel] SBUF arena peak words %d / %d ; engine op counts %s" % (AR.peak, AR.n, S.cnt))
    return nc


_NC_CACHE = {}


def kernel(x_prompt, x_sample, state_gla, p_prompt, p_sample, g_mix, w_in, w_a2, b_a, g_gla_norm, g_gmlp_ln, b_gmlp_ln,
           w_s, b_s, g_gmlp_out, w_out, g_ffn, w_ffn_in, w_ffn_out, w_ple, g_ple, g_ple_gate, w_ple_gate, g_final):
    f = lambda a: np.ascontiguousarray(np.asarray(a, dtype=np.float32))
    x_prompt, x_sample, state_gla, p_prompt, p_sample = f(x_prompt), f(x_sample), f(state_gla), f(p_prompt), f(p_sample)
    shared = {
        "g_mix": f(g_mix).reshape(D), "w_in": f(w_in).reshape(D, 5136), "w_a2": f(w_a2).reshape(16, 512),
        "b_a": f(b_a).reshape(1, 512), "g_gla": f(g_gla_norm).reshape(1024), "g_ln": f(g_gmlp_ln).reshape(1024),
        "b_ln": f(b_gmlp_ln).reshape(1024), "w_s": f(w_s).reshape(8, 128, 128), "b_s": f(b_s).reshape(1, 1024),
        "g_gout": f(g_gmlp_out).reshape(1024), "w_out": f(w_out).reshape(D, D), "g_ffn": f(g_ffn).reshape(D),
        "w_f1": f(w_ffn_in).reshape(D, 2 * DFF), "w_f2": f(w_ffn_out).reshape(DFF, D), "w_ple": f(w_ple).reshape(256, D),
        "g_ple": f(g_ple).reshape(D), "g_pg": f(g_ple_gate).reshape(D), "w_pg": f(w_ple_gate).reshape(D, D),
        "g_fin": f(g_final).reshape(D),
    }
    in_maps = []
    for c in range(8):
        b, half = c // 2, c % 2
        xs = x_sample[16 * c:16 * c + 16].reshape(128, D)
        ps = p_sample[0, 16 * c:16 * c + 16].reshape(128, 256)
        xm = np.concatenate([x_prompt[b, half * 1024:(half + 1) * 1024], xs], axis=0)
        pmm = np.concatenate([p_prompt[0, b, half * 1024:(half + 1) * 1024], ps], axis=0)
        xp = x_prompt[b, 0:1024] if half == 1 else np.zeros((1024, D), np.float32)
        m = {"xm": np.ascontiguousarray(xm), "xp": np.ascontiguousarray(xp), "pm": np.ascontiguousarray(pmm),
             "st0": np.ascontiguousarray(state_gla[0, 16 * c:16 * c + 16])}
        m.update(shared)
        in_maps.append(m)
    if "nc" not in _NC_CACHE:
        _NC_CACHE["nc"] = build_nc()
    nc = _NC_CACHE["nc"]
    res = run_bass_kernel_spmd(nc, in_maps, core_ids=list(range(8)))
    r = res.results
    y_prompt = np.zeros((4, 2048, D), np.float32)
    y_sample = np.zeros((128, 8, D), np.float32)
    sp = np.zeros((1, 4, 4, 128, 256), np.float32)
    ss = np.zeros((1, 128, 4, 128, 256), np.float32)
    vs = np.zeros((1, 128, 8, 1024), np.float32)
    for c in range(8):
        b, half = c // 2, c % 2
        yc = np.asarray(r[c]["y"])
        y_prompt[b, half * 1024:(half + 1) * 1024] = yc[0:1024]
        y_sample[16 * c:16 * c + 16] = yc[1024:1152].reshape(16, 8, D)
        if half == 1:
            sp[0, b] = np.asarray(r[c]["spo"])
        ss[0, 16 * c:16 * c + 16] = np.asarray(r[c]["sso"])
        vs[0, 16 * c:16 * c + 16] = np.asarray(r[c]["vso"]).reshape(16, 8, 1024)
    return (y_prompt, y_sample, sp, ss, vs)
```
